# Optimizing a Trainium2 kernel written in Bass

```python
import jax
import jax.numpy as jnp
from jax import lax
import numpy as np

D_MODEL = 1024
BATCH = 4
SEQ = 4096
DEPTH = 1

MOBA_HEADS = 8
MOBA_HEAD_DIM = 64
MOBA_BLOCK = 256
MOBA_TOPK = 3
MOBA_Q_CHUNK = 32
ROPE_THETA = 10000.0

RET_HEADS = 4
RET_QK_DIM = 128
RET_V_DIM = 256
RET_CHUNK = 128
RET_ANGLE_BASE = 10000.0

MOBA_W = MOBA_HEADS * MOBA_HEAD_DIM
RET_QK_W = RET_HEADS * RET_QK_DIM
RET_V_W = RET_HEADS * RET_V_DIM
IN_SPLITS = (MOBA_W, MOBA_W, MOBA_W, RET_QK_W, RET_QK_W, RET_V_W, RET_V_W, D_MODEL, D_MODEL)
IN_COLS = sum(IN_SPLITS)

D_FF = 2816
CONV_WIDTH = 3

N_MOD = 6
LN_EPS = 1e-5
DEEPNORM_ALPHA = (2.0 * DEPTH) ** 0.25
DEEPNORM_BETA = (8.0 * DEPTH) ** -0.25
NEG_BIG = -1e30

kernel_name = 'hybrid_moba_retention_convffn'


def layer_norm(x, g, b):
    xf = x.astype(jnp.float32)
    mu = jnp.mean(xf, -1, keepdims=True)
    var = jnp.mean(jnp.square(xf - mu), -1, keepdims=True)
    y = (xf - mu) * lax.rsqrt(var + LN_EPS)
    return (y * g.astype(jnp.float32) + b.astype(jnp.float32)).astype(x.dtype)


def head_norm(x):
    xf = x.astype(jnp.float32)
    mu = jnp.mean(xf, -1, keepdims=True)
    var = jnp.mean(jnp.square(xf - mu), -1, keepdims=True)
    return ((xf - mu) * lax.rsqrt(var + LN_EPS)).astype(x.dtype)


def rotary_half(x, pos):
    hd = x.shape[-1]
    inv_freq = ROPE_THETA ** (-jnp.arange(0, hd, 2, dtype=jnp.float32) / hd)
    ang = pos[:, None] * inv_freq[None, :]
    cos, sin = jnp.cos(ang), jnp.sin(ang)
    xf = x.astype(jnp.float32)
    x1, x2 = xf[..., : hd // 2], xf[..., hd // 2:]
    return jnp.concatenate([x1 * cos - x2 * sin, x2 * cos + x1 * sin], -1).astype(x.dtype)


def retnet_rotate(x, pos):
    dk = x.shape[-1]
    freq = 1.0 / (RET_ANGLE_BASE ** jnp.linspace(0.0, 1.0, dk // 2, dtype=jnp.float32))
    ang = pos[:, None] * freq[None, :]
    cos, sin = jnp.cos(ang), jnp.sin(ang)
    xf = x.astype(jnp.float32).reshape(x.shape[:-1] + (dk // 2, 2))
    x0, x1 = xf[..., 0], xf[..., 1]
    out = jnp.stack([x0 * cos - x1 * sin, x1 * cos + x0 * sin], -1)
    return out.reshape(x.shape).astype(x.dtype)


def moba_attention(q, k, v):
    b, h, s, hd = q.shape
    nb = -(-s // MOBA_BLOCK)
    pad = nb * MOBA_BLOCK - s
    k_pad = jnp.pad(k, ((0, 0), (0, 0), (0, pad), (0, 0)))
    v_pad = jnp.pad(v, ((0, 0), (0, 0), (0, pad), (0, 0)))
    k_blk = k_pad.reshape(b, h, nb, MOBA_BLOCK, hd)
    v_blk = v_pad.reshape(b, h, nb, MOBA_BLOCK, hd)
    k_mean = jnp.mean(k_blk.astype(jnp.float32), axis=3)
    topk = min(MOBA_TOPK, nb)
    scale = hd ** -0.5
    bi = jnp.arange(b)[:, None, None, None]
    hi = jnp.arange(h)[None, :, None, None]
    blk_ids = jnp.arange(nb)
    slot_ids = jnp.arange(topk)
    key_off = jnp.arange(MOBA_BLOCK)
    q_off = jnp.arange(MOBA_Q_CHUNK)

    def chunk(i):
        start = i * MOBA_Q_CHUNK
        n = start // MOBA_BLOCK
        qc = lax.dynamic_slice_in_dim(q, start, MOBA_Q_CHUNK, axis=2)
        k_own = lax.dynamic_slice_in_dim(k_pad, n * MOBA_BLOCK, MOBA_BLOCK, axis=2)
        v_own = lax.dynamic_slice_in_dim(v_pad, n * MOBA_BLOCK, MOBA_BLOCK, axis=2)
        causal = (n * MOBA_BLOCK + key_off)[None, :] <= (start + q_off)[:, None]
        l_own = jnp.einsum('bhqd,bhkd->bhqk', qc, k_own).astype(jnp.float32) * scale
        l_own = jnp.where(causal, l_own, NEG_BIG)
        gate = jnp.einsum('bhqd,bhnd->bhqn', qc.astype(jnp.float32), k_mean)
        gate = jnp.where(blk_ids < n, gate, NEG_BIG)
        _, sel = lax.top_k(gate, topk)
        valid = slot_ids < n
        k_sel = k_blk[bi, hi, sel]
        v_sel = v_blk[bi, hi, sel]
        l_past = jnp.einsum('bhqd,bhqtkd->bhqtk', qc, k_sel).astype(jnp.float32) * scale
        l_past = jnp.where(valid[:, None], l_past, NEG_BIG)
        logits = jnp.concatenate(
            [l_own, l_past.reshape(b, h, MOBA_Q_CHUNK, topk * MOBA_BLOCK)], axis=-1)
        p = jax.nn.softmax(logits, axis=-1).astype(v.dtype)
        p_own = p[..., :MOBA_BLOCK]
        p_past = p[..., MOBA_BLOCK:].reshape(b, h, MOBA_Q_CHUNK, topk, MOBA_BLOCK)
        return (jnp.einsum('bhqk,bhkd->bhqd', p_own, v_own)
                + jnp.einsum('bhqtk,bhqtkd->bhqd', p_past, v_sel))

    out = lax.map(chunk, jnp.arange(s // MOBA_Q_CHUNK))
    return jnp.transpose(out, (1, 0, 3, 2, 4)).reshape(b, s, h * hd)


def retention(q, k, v):
    b, h, s, dk = q.shape
    dv = v.shape[-1]
    c = RET_CHUNK
    nc = s // c
    dt = q.dtype
    log_g = jnp.log(1.0 - 2.0 ** (-5.0 - jnp.arange(h, dtype=jnp.float32)))
    idx = jnp.arange(c, dtype=jnp.float32)
    diff = idx[:, None] - idx[None, :]
    d_intra = jnp.where(diff >= 0, jnp.exp(log_g[:, None, None] * jnp.maximum(diff, 0.0)), 0.0)
    k_decay = jnp.exp(log_g[:, None] * (c - 1.0 - idx)[None, :])
    q_decay = jnp.exp(log_g[:, None] * (idx + 1.0)[None, :])
    chunk_decay = jnp.exp(log_g * c)
    qc = q.reshape(b, h, nc, c, dk)
    kc = k.reshape(b, h, nc, c, dk)
    vc = v.reshape(b, h, nc, c, dv)
    scores = jnp.einsum('bhncd,bhnkd->bhnck', qc, kc) * d_intra[None, :, None].astype(dt)
    intra = jnp.einsum('bhnck,bhnkv->bhncv', scores, vc)
    kv = jnp.einsum('bhnkd,bhnkv->nbhdv', kc * k_decay[None, :, None, :, None].astype(dt), vc)
    decay_c = chunk_decay[None, :, None, None].astype(dt)

    def step(state, kv_n):
        return decay_c * state + kv_n, state

    _, s_before = lax.scan(step, jnp.zeros((b, h, dk, dv), dt), kv)
    cross = jnp.einsum('bhncd,nbhdv->bhncv', qc * q_decay[None, :, None, :, None].astype(dt), s_before)
    return (intra + cross).reshape(b, h, s, dv)


def split_heads(t, n_heads):
    b, s, w = t.shape
    return t.reshape(b, s, n_heads, w // n_heads).transpose(0, 2, 1, 3)


def token_mixer(h, w_in, w_proj_moba, w_proj_ret, w_out):
    b, s, _ = h.shape
    proj = h @ w_in
    offsets = [int(o) for o in np.cumsum(IN_SPLITS)[:-1]]
    mq, mk, mv, rq, rk, rv, rg, g_a, g_r = jnp.split(proj, offsets, axis=-1)
    pos = jnp.arange(s, dtype=jnp.float32)
    mq = rotary_half(split_heads(mq, MOBA_HEADS), pos)
    mk = rotary_half(split_heads(mk, MOBA_HEADS), pos)
    y_a = moba_attention(mq, mk, split_heads(mv, MOBA_HEADS))
    rq = retnet_rotate(split_heads(rq, RET_HEADS), pos)
    rk = retnet_rotate(split_heads(rk, RET_HEADS), pos) * (RET_QK_DIM ** -0.5)
    y_r = retention(rq, rk, split_heads(rv, RET_HEADS))
    y_r = head_norm(y_r).transpose(0, 2, 1, 3).reshape(b, s, RET_V_W)
    y_r = jax.nn.silu(rg) * y_r
    merged = jax.nn.sigmoid(g_a) * (y_a @ w_proj_moba) + jax.nn.sigmoid(g_r) * (y_r @ w_proj_ret)
    return merged @ w_out


def conv_ffn(h, w_ff_gate, w_ff_up, ff_conv_w, ff_conv_b, w_ff_down):
    s = h.shape[1]
    g = h @ w_ff_gate
    u = h @ w_ff_up
    gp = jnp.pad(g, ((0, 0), (CONV_WIDTH - 1, 0), (0, 0)))
    g_conv = sum(gp[:, j:j + s, :] * ff_conv_w[j] for j in range(CONV_WIDTH)) + ff_conv_b
    a = jax.nn.gelu(g_conv, approximate=False) * u
    return a @ w_ff_down


def setup_inputs(seed: int = 0) -> dict:
    key = jax.random.key(seed)
    ks = jax.random.split(key, 17)
    f32 = jnp.float32
    beta = DEEPNORM_BETA

    def nrm(k, shape, scale):
        return jax.random.normal(k, shape, f32) * scale

    in_scales = (1.0, 1.0, beta, 1.0, 1.0, beta, 1.0, 1.0, 1.0)
    col_scale = jnp.concatenate([jnp.full((n,), sc, f32) for n, sc in zip(IN_SPLITS, in_scales)])
    return {
        'x': nrm(ks[0], (BATCH, SEQ, D_MODEL), 1.0),
        'c': nrm(ks[1], (BATCH, D_MODEL), 1.0),
        'w_ada': nrm(ks[2], (DEPTH, D_MODEL, N_MOD * D_MODEL), 0.5 * D_MODEL ** -0.5),
        'b_ada': nrm(ks[3], (DEPTH, N_MOD * D_MODEL), 0.02),
        'w_in': nrm(ks[4], (DEPTH, D_MODEL, IN_COLS), D_MODEL ** -0.5) * col_scale,
        'w_proj_moba': nrm(ks[5], (DEPTH, MOBA_W, D_MODEL), MOBA_W ** -0.5 * beta),
        'w_proj_ret': nrm(ks[6], (DEPTH, RET_V_W, D_MODEL), RET_V_W ** -0.5 * beta),
        'w_out': nrm(ks[7], (DEPTH, D_MODEL, D_MODEL), D_MODEL ** -0.5 * beta),
        'ln1_g': 1.0 + nrm(ks[8], (DEPTH, D_MODEL), 0.02),
        'ln1_b': nrm(ks[9], (DEPTH, D_MODEL), 0.02),
        'w_ff_gate': nrm(ks[10], (DEPTH, D_MODEL, D_FF), D_MODEL ** -0.5 * beta),
        'w_ff_up': nrm(ks[11], (DEPTH, D_MODEL, D_FF), D_MODEL ** -0.5 * beta),
        'ff_conv_w': nrm(ks[12], (DEPTH, CONV_WIDTH, D_FF), CONV_WIDTH ** -0.5),
        'ff_conv_b': nrm(ks[13], (DEPTH, D_FF), 0.02),
        'w_ff_down': nrm(ks[14], (DEPTH, D_FF, D_MODEL), D_FF ** -0.5 * beta),
        'ln2_g': 1.0 + nrm(ks[15], (DEPTH, D_MODEL), 0.02),
        'ln2_b': nrm(ks[16], (DEPTH, D_MODEL), 0.02),
    }


def reference(x, c, w_ada, b_ada, w_in, w_proj_moba, w_proj_ret, w_out, ln1_g, ln1_b,
              w_ff_gate, w_ff_up, ff_conv_w, ff_conv_b, w_ff_down, ln2_g, ln2_b):
    for l in range(DEPTH):
        mod = jax.nn.silu(c) @ w_ada[l] + b_ada[l]
        sh1, sc1, g1, sh2, sc2, g2 = jnp.split(mod[:, None, :], N_MOD, axis=-1)
        h = x * (1.0 + sc1) + sh1
        y = token_mixer(h, w_in[l], w_proj_moba[l], w_proj_ret[l], w_out[l])
        x = layer_norm(DEEPNORM_ALPHA * x + g1 * y, ln1_g[l], ln1_b[l])
        h = x * (1.0 + sc2) + sh2
        y = conv_ffn(h, w_ff_gate[l], w_ff_up[l], ff_conv_w[l], ff_conv_b[l], w_ff_down[l])
        x = layer_norm(DEEPNORM_ALPHA * x + g2 * y, ln2_g[l], ln2_b[l])
    return x
```

```python
from contextlib import ExitStack
import math
import numpy as np
import concourse.bass as bass
import concourse.mybir as mybir
from concourse.bass_utils import run_bass_kernel_spmd

F32 = mybir.dt.float32
BF16 = mybir.dt.bfloat16
AF = mybir.ActivationFunctionType
ALU = mybir.AluOpType
AX = mybir.AxisListType

D = 1024
SEQ = 4096
NB = 4
HALF = 2048
NQ = 2176
DFF = 2816
NFC = 22
ALPHA = 2.0 ** 0.25
LN_EPS = 1e-5
EPS_A = LN_EPS / (ALPHA * ALPHA)
BIG = 240000.0
NEGV = -30000.0
QG = [(0, 128, 1920)] + [(128 + 512 * i, 512, 2048 + 512 * i) for i in range(4)]

ENGS = ("pe", "act", "dve", "pool", "dma")
SAME_ENGINE_SYNC = {"pe": False, "act": True, "dve": True, "pool": True, "dma": True}
NDMASEM = {"dma": 32, "pool": 16}


class T:
    __slots__ = ("w", "rd")

    def __init__(self):
        self.w = None
        self.rd = []


class Op:
    __slots__ = ("eng", "fn", "deps", "sig", "sigval", "ndma", "sem", "prev", "phase")


class Sched:
    def __init__(self, nc, es):
        self.nc = nc
        self.sems = {e: es.enter_context(nc.semaphore("s_" + e)) for e in ENGS if e != "dma"}
        self.dsems = {e: [es.enter_context(nc.semaphore("d%s%d" % (e, i))) for i in range(NDMASEM[e])]
                      for e in NDMASEM}
        self.cnt = {e: 0 for e in ENGS}
        self.nd = {e: 0 for e in NDMASEM}
        self.tot = {e: [0] * NDMASEM[e] for e in NDMASEM}
        self.phase = 0
        self.ops = {e: [] for e in ENGS}
        self.nops = 0

    def op(self, eng, fn, reads=(), writes=(), ndma=1, dma=None):
        if dma is None:
            dma = eng == "dma"
        o = Op()
        o.eng = eng
        o.fn = fn
        o.ndma = ndma if dma else 0
        o.sig = False
        o.sigval = None
        o.sem = None
        o.prev = 0
        o.phase = self.phase
        deps = []
        for t in reads:
            if t.w is not None:
                deps.append(t.w)
        for t in writes:
            if t.w is not None:
                deps.append(t.w)
            deps.extend(t.rd)
        seen = set()
        o.deps = []
        for d in deps:
            if d is o or id(d) in seen or d.phase != self.phase:
                continue
            seen.add(id(d))
            if d.eng == eng and not SAME_ENGINE_SYNC[eng] and not d.ndma:
                continue
            o.deps.append(d)
        for t in reads:
            t.rd.append(o)
        for t in writes:
            t.w = o
            t.rd = []
        self.ops[eng].append(o)
        self.nops += 1
        return o

    def end_phase(self):
        nc = self.nc
        ops = self.ops
        for e in ENGS:
            for o in ops[e]:
                for d in o.deps:
                    d.sig = True
        bar = {}
        for e in ENGS:
            lastc = None
            for o in ops[e]:
                if not o.ndma:
                    lastc = o
            if lastc is not None:
                lastc.sig = True
            for o in ops[e]:
                if o.ndma:
                    s = self.nd[e] % NDMASEM[e]
                    self.nd[e] += 1
                    o.sem = self.dsems[e][s]
                    o.prev = self.tot[e][s]
                    self.tot[e][s] += 16 * o.ndma
                    o.sigval = self.tot[e][s]
                    bar[id(o.sem)] = (o.sem, o.sigval)
                else:
                    o.sem = self.sems[e]
                    if o.sig:
                        self.cnt[e] += 1
                        o.sigval = self.cnt[e]
            if lastc is not None:
                bar[id(lastc.sem)] = (lastc.sem, lastc.sigval)

        def run(e, eng):
            known = {}
            for o in ops[e]:
                need = {}
                for d in o.deps:
                    k = id(d.sem)
                    if known.get(k, 0) >= d.sigval:
                        continue
                    if k not in need or need[k][1] < d.sigval:
                        need[k] = (d.sem, d.sigval)
                if o.ndma and o.prev > 0:
                    k = id(o.sem)
                    if known.get(k, 0) < o.prev and (k not in need or need[k][1] < o.prev):
                        need[k] = (o.sem, o.prev)
                for k, (s, v) in need.items():
                    eng.wait_ge(s, v)
                    known[k] = v
                if o.ndma:
                    o.fn(eng, o.sem)
                else:
                    ins = o.fn(eng)
                    if o.sig:
                        ins.then_inc(o.sem, 1)
            for k, (s, v) in bar.items():
                if known.get(k, 0) < v:
                    eng.wait_ge(s, v)

        with nc.Block() as block:
            @block.tensor
            def _(eng):
                run("pe", eng)

            @block.scalar
            def _(eng):
                run("act", eng)

            @block.vector
            def _(eng):
                run("dve", eng)

            @block.gpsimd
            def _(eng):
                run("pool", eng)

            @block.sync
            def _(eng):
                run("dma", eng)

        self.ops = {e: [] for e in ENGS}
        self.phase += 1


def mm_group(S, out, pairs, reads, writes):
    pairs = list(pairs)

    def fn(e):
        n = len(pairs)
        ins = None
        for i, (l, r) in enumerate(pairs):
            ins = e.matmul(out, lhsT=l, rhs=r, start=(i == 0), stop=(i == n - 1))
        return ins
    return S.op("pe", fn, reads, writes)


def dma(S, out, in_, reads=(), writes=(), q="dma"):
    def fn(e, s):
        e.dma_start(out=out, in_=in_).then_inc(s, 16)
    return S.op(q, fn, reads, writes, ndma=1, dma=True)


def build_program():
    nc = bass.Bass("TRN2", target_bir_lowering=False)

    def din(name, shape, dt=F32):
        return nc.dram_tensor(name, list(shape), dt, kind="ExternalInput").ap()

    xT = din("xT", [D, SEQ])
    xq = din("xq", [NQ, D])
    cT = din("cT", [128, 8])
    w_ada = din("w_ada", [D, 6 * D])
    b_adaT = din("b_adaT", [128, 48])
    w_moba = din("w_moba", [4, D, 384])
    w_ret = din("w_ret", [4, D, 768])
    w_g = din("w_g", [D, 2048])
    w_pa = din("w_pa", [512, D])
    w_pr = din("w_pr", [D, D])
    w_o = din("w_o", [D, D])
    w_ffgu = din("w_ffgu", [NFC, D, 256])
    w_ffd = din("w_ffd", [DFF, D])
    convw = din("convw", [128, NFC * 3])
    convb = din("convb", [128, NFC])
    lnbc = din("lnbc", [4, 128, D])
    ln1T = din("ln1T", [128, 16])
    cosM = din("cosM", [128, SEQ])
    sinM = din("sinM", [128, SEQ])
    cosR = din("cosR", [128, SEQ])
    sinR = din("sinR", [128, SEQ])
    vb_d = din("vb", [128, 272])
    own_d = din("ownhot", [128, 272])
    ksp_d = din("ksp", [128, 128])
    epsq_d = din("epsq", [128, 4])
    dmask_d = din("dmask", [128, 512])
    hmask_d = din("hmask", [128, 256])
    tri_d = din("tri", [128, 128])
    onehot_d = din("onehot", [16, SEQ])
    ident_d = din("ident", [128, 128])
    flag_d = din("flag", [128, 1])
    out = nc.dram_tensor("out", [HALF, D], F32, kind="ExternalOutput").ap()
    scrM = nc.dram_tensor("scrM", [17, 128, 8, 128], BF16, kind="Internal").ap()
    scrH = nc.dram_tensor("scrH", [128, 8, NQ], BF16, kind="Internal").ap()

    gam = [1.0 - 2.0 ** (-5.0 - r) for r in range(4)]
    gC = [g ** 128 for g in gam]

    top = ExitStack()
    with top:
        S = Sched(nc, top)

        def sbuf(es, name, shape, dt=F32):
            return es.enter_context(nc.sbuf_tensor("sb_" + name, list(shape), dt))

        def psum(es, name, shape, dt=F32):
            return es.enter_context(nc.psum_tensor("ps_" + name, list(shape), dt))

        identF = sbuf(top, "identF", [128, 128])
        identB = sbuf(top, "identB", [128, 128], BF16)
        modT = sbuf(top, "modT", [128, 48])
        s1T = sbuf(top, "s1T", [128, 8])
        A2 = sbuf(top, "A2", [128, 8])
        B2 = sbuf(top, "B2", [128, 8])
        g1bc = sbuf(top, "g1bc", [128, D])
        g2bc = sbuf(top, "g2bc", [128, D])
        flag = sbuf(top, "flag", [128, 1])
        epsA = sbuf(top, "epsA", [128, 1])
        oneA = sbuf(top, "oneA", [128, 1])
        t_const = T()

        mix = ExitStack()
        mix.__enter__()
        h1T = sbuf(mix, "h1T", [128, 8, SEQ], BF16)
        AT = sbuf(mix, "AT", [128, 4, NQ], BF16)
        t_h1 = T()

        with ExitStack() as ph:
            cs = sbuf(ph, "cs", [128, 8])
            sc = sbuf(ph, "sc", [128, 8])
            bad = sbuf(ph, "bad", [128, 48])
            l1T = sbuf(ph, "l1T", [128, 16])
            onesF = sbuf(ph, "onesF", [128, 128])
            wstb = [sbuf(ph, "wstb%d" % i, [128, 8, 512]) for i in range(2)]
            modrow = sbuf(ph, "modrow", [1, 6 * D])
            one1 = sbuf(ph, "one1", [1, 1])
            prow = [psum(ph, "prow%d" % i, [128, 512]) for i in range(2)]
            t_prow = [T(), T()]
            t_modrow = T()
            xst = [sbuf(ph, "xst%d" % i, [128, 8, 512]) for i in range(2)]
            dg = [sbuf(ph, "dg%d" % i, [128, 128]) for i in range(2)]
            g12 = sbuf(ph, "g12", [128, 16])
            pmod = psum(ph, "pmod", [128, 512])
            pmodA = psum(ph, "pmodA", [128, 512])
            pbc = [psum(ph, "pbc%d" % i, [128, 1024]) for i in range(2)]
            t_cs, t_sc, t_bad, t_l1, t_ones, t_pmod, t_mod = T(), T(), T(), T(), T(), T(), T()
            t_wst = [T(), T()]
            t_xst = [T(), T()]
            t_dg = [T(), T()]
            t_pbc = [T(), T()]
            t_g12 = T()

            dma(S, identF[:], ident_d[:, :], writes=[t_const])
            dma(S, flag[:], flag_d[:, :], writes=[t_const])
            dma(S, cs[:], cT[:, :], writes=[t_cs])
            dma(S, bad[:], b_adaT[:, :], writes=[t_bad])
            dma(S, l1T[:], ln1T[:, :], writes=[t_l1])
            S.op("dve", lambda e: e.tensor_copy(out=identB[:], in_=identF[:]), reads=[t_const], writes=[t_const])
            S.op("dve", lambda e: e.memset(onesF[:], 1.0), writes=[t_ones])
            S.op("dve", lambda e: e.memset(one1[:], 1.0), writes=[t_ones])
            S.op("dve", lambda e: e.memset(epsA[:], EPS_A), writes=[t_const])
            S.op("dve", lambda e: e.memset(oneA[:], 1.0), writes=[t_const])
            S.op("act", lambda e: e.activation(out=sc[:], in_=cs[:], func=AF.Silu), reads=[t_cs], writes=[t_sc])
            w_ada_v = w_ada.rearrange("(k p) n -> p k n", p=128)
            xT_v = xT.rearrange("(k p) t -> p k t", p=128)

            def mod_block(blk):
                b = blk % 2
                dma(S, wstb[b][:], w_ada_v[:, :, blk * 512:(blk + 1) * 512], writes=[t_wst[b]])
                mm_group(S, prow[b][0:1, :], [(sc[:, k:k + 1], wstb[b][:, k, :]) for k in range(8)],
                         reads=[t_wst[b], t_sc], writes=[t_prow[b]])
                S.op("act", lambda e, b=b, blk=blk: e.copy(out=modrow[0:1, blk * 512:(blk + 1) * 512], in_=prow[b][0:1, :]),
                     reads=[t_prow[b]], writes=[t_modrow])

                def fn(e, blk=blk):
                    ins = None
                    pm = pmodA if blk < 4 else pmod
                    for j in range(4):
                        col = blk * 4 + j
                        ins = e.matmul(pm[:, col:col + 1], lhsT=modrow[0:1, col * 128:(col + 1) * 128],
                                       rhs=one1[0:1, 0:1], start=True, stop=True)
                    return ins
                S.op("pe", fn, reads=[t_modrow, t_ones], writes=[t_pmodp[0 if blk < 4 else 1]])
            t_pmodp = [T(), T(), T()]
            t_modA = T()
            for blk in range(4):
                mod_block(blk)
            S.op("dve", lambda e: e.tensor_tensor(out=modT[:, 0:16], in0=pmodA[:, 0:16], in1=bad[:, 0:16], op=ALU.add),
                 reads=[t_pmodp[0], t_bad], writes=[t_modA])
            S.op("dve", lambda e: e.tensor_scalar_add(out=s1T[:], in0=modT[:, 8:16], scalar1=1.0),
                 reads=[t_modA], writes=[t_modA])
            for g in range(8):
                b = g % 2
                dma(S, xst[b][:], xT_v[:, :, g * 512:(g + 1) * 512], writes=[t_xst[b]], q="pool")
                for k in range(8):
                    eng = "dve" if k % 2 == 0 else "pool"
                    S.op(eng, lambda e, b=b, k=k, g=g: e.tensor_scalar(
                        out=h1T[:, k, g * 512:(g + 1) * 512], in0=xst[b][:, k, :],
                        scalar1=s1T[:, k:k + 1], scalar2=modT[:, k:k + 1], op0=ALU.mult, op1=ALU.add),
                        reads=[t_xst[b], t_modA], writes=[t_h1])
            for blk in range(4, 12):
                mod_block(blk)
            S.op("dve", lambda e: e.tensor_tensor(out=modT[:, 16:48], in0=pmod[:, 16:48], in1=bad[:, 16:48], op=ALU.add),
                 reads=[t_pmodp[1], t_bad], writes=[t_mod])
            S.op("dve", lambda e: e.tensor_scalar_add(out=A2[:], in0=modT[:, 32:40], scalar1=1.0),
                 reads=[t_mod], writes=[t_mod])
            S.op("dve", lambda e: e.tensor_tensor(out=B2[:], in0=l1T[:, 8:16], in1=A2[:], op=ALU.mult),
                 reads=[t_mod, t_l1], writes=[t_mod])
            S.op("dve", lambda e: e.tensor_tensor(out=B2[:], in0=B2[:], in1=modT[:, 24:32], op=ALU.add),
                 reads=[t_mod], writes=[t_mod])
            S.op("dve", lambda e: e.tensor_tensor(out=A2[:], in0=A2[:], in1=l1T[:, 0:8], op=ALU.mult),
                 reads=[t_mod, t_l1], writes=[t_mod])
            S.op("dve", lambda e: e.tensor_scalar_mul(out=g12[:, 0:8], in0=modT[:, 16:24], scalar1=1.0 / ALPHA),
                 reads=[t_mod], writes=[t_g12])
            S.op("dve", lambda e: e.tensor_scalar_mul(out=g12[:, 8:16], in0=modT[:, 40:48], scalar1=1.0 / ALPHA),
                 reads=[t_g12], writes=[t_g12])
            for which, dst in ((0, g1bc), (1, g2bc)):
                for k in range(8):
                    b = k % 2
                    S.op("dve", lambda e, b=b, k=k, which=which: e.tensor_scalar_mul(
                        out=dg[b][:], in0=identF[:], scalar1=g12[:, which * 8 + k:which * 8 + k + 1]),
                        reads=[t_g12, t_const], writes=[t_dg[b]])
                    S.op("pe", lambda e, b=b, k=k, which=which: e.matmul(
                        pbc[which][:, k * 128:(k + 1) * 128], lhsT=onesF[:], rhs=dg[b][:], start=True, stop=True),
                        reads=[t_dg[b], t_ones], writes=[t_pbc[which]])
                S.op("act", lambda e, which=which, dst=dst: e.copy(out=dst[:], in_=pbc[which][:]),
                     reads=[t_pbc[which]], writes=[t_const])
            S.end_phase()

        with ExitStack() as ph:
            wm = [sbuf(ph, "wm%d" % i, [128, 8, 384], BF16) for i in range(2)]
            tb = [sbuf(ph, "tb%d" % i, [128, 2, 512]) for i in range(2)]
            KT = [sbuf(ph, "KT%d" % i, [80, SEQ], BF16) for i in range(2)]
            QT = [sbuf(ph, "QT%d" % i, [80, NQ], BF16) for i in range(2)]
            VA = [sbuf(ph, "VA%d" % i, [128, 32, 128], BF16) for i in range(2)]
            PT = [sbuf(ph, "PT%d" % i, [128, 512], BF16) for i in range(4)]
            tmp1 = [sbuf(ph, "tmpa%d" % i, [128, 512]) for i in range(2)]
            tmp2 = [sbuf(ph, "tmpb%d" % i, [128, 512]) for i in range(2)]
            vb = sbuf(ph, "vb", [128, 272])
            ownhot = sbuf(ph, "ownhot", [128, 272])
            gm2 = [sbuf(ph, "gm%d" % i, [128, 272]) for i in range(2)]
            sel2 = [sbuf(ph, "sel%d" % i, [128, 272]) for i in range(2)]
            top82 = [sbuf(ph, "top8_%d" % i, [128, 17, 8]) for i in range(2)]
            thr2 = [sbuf(ph, "thr%d" % i, [128, 17]) for i in range(2)]
            mbt2 = [sbuf(ph, "mbt%d" % i, [128, 272], BF16) for i in range(2)]
            kmf2 = [sbuf(ph, "kmf%d" % i, [64, 16]) for i in range(2)]
            kmb2 = [sbuf(ph, "kmb%d" % i, [64, 16], BF16) for i in range(2)]
            dmask = sbuf(ph, "dmask", [128, 512], BF16)
            hmask = sbuf(ph, "hmask", [128, 256], BF16)
            rden = sbuf(ph, "rden", [128, 256])
            pk = [psum(ph, "pk%d" % i, [128, 512]) for i in range(2)]
            pv = psum(ph, "pv", [128, 512])
            pss = [psum(ph, "pss%d" % i, [128, 512]) for i in range(3)]
            po = [psum(ph, "po%d" % i, [128, 512]) for i in range(2)]
            t_wm = [T(), T()]
            t_tb = [T(), T()]
            t_tbs = [T(), T()]
            t_KT = [[T() for _ in range(8)] for _ in range(2)]
            t_QT = [[T() for _ in range(5)] for _ in range(2)]
            t_QA = [[T() for _ in range(5)] for _ in range(2)]
            t_VA = [[T() for _ in range(8)] for _ in range(2)]
            t_PT = [T() for _ in range(4)]
            t_t1 = [T(), T()]
            t_t2 = [T(), T()]
            t_pk = [T(), T()]
            t_pv = T()
            t_pss = [T(), T(), T()]
            t_po = [T(), T()]
            t_msk, t_rden, t_AT = (T() for _ in range(3))
            t_gm2, t_sel2, t_top82, t_thr2, t_mbt2, t_kmf2, t_kmb2 = ([T(), T()] for _ in range(7))

            dma(S, vb[:], vb_d[:, :], writes=[t_msk])
            dma(S, ownhot[:], own_d[:, :], writes=[t_msk])
            dma(S, dmask[:], dmask_d[:, :], writes=[t_msk], q="pool")
            dma(S, hmask[:], hmask_d[:, :], writes=[t_msk], q="pool")
            for e2 in range(2):
                dma(S, KT[e2][64:80, :], onehot_d[:, :], writes=[t_KT[e2][0]], q="pool")
            S.op("dve", lambda e: e.memset(VA[0][:, :, 64:128], 1.0), writes=[t_VA[0][0]])
            S.op("dve", lambda e: e.memset(VA[1][:, :, 0:64], 1.0), writes=[t_VA[1][0]])
            w_moba_v = w_moba.rearrange("h (k p) n -> h p k n", p=128)
            cnt = {"tb": 0, "pss": 0, "pt": 0, "po": 0}

            def rope_side(hp, wcol, tsl, n, dst, dsl, t_dst):
                w = wm[hp % 2]
                b = cnt["tb"] % 2
                cnt["tb"] += 1
                dma(S, tb[b][:, 0, 0:n], cosM[:, tsl], writes=[t_tb[b]])
                dma(S, tb[b][:, 1, 0:n], sinM[:, tsl], writes=[t_tbs[b]])
                mm_group(S, pk[b][:, 0:n],
                         [(w[:, k, wcol:wcol + 128], h1T[:, k, tsl]) for k in range(8)],
                         reads=[t_wm[hp % 2], t_h1], writes=[t_pk[b]])
                S.op("dve", lambda e, b=b, n=n: e.tensor_tensor(out=tmp1[b][:, 0:n], in0=pk[b][:, 0:n],
                                                                in1=tb[b][:, 0, 0:n], op=ALU.mult),
                     reads=[t_pk[b], t_tb[b]], writes=[t_t1[b]])
                for (o_lo, i_lo) in ((0, 32), (32, 0), (64, 96), (96, 64)):
                    S.op("dve", lambda e, b=b, n=n, o_lo=o_lo, i_lo=i_lo: e.tensor_tensor(
                        out=tmp2[b][o_lo:o_lo + 32, 0:n], in0=pk[b][i_lo:i_lo + 32, 0:n],
                        in1=tb[b][o_lo:o_lo + 32, 1, 0:n], op=ALU.mult),
                        reads=[t_pk[b], t_tbs[b]], writes=[t_t2[b]])
                for e2 in range(2):
                    S.op("pool", lambda e, b=b, n=n, e2=e2: e.tensor_tensor(
                        out=dst[e2][0:64, dsl], in0=tmp1[b][64 * e2:64 * e2 + 64, 0:n],
                        in1=tmp2[b][64 * e2:64 * e2 + 64, 0:n], op=ALU.add),
                        reads=[t_t1[b], t_t2[b]], writes=[t_dst[e2]])

            dma(S, wm[0][:], w_moba_v[0], writes=[t_wm[0]], q="pool")
            for hp in range(4):
                w = wm[hp % 2]
                if hp < 3:
                    dma(S, wm[(hp + 1) % 2][:], w_moba_v[hp + 1], writes=[t_wm[(hp + 1) % 2]], q="pool")
                def vproj(g, w=w, hp=hp):
                    def fnv(e, g=g, w=w):
                        ins = None
                        for j in range(4):
                            t = g * 4 + j
                            for k in range(8):
                                ins = e.matmul(pv[:, j * 128:(j + 1) * 128], lhsT=h1T[:, k, t * 128:(t + 1) * 128],
                                               rhs=w[:, k, 256:384], start=(k == 0), stop=(k == 7))
                        return ins
                    S.op("pe", fnv, reads=[t_wm[hp % 2], t_h1], writes=[t_pv])
                    pv3 = pv[:].rearrange("p (j n) -> p j n", n=128)
                    S.op("act", lambda e, g=g, pv3=pv3: e.copy(out=VA[0][:, g * 4:(g + 1) * 4, 0:64], in_=pv3[:, :, 0:64]),
                         reads=[t_pv], writes=[t_VA[0][g]])
                    S.op("act", lambda e, g=g, pv3=pv3: e.copy(out=VA[1][:, g * 4:(g + 1) * 4, 64:128], in_=pv3[:, :, 64:128]),
                         reads=[t_pv], writes=[t_VA[1][g]])

                for g in range(8):
                    rope_side(hp, 0, slice(g * 512, (g + 1) * 512), 512, KT, slice(g * 512, (g + 1) * 512),
                              [t_KT[0][g], t_KT[1][g]])
                    if g < 6:
                        vproj(g)
                for gi, (q0, n, l0) in enumerate(QG):
                    rope_side(hp, 128, slice(l0, l0 + n), n, QT, slice(q0, q0 + n), [t_QT[0][gi], t_QT[1][gi]])

                def gateA1(e2):
                    S.op("dve", lambda e, e2=e2: e.tensor_reduce(
                        out=kmf2[e2][:], in_=KT[e2][0:64, :].rearrange("p (j n) -> p j n", n=256), axis=AX.X, op=ALU.add),
                        reads=t_KT[e2], writes=[t_kmf2[e2]])
                    S.op("act", lambda e, e2=e2: e.mul(out=kmb2[e2][:], in_=kmf2[e2][:], mul=1.0 / 256.0),
                         reads=[t_kmf2[e2]], writes=[t_kmb2[e2]])

                def gateA2(e2):
                    gm, sel, top8, thr, mbt = gm2[e2], sel2[e2], top82[e2], thr2[e2], mbt2[e2]

                    def fng(e):
                        ins = None
                        for qt in range(17):
                            ins = e.matmul(pk[0][:, qt * 16:(qt + 1) * 16], lhsT=QT[e2][0:64, qt * 128:(qt + 1) * 128],
                                           rhs=kmb2[e2][:], start=True, stop=True)
                        return ins
                    S.op("pe", fng, reads=t_QT[e2] + [t_kmb2[e2]], writes=[t_pk[0]])
                    S.op("dve", lambda e: e.tensor_tensor(out=gm[:], in0=pk[0][:, 0:272], in1=vb[:], op=ALU.add),
                         reads=[t_pk[0], t_msk], writes=[t_gm2[e2]])

                    def fnmax(e):
                        ins = None
                        for qt in range(17):
                            ins = e.max(out=top8[:, qt, :], in_=gm[:, qt * 16:(qt + 1) * 16])
                        return ins
                    S.op("dve", fnmax, reads=[t_gm2[e2]], writes=[t_top82[e2]])
                    S.op("dve", lambda e: e.tensor_scalar_max(out=thr[:], in0=top8[:, :, 2], scalar1=-10000.0),
                         reads=[t_top82[e2]], writes=[t_thr2[e2]])
                    S.op("dve", lambda e: e.tensor_tensor(
                        out=sel[:].rearrange("p (t j) -> p t j", j=16), in0=gm[:].rearrange("p (t j) -> p t j", j=16),
                        in1=thr[:].unsqueeze(2).to_broadcast([128, 17, 16]), op=ALU.is_ge),
                        reads=[t_gm2[e2], t_thr2[e2]], writes=[t_sel2[e2]])
                    S.op("dve", lambda e: e.tensor_tensor(out=sel[:], in0=sel[:], in1=ownhot[:], op=ALU.max),
                         reads=[t_sel2[e2], t_msk], writes=[t_sel2[e2]])
                    S.op("dve", lambda e: e.tensor_scalar(out=mbt[:], in0=sel[:], scalar1=-1.0, scalar2=BIG,
                                                          op0=ALU.add, op1=ALU.mult),
                         reads=[t_sel2[e2]], writes=[t_mbt2[e2]])

                def gateB(e2):
                    mbt = mbt2[e2]
                    for gi, (q0, n, l0) in enumerate(QG):
                        def fnt(e, q0=q0, n=n):
                            ins = None
                            for j in range(n // 128):
                                qt = q0 // 128 + j
                                ins = e.matmul(pk[1][0:16, j * 128:(j + 1) * 128], lhsT=mbt[:, qt * 16:(qt + 1) * 16],
                                               rhs=identB[:], start=True, stop=True)
                            return ins
                        S.op("pe", fnt, reads=[t_mbt2[e2], t_const], writes=[t_pk[1]])
                        S.op("act", lambda e, q0=q0, n=n: e.copy(out=QT[e2][64:80, q0:q0 + n], in_=pk[1][0:16, 0:n]),
                             reads=[t_pk[1]], writes=[t_QA[e2][gi]])

                gateA1(0)
                gateA2(0)
                gateA1(1)
                vproj(6)
                vproj(7)
                gateB(0)
                units = []
                for e2 in range(2):
                    for qb in range(9):
                        if qb == 0:
                            q0, nq, lb, gi = 0, 128, 7, 0
                        else:
                            q0, nq, lb, gi = 128 + 256 * (qb - 1), 256, 7 + qb, 1 + (qb - 1) // 2
                        npair = lb + 1
                        pb = cnt["po"] % 2
                        cnt["po"] += 1
                        for pi in range(npair):
                            units.append((e2, qb, q0, nq, gi, npair, pb, pi))
                SKEW = 2
                bufs = {}
                for idx in range(len(units) + SKEW):
                    if idx == 8:
                        gateA2(1)
                    if idx == 28:
                        gateB(1)
                    if idx < len(units):
                        e2, qb, q0, nq, gi, npair, pb, pi = units[idx]
                        sb_ = cnt["pss"] % 3
                        cnt["pss"] += 1
                        ptb = cnt["pt"] % 4
                        cnt["pt"] += 1
                        bufs[idx] = ptb

                        def fqk(e, e2=e2, pi=pi, q0=q0, nq=nq, sb_=sb_):
                            ins = None
                            for j in range(2):
                                kt = 2 * pi + j
                                ins = e.matmul(pss[sb_][:, j * nq:(j + 1) * nq], lhsT=KT[e2][0:80, kt * 128:(kt + 1) * 128],
                                               rhs=QT[e2][0:80, q0:q0 + nq], start=True, stop=True)
                            return ins
                        S.op("pe", fqk, reads=[t_KT[e2][pi // 2], t_KT[e2][0], t_QT[e2][gi], t_QA[e2][gi]],
                             writes=[t_pss[sb_]])
                        S.op("act", lambda e, sb_=sb_, ptb=ptb, nq=nq: e.activation(
                            out=PT[ptb][:, 0:2 * nq], in_=pss[sb_][:, 0:2 * nq], func=AF.Exp, scale=0.125),
                            reads=[t_pss[sb_]], writes=[t_PT[ptb]])
                        if pi == npair - 1:
                            msk = hmask if qb == 0 else dmask
                            S.op("pool", lambda e, ptb=ptb, nq=nq, msk=msk: e.tensor_tensor(
                                out=PT[ptb][:, 0:2 * nq], in0=PT[ptb][:, 0:2 * nq], in1=msk[:, 0:2 * nq], op=ALU.mult),
                                reads=[t_PT[ptb], t_msk], writes=[t_PT[ptb]])
                    if idx - SKEW >= 0:
                        e2, qb, q0, nq, gi, npair, pb, pi = units[idx - SKEW]
                        ptb = bufs.pop(idx - SKEW)

                        def fpv(e, e2=e2, pi=pi, nq=nq, ptb=ptb, pb=pb, npair=npair):
                            ins = None
                            for j in range(2):
                                kt = 2 * pi + j
                                ins = e.matmul(po[pb][:, 0:nq], lhsT=VA[e2][:, kt, :], rhs=PT[ptb][:, j * nq:(j + 1) * nq],
                                               start=(pi == 0 and j == 0), stop=(pi == npair - 1 and j == 1))
                            return ins
                        S.op("pe", fpv, reads=[t_VA[e2][pi // 2], t_VA[e2][0], t_PT[ptb]], writes=[t_po[pb]])
                        if pi == npair - 1:
                            nlo, dlo = (0, 64) if e2 == 0 else (64, 0)
                            S.op("dve", lambda e, pb=pb, nq=nq, nlo=nlo, dlo=dlo: e.reciprocal(
                                out=rden[nlo:nlo + 64, 0:nq], in_=po[pb][dlo:dlo + 64, 0:nq]),
                                reads=[t_po[pb]], writes=[t_rden])
                            S.op("dve", lambda e, pb=pb, nq=nq, nlo=nlo, hp=hp, q0=q0: e.tensor_tensor(
                                out=AT[nlo:nlo + 64, hp, q0:q0 + nq], in0=po[pb][nlo:nlo + 64, 0:nq],
                                in1=rden[nlo:nlo + 64, 0:nq], op=ALU.mult),
                                reads=[t_po[pb], t_rden], writes=[t_AT])
            S.end_phase()

        rts = ExitStack()
        rts.__enter__()
        RT = sbuf(rts, "RT", [128, 8, NQ], BF16)
        with ExitStack() as ph:
            wkq = sbuf(ph, "wkq", [128, 8, 256], BF16)
            wvg2 = [sbuf(ph, "wvg%d" % i, [128, 8, 512], BF16) for i in range(2)]
            tb = [sbuf(ph, "rtb%d" % i, [128, 2, 512]) for i in range(2)]
            tmp1 = [sbuf(ph, "rtmpa%d" % i, [128, 512]) for i in range(2)]
            tmp2 = [sbuf(ph, "rtmpb%d" % i, [128, 512]) for i in range(2)]
            ktg = [sbuf(ph, "ktg%d" % i, [128, 512], BF16) for i in range(2)]
            KTr = sbuf(ph, "KTr", [128, 17 * 128], BF16)
            Ktok = sbuf(ph, "Ktok", [128, 17, 128], BF16)
            Ktp = [sbuf(ph, "Ktp%d" % i, [128, 4, 128], BF16) for i in range(2)]
            Vr = sbuf(ph, "Vr", [128, 17, 256], BF16)
            Vp = [sbuf(ph, "Vp%d" % i, [128, 4, 256], BF16) for i in range(2)]
            QTr = sbuf(ph, "QTr", [128, NQ], BF16)
            SG = [sbuf(ph, "SG%d" % i, [128, 256], BF16) for i in range(4)]
            ex = [sbuf(ph, "ex%d" % i, [128, 256]) for i in range(2)]
            t_ex = [T(), T()]
            Sst = sbuf(ph, "Sst", [128, 256])
            S16 = [sbuf(ph, "S16_%d" % i, [128, 256], BF16) for i in range(2)]
            SD = [sbuf(ph, "SD%d" % i, [128, 128], BF16) for i in range(2)]
            yn = [sbuf(ph, "yn%d" % i, [128, 256], BF16) for i in range(2)]
            Rtok = [sbuf(ph, "Rtok%d" % i, [128, 256], BF16) for i in range(2)]
            st6 = [sbuf(ph, "st6_%d" % i, [128, 6]) for i in range(2)]
            mvr = [sbuf(ph, "mvr%d" % i, [128, 2]) for i in range(2)]
            rstdr = [sbuf(ph, "rstdr%d" % i, [128, 1]) for i in range(2)]
            ksp = sbuf(ph, "ksp", [128, 128])
            epsq = sbuf(ph, "epsq", [128, 4])
            tri = sbuf(ph, "tri", [128, 128])
            pk = [psum(ph, "rpk%d" % i, [128, 512]) for i in range(2)]
            pv = [psum(ph, "rpv%d" % i, [128, 512]) for i in range(2)]
            ptr = psum(ph, "ptr", [128, 1024], BF16)
            pS = psum(ph, "pS", [128, 512])
            py = [psum(ph, "py%d" % i, [128, 512]) for i in range(2)]
            t_wkq = T()
            t_wvg2 = [T(), T()]
            t_tb = [T(), T()]
            t_tbs = [T(), T()]
            t_t1 = [T(), T()]
            t_t2 = [T(), T()]
            t_ktg = [T(), T()]
            t_KTr, t_Ktok, t_Vr, t_QTr, t_S, t_cst = (T() for _ in range(6))
            t_SG = [T(), T(), T(), T()]
            t_psc = [T(), T()]
            t_stt = [T(), T()]
            t_ptr2 = [T(), T()]
            t_Ktp = [T(), T()]
            t_Vp = [T(), T()]
            t_S16 = [T(), T()]
            t_SD = [T(), T()]
            t_yn = [T(), T()]
            t_Rtok = [T(), T()]
            t_st, t_RT = T(), T()
            t_pk = [T(), T()]
            t_pv = [T(), T()]
            t_ptr, t_pS = T(), T()
            t_py = [T(), T()]
            dma(S, ksp[:], ksp_d[:, :], writes=[t_cst])
            dma(S, epsq[:], epsq_d[:, :], writes=[t_cst])
            dma(S, tri[:], tri_d[:, :], writes=[t_cst])
            w_ret_v = w_ret.rearrange("h (k p) n -> h p k n", p=128)
            cnt = {"tb": 0, "pv": 0, "py": 0, "s16": 0, "sd": 0}

            def rrope(wcol, tsl, n, dst_ap, t_dst):
                b = cnt["tb"] % 2
                cnt["tb"] += 1
                dma(S, tb[b][:, 0, 0:n], cosR[:, tsl], writes=[t_tb[b]])
                dma(S, tb[b][:, 1, 0:n], sinR[:, tsl], writes=[t_tbs[b]])
                mm_group(S, pk[b][:, 0:n],
                         [(wkq[:, k, wcol:wcol + 128], h1T[:, k, tsl]) for k in range(8)],
                         reads=[t_wkq, t_h1], writes=[t_pk[b]])
                S.op("dve", lambda e, b=b, n=n: e.tensor_tensor(out=tmp1[b][:, 0:n], in0=pk[b][:, 0:n],
                                                                in1=tb[b][:, 0, 0:n], op=ALU.mult),
                     reads=[t_pk[b], t_tb[b]], writes=[t_t1[b]])
                S.op("dve", lambda e, b=b, n=n: e.tensor_tensor(out=tmp2[b][0:64, 0:n], in0=pk[b][64:128, 0:n],
                                                                in1=tb[b][0:64, 1, 0:n], op=ALU.mult),
                     reads=[t_pk[b], t_tbs[b]], writes=[t_t2[b]])
                S.op("dve", lambda e, b=b, n=n: e.tensor_tensor(out=tmp2[b][64:128, 0:n], in0=pk[b][0:64, 0:n],
                                                                in1=tb[b][64:128, 1, 0:n], op=ALU.mult),
                     reads=[t_pk[b], t_tbs[b]], writes=[t_t2[b]])
                S.op("pool", lambda e, b=b, n=n: e.tensor_tensor(out=dst_ap, in0=tmp1[b][:, 0:n], in1=tmp2[b][:, 0:n],
                                                                 op=ALU.add),
                     reads=[t_t1[b], t_t2[b]], writes=[t_dst])

            dma(S, wkq[:], w_ret_v[0][:, :, 0:256], writes=[t_wkq], q="pool")
            dma(S, wvg2[0][:], w_ret_v[0][:, :, 256:768], writes=[t_wvg2[0]], q="pool")
            for r in range(4):
                wvg = wvg2[r % 2]
                t_wvg = t_wvg2[r % 2]
                if r < 3:
                    dma(S, wvg2[(r + 1) % 2][:], w_ret_v[r + 1][:, :, 256:768], writes=[t_wvg2[(r + 1) % 2]], q="pool")
                for g in range(8):
                    tsl = slice(g * 512, (g + 1) * 512)
                    kb = g % 2
                    if g < 4:
                        rrope(0, tsl, 512, ktg[kb][:, :], t_ktg[kb])
                    else:
                        c0 = 4 * g - 15
                        rrope(0, tsl, 512, KTr[:, c0 * 128:(c0 + 4) * 128], t_KTr)
                    if g == 3:
                        S.op("pool", lambda e, kb=kb: e.tensor_copy(out=KTr[:, 0:128], in_=ktg[kb][:, 384:512]),
                             reads=[t_ktg[kb]], writes=[t_KTr])
                    for half in range(2):
                        vb_ = cnt["pv"] % 2
                        cnt["pv"] += 1

                        def fv(e, g=g, half=half, vb_=vb_, wvg=wvg):
                            ins = None
                            for j in range(2):
                                t = g * 4 + half * 2 + j
                                for k in range(8):
                                    ins = e.matmul(pv[vb_][:, j * 256:(j + 1) * 256], lhsT=h1T[:, k, t * 128:(t + 1) * 128],
                                                   rhs=wvg[:, k, 0:256], start=(k == 0), stop=(k == 7))
                            return ins
                        S.op("pe", fv, reads=[t_wvg, t_h1], writes=[t_pv[vb_]])
                        for j in range(2):
                            n_ = g * 4 + half * 2 + j
                            col = r * 32 + n_
                            if n_ <= 14:
                                dstap, tdst = Vp[kb][:, half * 2 + j, :], t_Vp[kb]
                            else:
                                dstap, tdst = Vr[:, n_ - 15, :], t_Vr
                            S.op("act", lambda e, vb_=vb_, j=j, col=col, dstap=dstap: e.activation(
                                out=dstap, in_=pv[vb_][:, j * 256:(j + 1) * 256], func=AF.Copy, scale=ksp[:, col:col + 1]),
                                reads=[t_pv[vb_], t_cst], writes=[tdst])
                    src = ktg[kb] if g < 4 else None

                    def ftr(e, g=g, kb=kb):
                        ins = None
                        for j in range(4):
                            if g < 4:
                                in_ap = ktg[kb][:, j * 128:(j + 1) * 128]
                            else:
                                c = 4 * g - 15 + j
                                in_ap = KTr[:, c * 128:(c + 1) * 128]
                            ins = e.transpose(ptr[:, j * 128:(j + 1) * 128], in_ap, identB[:])
                        return ins
                    S.op("pe", ftr, reads=[t_ktg[kb] if g < 4 else t_KTr, t_const], writes=[t_ptr])
                    ptr3 = ptr[:, 0:512].rearrange("p (j n) -> p j n", n=128)
                    if g < 3:
                        S.op("act", lambda e, kb=kb, ptr3=ptr3, r=r: e.mul(out=Ktp[kb][:], in_=ptr3, mul=gC[r]),
                             reads=[t_ptr], writes=[t_Ktp[kb]])
                    elif g == 3:
                        S.op("act", lambda e, kb=kb, ptr3=ptr3, r=r: e.mul(out=Ktp[kb][:, 0:3, :], in_=ptr3[:, 0:3, :], mul=gC[r]),
                             reads=[t_ptr], writes=[t_Ktp[kb]])
                        S.op("act", lambda e, ptr3=ptr3, r=r: e.mul(out=Ktok[:, 0, :], in_=ptr3[:, 3, :], mul=gC[r]),
                             reads=[t_ptr], writes=[t_Ktok])
                    else:
                        c0 = 4 * g - 15
                        S.op("act", lambda e, c0=c0, ptr3=ptr3, r=r: e.mul(out=Ktok[:, c0:c0 + 4, :], in_=ptr3, mul=gC[r]),
                             reads=[t_ptr], writes=[t_Ktok])
                    if g < 4:
                        nn = 4 if g < 3 else 3

                        def fs(e, g=g, kb=kb, nn=nn):
                            ins = None
                            for j in range(nn):
                                n_ = g * 4 + j
                                ins = e.matmul(pS[:, 0:256], lhsT=Ktp[kb][:, j, :], rhs=Vp[kb][:, j, :],
                                               start=(n_ == 0), stop=(n_ == 14))
                            return ins
                        S.op("pe", fs, reads=[t_Ktp[kb], t_Vp[kb]], writes=[t_pS])
                S.op("dve", lambda e: e.tensor_copy(out=Sst[:], in_=pS[:, 0:256]), reads=[t_pS], writes=[t_S])
                sbi = cnt["s16"] % 2
                cnt["s16"] += 1
                S.op("act", lambda e, sbi=sbi: e.copy(out=S16[sbi][:], in_=pS[:, 0:256]), reads=[t_pS], writes=[t_S16[sbi]])
                for gi, (q0, n, l0) in enumerate(QG):
                    rrope(128, slice(l0, l0 + n), n, QTr[:, q0:q0 + n], t_QTr)
                if r < 3:
                    dma(S, wkq[:], w_ret_v[r + 1][:, :, 0:256], writes=[t_wkq], q="pool")
                s16_of = {}

                def stageA(qc, wvg=wvg, t_wvg=t_wvg):
                    csl = slice(qc * 128, (qc + 1) * 128)
                    l0 = 1920 + qc * 128
                    vb_ = cnt["pv"] % 2
                    cnt["pv"] += 1
                    g3 = qc % 4
                    x2 = qc % 2
                    p2 = qc % 2
                    mm_group(S, pv[vb_][:, 0:256], [(h1T[:, k, l0:l0 + 128], wvg[:, k, 256:512]) for k in range(8)],
                             reads=[t_wvg, t_h1], writes=[t_pv[vb_]])
                    S.op("act", lambda e, vb_=vb_, x2=x2: e.activation(out=ex[x2][:], in_=pv[vb_][:, 0:256], func=AF.Exp, scale=-1.0),
                         reads=[t_pv[vb_]], writes=[t_ex[x2]])
                    S.op("act", lambda e, x2=x2: e.activation(out=ex[x2][:], in_=ex[x2][:], func=AF.Ln, bias=oneA[:], scale=1.0),
                         reads=[t_ex[x2], t_const], writes=[t_ex[x2]])
                    S.op("act", lambda e, x2=x2: e.activation(out=ex[x2][:], in_=ex[x2][:], func=AF.Exp, scale=-1.0),
                         reads=[t_ex[x2]], writes=[t_ex[x2]])
                    S.op("dve", lambda e, vb_=vb_, g3=g3, x2=x2: e.tensor_tensor(out=SG[g3][:], in0=pv[vb_][:, 0:256], in1=ex[x2][:], op=ALU.mult),
                         reads=[t_pv[vb_], t_ex[x2]], writes=[t_SG[g3]])
                    S.op("pe", lambda e, csl=csl, p2=p2: e.matmul(pk[p2][:, 0:128], lhsT=KTr[:, csl], rhs=QTr[:, csl],
                                                               start=True, stop=True),
                         reads=[t_KTr, t_QTr], writes=[t_pk[p2]])
                    S.op("dve", lambda e, p2=p2: e.tensor_tensor(out=SD[p2][:], in0=pk[p2][:, 0:128], in1=tri[:], op=ALU.mult),
                         reads=[t_pk[p2], t_cst], writes=[t_SD[p2]])

                def stageB(qc, r=r):
                    csl = slice(qc * 128, (qc + 1) * 128)
                    p2 = qc % 2
                    sbi = s16_of[qc]

                    def fy(e, qc=qc, csl=csl, p2=p2, sbi=sbi):
                        e.matmul(py[p2][:, 0:256], lhsT=SD[p2][:], rhs=Vr[:, qc, :], start=True, stop=False)
                        return e.matmul(py[p2][:, 0:256], lhsT=QTr[:, csl], rhs=S16[sbi][:], start=False, stop=True)
                    S.op("pe", fy, reads=[t_SD[p2], t_Vr, t_QTr, t_S16[sbi]], writes=[t_py[p2]])
                    if qc < 16:
                        S.op("pe", lambda e, qc=qc: e.matmul(pS[:, 0:256], lhsT=Ktok[:, qc, :], rhs=Vr[:, qc, :], start=True, stop=True),
                             reads=[t_Ktok, t_Vr], writes=[t_pS])
                        S.op("dve", lambda e, r=r: e.scalar_tensor_tensor(out=Sst[:], in0=Sst[:], scalar=gC[r], in1=pS[:, 0:256],
                                                                          op0=ALU.mult, op1=ALU.add),
                             reads=[t_pS, t_S], writes=[t_S])
                        nsb = 1 - sbi
                        s16_of[qc + 1] = nsb
                        S.op("act", lambda e, nsb=nsb: e.copy(out=S16[nsb][:], in_=Sst[:]), reads=[t_S], writes=[t_S16[nsb]])

                def stageC1(qc, r=r):
                    p2 = qc % 2
                    S.op("dve", lambda e, p2=p2: e.bn_stats(out=st6[p2][:], in_=py[p2][:, 0:256]), reads=[t_py[p2]], writes=[t_stt[p2]])
                    S.op("dve", lambda e, p2=p2: e.bn_aggr(out=mvr[p2][:], in_=st6[p2][:]), reads=[t_stt[p2]], writes=[t_stt[p2]])
                    S.op("act", lambda e, r=r, p2=p2: e.activation(out=rstdr[p2][:], in_=mvr[p2][:, 1:2], func=AF.Ln,
                                                                   bias=epsq[:, r:r + 1], scale=1.0),
                         reads=[t_stt[p2], t_cst], writes=[t_stt[p2]])
                    S.op("act", lambda e, p2=p2: e.activation(out=rstdr[p2][:], in_=rstdr[p2][:], func=AF.Exp, scale=-0.5),
                         reads=[t_stt[p2]], writes=[t_stt[p2]])
                    S.op("dve", lambda e, p2=p2: e.tensor_scalar(out=yn[p2][:], in0=py[p2][:, 0:256], scalar1=mvr[p2][:, 0:1],
                                                                 scalar2=rstdr[p2][:], op0=ALU.subtract, op1=ALU.mult),
                         reads=[t_py[p2], t_stt[p2]], writes=[t_yn[p2]])

                def stageC2(qc, r=r):
                    csl = slice(qc * 128, (qc + 1) * 128)
                    p2 = qc % 2
                    g3 = qc % 4
                    S.op("pool", lambda e, p2=p2, g3=g3: e.tensor_tensor(out=Rtok[p2][:], in0=yn[p2][:], in1=SG[g3][:], op=ALU.mult),
                         reads=[t_yn[p2], t_SG[g3]], writes=[t_Rtok[p2]])
                    o0 = 512

                    def ftr2(e, p2=p2, o0=o0):
                        e.transpose(ptr[:, o0:o0 + 128], Rtok[p2][:, 0:128], identB[:])
                        return e.transpose(ptr[:, o0 + 128:o0 + 256], Rtok[p2][:, 128:256], identB[:])
                    S.op("pe", ftr2, reads=[t_Rtok[p2], t_const], writes=[t_ptr])
                    S.op("act", lambda e, r=r, csl=csl, o0=o0: e.copy(
                        out=RT[:, 2 * r:2 * r + 2, csl], in_=ptr[:, o0:o0 + 256].rearrange("p (j n) -> p j n", n=128)),
                        reads=[t_ptr], writes=[t_RT])

                s16_of[0] = sbi
                for st in range(17 + 3):
                    if st < 17:
                        stageA(st)
                    if 0 <= st - 1 < 17:
                        stageB(st - 1)
                    if 0 <= st - 2 < 17:
                        stageC1(st - 2)
                    if 0 <= st - 3 < 17:
                        stageC2(st - 3)
            S.end_phase()

        with ExitStack() as ph:
            wg = [sbuf(ph, "wg%d" % i, [128, 8, 256], BF16) for i in range(2)]
            wpa = [sbuf(ph, "wpa%d" % i, [128, 4, 128], BF16) for i in range(2)]
            wpr = [sbuf(ph, "wpr%d" % i, [128, 8, 128], BF16) for i in range(2)]
            sa = [sbuf(ph, "sa%d" % i, [128, 512]) for i in range(2)]
            sr = [sbuf(ph, "sr%d" % i, [128, 512]) for i in range(2)]
            m1 = [sbuf(ph, "m1%d" % i, [128, 512]) for i in range(2)]
            m2 = [sbuf(ph, "m2%d" % i, [128, 512]) for i in range(2)]
            mT = [sbuf(ph, "mT%d" % i, [128, 512], BF16) for i in range(2)]
            pg = [[psum(ph, "pg%d_%d" % (i, j), [128, 512]) for j in range(4)] for i in range(2)]
            t_w = [T(), T()]
            t_pg = [[T() for _ in range(4)] for _ in range(2)]
            t_sa, t_sr, t_m1, t_m2, t_mT = ([T(), T()] for _ in range(5))
            t_AT2, t_RT2 = T(), T()
            w_g_v = w_g.rearrange("(k p) n -> p k n", p=128)
            w_pa_v = w_pa.rearrange("(k p) n -> p k n", p=128)
            w_pr_v = w_pr.rearrange("(k p) n -> p k n", p=128)
            it = 0

            def load_gw(f):
                wb = f % 2
                fs_ = slice(f * 128, (f + 1) * 128)
                dma(S, wg[wb][:, :, 0:128], w_g_v[:, :, f * 128:(f + 1) * 128], writes=[t_w[wb]], q="pool")
                dma(S, wg[wb][:, :, 128:256], w_g_v[:, :, 1024 + f * 128:1024 + (f + 1) * 128], writes=[t_w[wb]], q="pool")
                dma(S, wpa[wb][:], w_pa_v[:, :, fs_], writes=[t_w[wb]], q="pool")
                dma(S, wpr[wb][:], w_pr_v[:, :, fs_], writes=[t_w[wb]], q="pool")
            load_gw(0)
            for f in range(8):
                wb = f % 2
                if f < 7:
                    load_gw(f + 1)
                for gi, (q0, n, l0) in enumerate(QG):
                    b = it % 2
                    it += 1
                    mm_group(S, pg[b][0][:, 0:n], [(wg[wb][:, k, 0:128], h1T[:, k, l0:l0 + n]) for k in range(8)],
                             reads=[t_w[wb], t_h1], writes=[t_pg[b][0]])
                    mm_group(S, pg[b][1][:, 0:n], [(wg[wb][:, k, 128:256], h1T[:, k, l0:l0 + n]) for k in range(8)],
                             reads=[t_w[wb], t_h1], writes=[t_pg[b][1]])
                    mm_group(S, pg[b][2][:, 0:n], [(wpa[wb][:, k, :], AT[:, k, q0:q0 + n]) for k in range(4)],
                             reads=[t_w[wb], t_AT2], writes=[t_pg[b][2]])
                    mm_group(S, pg[b][3][:, 0:n], [(wpr[wb][:, k, :], RT[:, k, q0:q0 + n]) for k in range(8)],
                             reads=[t_w[wb], t_RT2], writes=[t_pg[b][3]])
                    S.op("act", lambda e, b=b, n=n: e.activation(out=sa[b][:, 0:n], in_=pg[b][0][:, 0:n], func=AF.Sigmoid),
                         reads=[t_pg[b][0]], writes=[t_sa[b]])
                    S.op("act", lambda e, b=b, n=n: e.activation(out=sr[b][:, 0:n], in_=pg[b][1][:, 0:n], func=AF.Sigmoid),
                         reads=[t_pg[b][1]], writes=[t_sr[b]])
                    S.op("dve", lambda e, b=b, n=n: e.tensor_tensor(out=m1[b][:, 0:n], in0=pg[b][2][:, 0:n], in1=sa[b][:, 0:n], op=ALU.mult),
                         reads=[t_pg[b][2], t_sa[b]], writes=[t_m1[b]])
                    S.op("dve", lambda e, b=b, n=n: e.tensor_tensor(out=m2[b][:, 0:n], in0=pg[b][3][:, 0:n], in1=sr[b][:, 0:n], op=ALU.mult),
                         reads=[t_pg[b][3], t_sr[b]], writes=[t_m2[b]])
                    S.op("pool", lambda e, b=b, n=n: e.tensor_tensor(out=mT[b][:, 0:n], in0=m1[b][:, 0:n], in1=m2[b][:, 0:n], op=ALU.add),
                         reads=[t_m1[b], t_m2[b]], writes=[t_mT[b]])
                    dma(S, scrM[q0 // 128:(q0 + n) // 128, :, f, :].rearrange("t p c -> p t c"),
                        mT[b][:, 0:n].rearrange("p (t c) -> p t c", c=128), reads=[t_mT[b]])
            S.end_phase()
        rts.close()
        mix.close()

        with ExitStack() as ph:
            NBF = 3
            wo = sbuf(ph, "wo", [128, 8, D], BF16)
            lg = sbuf(ph, "lg", [128, D])
            lb_ = sbuf(ph, "lb", [128, D])
            mTt = [sbuf(ph, "mTt%d" % i, [128, 8, 128], BF16) for i in range(NBF)]
            xt = [sbuf(ph, "xt%d" % i, [128, D]) for i in range(NBF)]
            z = [sbuf(ph, "z%d" % i, [128, D]) for i in range(NBF)]
            zn = [sbuf(ph, "zn%d" % i, [128, D]) for i in range(NBF)]
            x1 = [sbuf(ph, "x1%d" % i, [128, D]) for i in range(NBF)]
            h2t = [sbuf(ph, "h2t%d" % i, [128, 8, 128], BF16) for i in range(NBF)]
            st12 = [sbuf(ph, "st12_%d" % i, [128, 2, 6]) for i in range(NBF)]
            mv = [sbuf(ph, "mv1_%d" % i, [128, 2]) for i in range(NBF)]
            rstd = [sbuf(ph, "rstd1_%d" % i, [128, 1]) for i in range(NBF)]
            py = [psum(ph, "lpy%d" % i, [128, 1024]) for i in range(2)]
            pz = [psum(ph, "lpz%d" % i, [128, 1024]) for i in range(2)]
            t_wo, t_lc = T(), T()
            t_mTt, t_xt, t_z, t_zn, t_x1, t_h2t, t_st = ([T() for _ in range(NBF)] for _ in range(7))
            t_py, t_pz = [T(), T()], [T(), T()]
            dma(S, wo[:], w_o.rearrange("(k p) n -> p k n", p=128), writes=[t_wo], q="pool")
            dma(S, lg[:], lnbc[0], writes=[t_lc])
            dma(S, lb_[:], lnbc[1], writes=[t_lc])

            def l1_A(qt):
                b = qt % NBF
                pb = qt % 2
                dma(S, mTt[b][:], scrM[qt], writes=[t_mTt[b]])
                dma(S, xt[b][:], xq[qt * 128:(qt + 1) * 128, :], writes=[t_xt[b]])
                for half in range(2):
                    mm_group(S, py[pb][:, half * 512:(half + 1) * 512],
                             [(mTt[b][:, k, :], wo[:, k, half * 512:(half + 1) * 512]) for k in range(8)],
                             reads=[t_mTt[b], t_wo], writes=[t_py[pb]])
                S.op("dve", lambda e: e.tensor_tensor(out=z[b][:], in0=py[pb][:], in1=g1bc[:], op=ALU.mult),
                     reads=[t_py[pb], t_const], writes=[t_z[b]])
                S.op("pool", lambda e: e.tensor_tensor(out=z[b][:], in0=z[b][:], in1=xt[b][:], op=ALU.add),
                     reads=[t_z[b], t_xt[b]], writes=[t_z[b]])
                for half in range(2):
                    S.op("dve", lambda e, half=half: e.bn_stats(out=st12[b][:, half, :], in_=z[b][:, half * 512:(half + 1) * 512]),
                         reads=[t_z[b]], writes=[t_st[b]])
                S.op("dve", lambda e: e.bn_aggr(out=mv[b][:], in_=st12[b][:]), reads=[t_st[b]], writes=[t_st[b]])
                S.op("act", lambda e: e.activation(out=rstd[b][:], in_=mv[b][:, 1:2], func=AF.Sqrt, bias=epsA[:], scale=1.0),
                     reads=[t_st[b], t_const], writes=[t_st[b]])
                S.op("dve", lambda e: e.reciprocal(out=rstd[b][:], in_=rstd[b][:]), reads=[t_st[b]], writes=[t_st[b]])
                S.op("dve", lambda e: e.tensor_scalar(out=zn[b][:], in0=z[b][:], scalar1=mv[b][:, 0:1], scalar2=rstd[b][:],
                                                      op0=ALU.subtract, op1=ALU.mult),
                     reads=[t_z[b], t_st[b]], writes=[t_zn[b]])
                if qt >= 1:
                    S.op("pool", lambda e: e.tensor_tensor(out=x1[b][:], in0=zn[b][:], in1=lg[:], op=ALU.mult),
                         reads=[t_zn[b], t_lc], writes=[t_x1[b]])
                    S.op("dve", lambda e: e.tensor_tensor(out=x1[b][:], in0=x1[b][:], in1=lb_[:], op=ALU.add),
                         reads=[t_x1[b], t_lc], writes=[t_x1[b]])
                    dma(S, out[(qt - 1) * 128:qt * 128, :], x1[b][:], reads=[t_x1[b]], q="pool")

            def l1_B(qt):
                b = qt % NBF
                pb = qt % 2

                def ftz(e):
                    ins = None
                    for k in range(8):
                        ins = e.transpose(pz[pb][:, k * 128:(k + 1) * 128], zn[b][:, k * 128:(k + 1) * 128], identF[:])
                    return ins
                S.op("pe", ftz, reads=[t_zn[b], t_const], writes=[t_pz[pb]])

                def fh2(e):
                    ins = None
                    for k in range(8):
                        ins = e.activation(out=h2t[b][:, k, :], in_=pz[pb][:, k * 128:(k + 1) * 128], func=AF.Identity,
                                           scale=A2[:, k:k + 1], bias=B2[:, k:k + 1])
                    return ins
                S.op("act", fh2, reads=[t_pz[pb], t_const], writes=[t_h2t[b]])
                dma(S, scrH[:, :, qt * 128:(qt + 1) * 128], h2t[b][:], reads=[t_h2t[b]], q="pool")

            for st_ in range(17 + 1):
                if st_ < 17:
                    l1_A(st_)
                if st_ - 1 >= 0:
                    l1_B(st_ - 1)
            S.end_phase()

        with ExitStack() as ph:
            wd = sbuf(ph, "wd", [128, NFC, D], BF16)
            l2g = sbuf(ph, "l2g", [128, D])
            l2b = sbuf(ph, "l2b", [128, D])
            cw = sbuf(ph, "cw", [128, NFC * 3])
            cb_ = sbuf(ph, "cb", [128, NFC])
            h2h = sbuf(ph, "h2h", [128, 8, 128], BF16)
            h2s2 = [sbuf(ph, "h2s%d" % i, [128, 8, 1024], BF16) for i in range(2)]
            t_h2s2 = [T(), T()]
            wgu = [sbuf(ph, "wgu%d" % i, [128, 8, 256], BF16) for i in range(3)]
            gs = [sbuf(ph, "gs%d" % i, [128, 1026]) for i in range(2)]
            gc = [sbuf(ph, "gc%d" % i, [128, 512]) for i in range(2)]
            ge = [sbuf(ph, "ge%d" % i, [128, 512]) for i in range(2)]
            gsave = sbuf(ph, "gsave", [128, NFC, 2])
            aT = sbuf(ph, "aT", [128, NFC, 1024], BF16)
            x1t = [sbuf(ph, "x1t%d" % i, [128, D]) for i in range(2)]
            z = [sbuf(ph, "fz%d" % i, [128, D]) for i in range(2)]
            ot = [sbuf(ph, "ot%d" % i, [128, D]) for i in range(2)]
            st12 = sbuf(ph, "fst12", [128, 2, 6])
            mv = sbuf(ph, "fmv", [128, 2])
            rstd = sbuf(ph, "frstd", [128, 1])
            pgt = [psum(ph, "fpg%d" % i, [128, 512]) for i in range(2)]
            put = [psum(ph, "fpu%d" % i, [128, 512]) for i in range(2)]
            phh = psum(ph, "fph", [128, 512])
            py2 = psum(ph, "fpy", [128, 1024])
            t_wd, t_lc, t_h2h, t_h2s, t_gsave, t_aT, t_st, t_ph, t_py2 = (T() for _ in range(9))
            t_wgu, t_gs, t_gc, t_ge, t_x1t, t_z, t_ot, t_pg, t_pu = ([T(), T(), T()] for _ in range(9))
            dma(S, l2g[:], lnbc[2], writes=[t_lc])
            dma(S, l2b[:], lnbc[3], writes=[t_lc])
            dma(S, cw[:], convw[:, :], writes=[t_lc])
            dma(S, cb_[:], convb[:, :], writes=[t_lc])
            dma(S, h2h[:], scrH[:, :, 0:128], writes=[t_h2h])
            w_gu_v = w_ffgu.rearrange("i (k p) n -> i p k n", p=128)
            it = 0

            def load_gu(u):
                dma(S, wgu[u % 3][:], w_gu_v[u % NFC], writes=[t_wgu[u % 3]], q="pool")
            load_gu(0)
            load_gu(1)
            dma(S, wd[:], w_ffd.rearrange("(i p) n -> p i n", p=128), writes=[t_wd], q="pool")
            for sg in range(2):
                dma(S, h2s2[sg][:], scrH[:, :, 128 + 1024 * sg:128 + 1024 * (sg + 1)], writes=[t_h2s2[sg]])
            for sg in range(2):
                h2s = h2s2[sg]
                t_h2s = t_h2s2[sg]
                for i in range(NFC):
                    u = sg * NFC + i
                    wb = u % 3
                    gb = i % 2
                    if u + 2 < 2 * NFC:
                        load_gu(u + 2)
                    if sg == 0:
                        mm_group(S, phh[:, 0:2], [(wgu[wb][:, k, 0:128], h2h[:, k, 126:128]) for k in range(8)],
                                 reads=[t_wgu[wb], t_h2h], writes=[t_ph])
                        S.op("act", lambda e, gb=gb: e.mul(out=gs[gb][:, 0:2], in_=phh[:, 0:2], mul=flag[:, 0:1]),
                             reads=[t_ph, t_const], writes=[t_gs[gb]])
                    else:
                        S.op("act", lambda e, gb=gb, i=i: e.copy(out=gs[gb][:, 0:2], in_=gsave[:, i, :]),
                             reads=[t_gsave], writes=[t_gs[gb]])
                    for half in range(2):
                        pb = it % 2
                        it += 1
                        tsl = slice(half * 512, (half + 1) * 512)
                        mm_group(S, pgt[pb][:], [(wgu[wb][:, k, 0:128], h2s[:, k, tsl]) for k in range(8)],
                                 reads=[t_wgu[wb], t_h2s], writes=[t_pg[pb]])
                        mm_group(S, put[pb][:], [(wgu[wb][:, k, 128:256], h2s[:, k, tsl]) for k in range(8)],
                                 reads=[t_wgu[wb], t_h2s], writes=[t_pu[pb]])
                        o0 = 2 + half * 512
                        S.op("act", lambda e, gb=gb, pb=pb, o0=o0: e.copy(out=gs[gb][:, o0:o0 + 512], in_=pgt[pb][:]),
                             reads=[t_pg[pb]], writes=[t_gs[gb]])
                        S.op("act", lambda e, pb=pb, i=i: e.activation(out=gc[pb][:], in_=pgt[pb][:], func=AF.Identity,
                                                                       scale=cw[:, 3 * i + 2:3 * i + 3], bias=cb_[:, i:i + 1]),
                             reads=[t_pg[pb], t_lc], writes=[t_gc[pb]])
                        S.op("dve", lambda e, gb=gb, pb=pb, i=i, o0=o0: e.scalar_tensor_tensor(
                            out=gc[pb][:], in0=gs[gb][:, o0 - 1:o0 + 511], scalar=cw[:, 3 * i + 1:3 * i + 2], in1=gc[pb][:],
                            op0=ALU.mult, op1=ALU.add), reads=[t_gs[gb], t_gc[pb], t_lc], writes=[t_gc[pb]])
                        S.op("dve", lambda e, gb=gb, pb=pb, i=i, o0=o0: e.scalar_tensor_tensor(
                            out=gc[pb][:], in0=gs[gb][:, o0 - 2:o0 + 510], scalar=cw[:, 3 * i:3 * i + 1], in1=gc[pb][:],
                            op0=ALU.mult, op1=ALU.add), reads=[t_gs[gb], t_gc[pb], t_lc], writes=[t_gc[pb]])
                        S.op("act", lambda e, pb=pb: e.activation(out=ge[pb][:], in_=gc[pb][:], func=AF.Gelu),
                             reads=[t_gc[pb]], writes=[t_ge[pb]])
                        S.op("dve", lambda e, pb=pb, i=i, tsl=tsl: e.tensor_tensor(out=aT[:, i, tsl], in0=put[pb][:], in1=ge[pb][:], op=ALU.mult),
                             reads=[t_pu[pb], t_ge[pb]], writes=[t_aT])
                    if sg == 0:
                        S.op("pool", lambda e, gb=gb, i=i: e.tensor_copy(out=gsave[:, i, :], in_=gs[gb][:, 1024:1026]),
                             reads=[t_gs[gb]], writes=[t_gsave])
                for t in range(8):
                    b = t % 2
                    row0 = sg * 1024 + t * 128
                    dma(S, x1t[b][:], out[row0:row0 + 128, :], writes=[t_x1t[b]])
                    for half in range(2):
                        mm_group(S, py2[:, half * 512:(half + 1) * 512],
                                 [(aT[:, i, t * 128:(t + 1) * 128], wd[:, i, half * 512:(half + 1) * 512]) for i in range(NFC)],
                                 reads=[t_aT, t_wd], writes=[t_py2])
                    S.op("dve", lambda e, b=b: e.tensor_tensor(out=z[b][:], in0=py2[:], in1=g2bc[:], op=ALU.mult),
                         reads=[t_py2, t_const], writes=[t_z[b]])
                    S.op("pool", lambda e, b=b: e.tensor_tensor(out=z[b][:], in0=z[b][:], in1=x1t[b][:], op=ALU.add),
                         reads=[t_z[b], t_x1t[b]], writes=[t_z[b]])
                    for half in range(2):
                        S.op("dve", lambda e, b=b, half=half: e.bn_stats(out=st12[:, half, :], in_=z[b][:, half * 512:(half + 1) * 512]),
                             reads=[t_z[b]], writes=[t_st])
                    S.op("dve", lambda e: e.bn_aggr(out=mv[:], in_=st12[:]), reads=[t_st], writes=[t_st])
                    S.op("act", lambda e: e.activation(out=rstd[:], in_=mv[:, 1:2], func=AF.Sqrt, bias=epsA[:], scale=1.0),
                         reads=[t_st, t_const], writes=[t_st])
                    S.op("dve", lambda e: e.reciprocal(out=rstd[:], in_=rstd[:]), reads=[t_st], writes=[t_st])
                    S.op("dve", lambda e, b=b: e.tensor_scalar(out=ot[b][:], in0=z[b][:], scalar1=mv[:, 0:1], scalar2=rstd[:],
                                                               op0=ALU.subtract, op1=ALU.mult),
                         reads=[t_z[b], t_st], writes=[t_ot[b]])
                    S.op("pool", lambda e, b=b: e.tensor_tensor(out=ot[b][:], in0=ot[b][:], in1=l2g[:], op=ALU.mult),
                         reads=[t_ot[b], t_lc], writes=[t_ot[b]])
                    S.op("pool", lambda e, b=b: e.tensor_tensor(out=ot[b][:], in0=ot[b][:], in1=l2b[:], op=ALU.add),
                         reads=[t_ot[b], t_lc], writes=[t_ot[b]])
                    dma(S, out[row0:row0 + 128, :], ot[b][:], reads=[t_ot[b], t_x1t[b]], q="pool")
            S.end_phase()
    return nc


def _const_tables(c):
    f32 = np.float32
    lt = np.arange(SEQ)
    pos = (lt if c == 1 else np.maximum(lt - HALF, 0)).astype(f32)
    inv_freq = (10000.0 ** (-np.arange(0, 64, 2, dtype=f32) / 64)).astype(f32)
    ang = pos[None, :] * inv_freq[:, None]
    cos32, sin32 = np.cos(ang).astype(f32), np.sin(ang).astype(f32)
    cos64 = np.concatenate([cos32, cos32], 0)
    sin64 = np.concatenate([-sin32, sin32], 0)
    cosM = np.concatenate([cos64, cos64], 0)
    sinM = np.concatenate([sin64, sin64], 0)
    freq = (1.0 / (10000.0 ** np.linspace(0.0, 1.0, 64, dtype=f32))).astype(f32)
    angr = pos[None, :] * freq[:, None]
    cr, sr = np.cos(angr).astype(f32), np.sin(angr).astype(f32)
    cosR = np.concatenate([cr, cr], 0)
    sinR = np.concatenate([-sr, sr], 0)
    vb = np.full((17, 16), NEGV, f32)
    own = np.zeros((17, 16), f32)
    for qt in range(17):
        lb = 7 if qt == 0 else 8 + (qt - 1) // 2
        for j in range(16):
            if j < lb and (j >= 8 or c == 1):
                vb[qt, j] = 0.0
        own[qt, lb] = 1.0
    vb = np.broadcast_to(vb.reshape(1, 272), (128, 272)).copy()
    own = np.broadcast_to(own.reshape(1, 272), (128, 272)).copy()
    p = np.arange(128, dtype=np.float64)
    ksp = np.zeros((128, 128), np.float64)
    epsq = np.zeros((128, 4), np.float64)
    for r in range(4):
        g = 1.0 - 2.0 ** (-5.0 - r)
        gc = g ** 128
        ks = (128.0 ** -0.5) * g ** (-(p + 1.0))
        for n in range(32):
            if n <= 14:
                wgt = gc ** (14 - n) * float(c)
            elif n == 15:
                wgt = float(c)
            else:
                wgt = 1.0
            ksp[:, r * 32 + n] = ks * wgt
        epsq[:, r] = LN_EPS / (g ** (p + 1.0)) ** 2
    k = np.arange(128)[:, None]
    q = np.arange(256)[None, :]
    tri256 = (k <= q).astype(f32)
    tri256b = ((k + 128) <= q).astype(f32)
    dmask = np.concatenate([tri256, tri256b], 1)
    tri = (k <= np.arange(128)[None, :]).astype(f32)
    hmask = np.concatenate([np.ones((128, 128), f32), tri], 1)
    onehot = (np.arange(SEQ)[None, :] // 256 == np.arange(16)[:, None]).astype(f32)
    return dict(cosM=cosM, sinM=sinM, cosR=cosR, sinR=sinR, vb=vb, ownhot=own, ksp=ksp.astype(f32),
                epsq=epsq.astype(f32), dmask=dmask, hmask=hmask, tri=tri, onehot=onehot,
                ident=np.eye(128, dtype=f32), flag=np.full((128, 1), float(c), f32))


_NC_CACHE = {}


def kernel(x, c, w_ada, b_ada, w_in, w_proj_moba, w_proj_ret, w_out, ln1_g, ln1_b,
           w_ff_gate, w_ff_up, ff_conv_w, ff_conv_b, w_ff_down, ln2_g, ln2_b):
    f32 = np.float32
    x = np.asarray(x, f32)
    c = np.asarray(c, f32)
    w_in0 = np.asarray(w_in, f32)[0]
    perm_m = np.concatenate([np.arange(32, 64), np.arange(0, 32)])
    perm_r = np.concatenate([np.arange(0, 128, 2), np.arange(1, 128, 2)])
    w_moba = np.empty((4, D, 384), f32)
    for hp in range(4):
        mq = w_in0[:, hp * 128:(hp + 1) * 128]
        mk = w_in0[:, 512 + hp * 128:512 + (hp + 1) * 128]
        mv = w_in0[:, 1024 + hp * 128:1024 + (hp + 1) * 128]
        w_moba[hp] = np.concatenate([mk, mq, mv], 1)
    w_ret = np.empty((4, D, 768), f32)
    for r in range(4):
        rq = w_in0[:, 1536 + r * 128:1536 + (r + 1) * 128]
        rk = w_in0[:, 2048 + r * 128:2048 + (r + 1) * 128]
        rv = w_in0[:, 2560 + r * 256:2560 + (r + 1) * 256]
        rg = w_in0[:, 3584 + r * 256:3584 + (r + 1) * 256]
        w_ret[r] = np.concatenate([rk[:, perm_r], rq[:, perm_r], rv, rg], 1)
    w_g = np.ascontiguousarray(w_in0[:, 4608:6656])
    wgate = np.asarray(w_ff_gate, f32)[0]
    wup = np.asarray(w_ff_up, f32)[0]
    w_ffgu = np.empty((NFC, D, 256), f32)
    for i in range(NFC):
        w_ffgu[i, :, 0:128] = wgate[:, i * 128:(i + 1) * 128]
        w_ffgu[i, :, 128:256] = wup[:, i * 128:(i + 1) * 128]
    convw = np.asarray(ff_conv_w, f32)[0]
    convwT = np.ascontiguousarray(convw.reshape(3, NFC, 128).transpose(2, 1, 0).reshape(128, NFC * 3))
    convbT = np.ascontiguousarray(np.asarray(ff_conv_b, f32)[0].reshape(NFC, 128).T)
    lnbc = np.stack([np.broadcast_to(np.asarray(v, f32)[0][None, :], (128, D)) for v in (ln1_g, ln1_b, ln2_g, ln2_b)]).copy()
    ln1T = np.ascontiguousarray(np.concatenate([np.asarray(ln1_g, f32)[0].reshape(8, 128).T,
                                                np.asarray(ln1_b, f32)[0].reshape(8, 128).T], 1))
    shared = dict(
        w_ada=np.ascontiguousarray(np.asarray(w_ada, f32)[0]),
        b_adaT=np.ascontiguousarray(np.asarray(b_ada, f32)[0].reshape(48, 128).T),
        w_moba=w_moba, w_ret=w_ret, w_g=w_g,
        w_pa=np.ascontiguousarray(np.asarray(w_proj_moba, f32)[0]),
        w_pr=np.ascontiguousarray(np.asarray(w_proj_ret, f32)[0]),
        w_o=np.ascontiguousarray(np.asarray(w_out, f32)[0]),
        w_ffgu=w_ffgu, w_ffd=np.ascontiguousarray(np.asarray(w_ff_down, f32)[0]),
        convw=convwT, convb=convbT, lnbc=lnbc, ln1T=ln1T,
    )
    consts = [_const_tables(0), _const_tables(1)]
    in_maps = []
    for core in range(8):
        b, h = core // 2, core % 2
        if h == 1:
            xT = np.ascontiguousarray(x[b].T)
            xq_ = np.ascontiguousarray(x[b, 1920:4096])
        else:
            xT = np.concatenate([np.zeros((D, HALF), f32), x[b, :HALF].T], 1)
            xq_ = np.concatenate([np.zeros((128, D), f32), x[b, :HALF]], 0)
        m = dict(shared)
        m.update(consts[h])
        m["xT"] = np.ascontiguousarray(xT)
        m["xq"] = np.ascontiguousarray(xq_)
        m["cT"] = np.ascontiguousarray(c[b].reshape(8, 128).T)
        in_maps.append(m)
    if "nc" not in _NC_CACHE:
        _NC_CACHE["nc"] = build_program()
    nc = _NC_CACHE["nc"]
    res = run_bass_kernel_spmd(nc, in_maps, core_ids=list(range(8)))
    outp = np.empty((NB, SEQ, D), f32)
    for core in range(8):
        b, h = core // 2, core % 2
        outp[b, h * HALF:(h + 1) * HALF] = res.results[core]["out"]
    return outp
```

```python
from contextlib import ExitStack
import math
import numpy as np
import concourse.bass as bass
import concourse.mybir as mybir
from concourse.bass_utils import run_bass_kernel_spmd

F32 = mybir.dt.float32
BF16 = mybir.dt.bfloat16
AF = mybir.ActivationFunctionType
ALU = mybir.AluOpType
AX = mybir.AxisListType

D = 1024
SEQ = 4096
NB = 4
HALF = 2048
NQ = 2176
DFF = 2816
NFC = 22
ALPHA = 2.0 ** 0.25
LN_EPS = 1e-5
EPS_A = LN_EPS / (ALPHA * ALPHA)
BIG = 240000.0
NEGV = -30000.0
QG = [(0, 128, 1920)] + [(128 + 512 * i, 512, 2048 + 512 * i) for i in range(4)]

ENGS = ("pe", "act", "dve", "pool", "dma")
SAME_ENGINE_SYNC = {"pe": False, "act": True, "dve": True, "pool": True, "dma": True}
NDMASEM = {"dma": 32, "pool": 16}


class T:
    __slots__ = ("w", "rd")

    def __init__(self):
        self.w = None
        self.rd = []


class Op:
    __slots__ = ("eng", "fn", "deps", "sig", "sigval", "ndma", "sem", "prev", "phase")


class Sched:
    def __init__(self, nc, es):
        self.nc = nc
        self.sems = {e: es.enter_context(nc.semaphore("s_" + e)) for e in ENGS if e != "dma"}
        self.dsems = {e: [es.enter_context(nc.semaphore("d%s%d" % (e, i))) for i in range(NDMASEM[e])]
                      for e in NDMASEM}
        self.cnt = {e: 0 for e in ENGS}
        self.nd = {e: 0 for e in NDMASEM}
        self.tot = {e: [0] * NDMASEM[e] for e in NDMASEM}
        self.phase = 0
        self.ops = {e: [] for e in ENGS}
        self.nops = 0

    def op(self, eng, fn, reads=(), writes=(), ndma=1, dma=None):
        if dma is None:
            dma = eng == "dma"
        o = Op()
        o.eng = eng
        o.fn = fn
        o.ndma = ndma if dma else 0
        o.sig = False
        o.sigval = None
        o.sem = None
        o.prev = 0
        o.phase = self.phase
        deps = []
        for t in reads:
            if t.w is not None:
                deps.append(t.w)
        for t in writes:
            if t.w is not None:
                deps.append(t.w)
            deps.extend(t.rd)
        seen = set()
        o.deps = []
        for d in deps:
            if d is o or id(d) in seen or d.phase != self.phase:
                continue
            seen.add(id(d))
            if d.eng == eng and not SAME_ENGINE_SYNC[eng] and not d.ndma:
                continue
            o.deps.append(d)
        for t in reads:
            t.rd.append(o)
        for t in writes:
            t.w = o
            t.rd = []
        self.ops[eng].append(o)
        self.nops += 1
        return o

    def end_phase(self):
        nc = self.nc
        ops = self.ops
        for e in ENGS:
            for o in ops[e]:
                for d in o.deps:
                    d.sig = True
        bar = {}
        for e in ENGS:
            lastc = None
            for o in ops[e]:
                if not o.ndma:
                    lastc = o
            if lastc is not None:
                lastc.sig = True
            for o in ops[e]:
                if o.ndma:
                    s = self.nd[e] % NDMASEM[e]
                    self.nd[e] += 1
                    o.sem = self.dsems[e][s]
                    o.prev = self.tot[e][s]
                    self.tot[e][s] += 16 * o.ndma
                    o.sigval = self.tot[e][s]
                    bar[id(o.sem)] = (o.sem, o.sigval)
                else:
                    o.sem = self.sems[e]
                    if o.sig:
                        self.cnt[e] += 1
                        o.sigval = self.cnt[e]
            if lastc is not None:
                bar[id(lastc.sem)] = (lastc.sem, lastc.sigval)

        def run(e, eng):
            known = {}
            for o in ops[e]:
                need = {}
                for d in o.deps:
                    k = id(d.sem)
                    if known.get(k, 0) >= d.sigval:
                        continue
                    if k not in need or need[k][1] < d.sigval:
                        need[k] = (d.sem, d.sigval)
                if o.ndma and o.prev > 0:
                    k = id(o.sem)
                    if known.get(k, 0) < o.prev and (k not in need or need[k][1] < o.prev):
                        need[k] = (o.sem, o.prev)
                for k, (s, v) in need.items():
                    eng.wait_ge(s, v)
                    known[k] = v
                if o.ndma:
                    o.fn(eng, o.sem)
                else:
                    ins = o.fn(eng)
                    if o.sig:
                        ins.then_inc(o.sem, 1)
            for k, (s, v) in bar.items():
                if known.get(k, 0) < v:
                    eng.wait_ge(s, v)

        with nc.Block() as block:
            @block.tensor
            def _(eng):
                run("pe", eng)

            @block.scalar
            def _(eng):
                run("act", eng)

            @block.vector
            def _(eng):
                run("dve", eng)

            @block.gpsimd
            def _(eng):
                run("pool", eng)

            @block.sync
            def _(eng):
                run("dma", eng)

        self.ops = {e: [] for e in ENGS}
        self.phase += 1


def mm_group(S, out, pairs, reads, writes):
    pairs = list(pairs)

    def fn(e):
        n = len(pairs)
        ins = None
        for i, (l, r) in enumerate(pairs):
            ins = e.matmul(out, lhsT=l, rhs=r, start=(i == 0), stop=(i == n - 1))
        return ins
    return S.op("pe", fn, reads, writes)


def dma(S, out, in_, reads=(), writes=(), q="dma"):
    def fn(e, s):
        e.dma_start(out=out, in_=in_).then_inc(s, 16)
    return S.op(q, fn, reads, writes, ndma=1, dma=True)


def build_program():
    nc = bass.Bass("TRN2", target_bir_lowering=False)

    def din(name, shape, dt=F32):
        return nc.dram_tensor(name, list(shape), dt, kind="ExternalInput").ap()

    xT = din("xT", [D, SEQ])
    xq = din("xq", [NQ, D])
    cT = din("cT", [128, 8])
    w_ada = din("w_ada", [D, 6 * D])
    b_adaT = din("b_adaT", [128, 48])
    w_moba = din("w_moba", [4, D, 384])
    w_ret = din("w_ret", [4, D, 768])
    w_g = din("w_g", [D, 2048])
    w_pa = din("w_pa", [512, D])
    w_pr = din("w_pr", [D, D])
    w_o = din("w_o", [D, D])
    w_ffgu = din("w_ffgu", [NFC, D, 256])
    w_ffd = din("w_ffd", [DFF, D])
    convw = din("convw", [128, NFC * 3])
    convb = din("convb", [128, NFC])
    lnbc = din("lnbc", [4, 128, D])
    ln1T = din("ln1T", [128, 16])
    cosM = din("cosM", [128, SEQ])
    sinM = din("sinM", [128, SEQ])
    cosR = din("cosR", [128, SEQ])
    sinR = din("sinR", [128, SEQ])
    vb_d = din("vb", [128, 272])
    own_d = din("ownhot", [128, 272])
    ksp_d = din("ksp", [128, 128])
    epsq_d = din("epsq", [128, 4])
    dmask_d = din("dmask", [128, 512])
    hmask_d = din("hmask", [128, 256])
    tri_d = din("tri", [128, 128])
    onehot_d = din("onehot", [16, SEQ])
    ident_d = din("ident", [128, 128])
    flag_d = din("flag", [128, 1])
    out = nc.dram_tensor("out", [HALF, D], F32, kind="ExternalOutput").ap()
    scrM = nc.dram_tensor("scrM", [17, 128, 8, 128], BF16, kind="Internal").ap()
    scrH = nc.dram_tensor("scrH", [128, 8, NQ], BF16, kind="Internal").ap()

    gam = [1.0 - 2.0 ** (-5.0 - r) for r in range(4)]
    gC = [g ** 128 for g in gam]

    top = ExitStack()
    with top:
        S = Sched(nc, top)

        def sbuf(es, name, shape, dt=F32):
            return es.enter_context(nc.sbuf_tensor("sb_" + name, list(shape), dt))

        def psum(es, name, shape, dt=F32):
            return es.enter_context(nc.psum_tensor("ps_" + name, list(shape), dt))

        identF = sbuf(top, "identF", [128, 128])
        identB = sbuf(top, "identB", [128, 128], BF16)
        modT = sbuf(top, "modT", [128, 48])
        s1T = sbuf(top, "s1T", [128, 8])
        A2 = sbuf(top, "A2", [128, 8])
        B2 = sbuf(top, "B2", [128, 8])
        g1bc = sbuf(top, "g1bc", [128, D])
        g2bc = sbuf(top, "g2bc", [128, D])
        flag = sbuf(top, "flag", [128, 1])
        epsA = sbuf(top, "epsA", [128, 1])
        oneA = sbuf(top, "oneA", [128, 1])
        t_const = T()

        mix = ExitStack()
        mix.__enter__()
        h1T = sbuf(mix, "h1T", [128, 8, SEQ], BF16)
        AT = sbuf(mix, "AT", [128, 4, NQ], BF16)
        t_h1 = T()

        with ExitStack() as ph:
            cs = sbuf(ph, "cs", [128, 8])
            sc = sbuf(ph, "sc", [128, 8])
            bad = sbuf(ph, "bad", [128, 48])
            l1T = sbuf(ph, "l1T", [128, 16])
            onesF = sbuf(ph, "onesF", [128, 128])
            wstb = [sbuf(ph, "wstb%d" % i, [128, 8, 512]) for i in range(2)]
            modrow = sbuf(ph, "modrow", [1, 6 * D])
            one1 = sbuf(ph, "one1", [1, 1])
            prow = [psum(ph, "prow%d" % i, [128, 512]) for i in range(2)]
            t_prow = [T(), T()]
            t_modrow = T()
            xst = [sbuf(ph, "xst%d" % i, [128, 8, 512]) for i in range(2)]
            dg = [sbuf(ph, "dg%d" % i, [128, 128]) for i in range(2)]
            g12 = sbuf(ph, "g12", [128, 16])
            pmod = psum(ph, "pmod", [128, 512])
            pmodA = psum(ph, "pmodA", [128, 512])
            pbc = [psum(ph, "pbc%d" % i, [128, 1024]) for i in range(2)]
            t_cs, t_sc, t_bad, t_l1, t_ones, t_pmod, t_mod = T(), T(), T(), T(), T(), T(), T()
            t_wst = [T(), T()]
            t_xst = [T(), T()]
            t_dg = [T(), T()]
            t_pbc = [T(), T()]
            t_g12 = T()

            dma(S, identF[:], ident_d[:, :], writes=[t_const])
            dma(S, flag[:], flag_d[:, :], writes=[t_const])
            dma(S, cs[:], cT[:, :], writes=[t_cs])
            dma(S, bad[:], b_adaT[:, :], writes=[t_bad])
            dma(S, l1T[:], ln1T[:, :], writes=[t_l1])
            S.op("dve", lambda e: e.tensor_copy(out=identB[:], in_=identF[:]), reads=[t_const], writes=[t_const])
            S.op("dve", lambda e: e.memset(onesF[:], 1.0), writes=[t_ones])
            S.op("dve", lambda e: e.memset(one1[:], 1.0), writes=[t_ones])
            S.op("dve", lambda e: e.memset(epsA[:], EPS_A), writes=[t_const])
            S.op("dve", lambda e: e.memset(oneA[:], 1.0), writes=[t_const])
            S.op("act", lambda e: e.activation(out=sc[:], in_=cs[:], func=AF.Silu), reads=[t_cs], writes=[t_sc])
            w_ada_v = w_ada.rearrange("(k p) n -> p k n", p=128)
            xT_v = xT.rearrange("(k p) t -> p k t", p=128)

            def mod_block(blk):
                b = blk % 2
                dma(S, wstb[b][:], w_ada_v[:, :, blk * 512:(blk + 1) * 512], writes=[t_wst[b]])
                mm_group(S, prow[b][0:1, :], [(sc[:, k:k + 1], wstb[b][:, k, :]) for k in range(8)],
                         reads=[t_wst[b], t_sc], writes=[t_prow[b]])
                S.op("act", lambda e, b=b, blk=blk: e.copy(out=modrow[0:1, blk * 512:(blk + 1) * 512], in_=prow[b][0:1, :]),
                     reads=[t_prow[b]], writes=[t_modrow])

                def fn(e, blk=blk):
                    ins = None
                    pm = pmodA if blk < 4 else pmod
                    for j in range(4):
                        col = blk * 4 + j
                        ins = e.matmul(pm[:, col:col + 1], lhsT=modrow[0:1, col * 128:(col + 1) * 128],
                                       rhs=one1[0:1, 0:1], start=True, stop=True)
                    return ins
                S.op("pe", fn, reads=[t_modrow, t_ones], writes=[t_pmodp[0 if blk < 4 else 1]])
            t_pmodp = [T(), T(), T()]
            t_modA = T()
            for blk in range(4):
                mod_block(blk)
            S.op("dve", lambda e: e.tensor_tensor(out=modT[:, 0:16], in0=pmodA[:, 0:16], in1=bad[:, 0:16], op=ALU.add),
                 reads=[t_pmodp[0], t_bad], writes=[t_modA])
            S.op("dve", lambda e: e.tensor_scalar_add(out=s1T[:], in0=modT[:, 8:16], scalar1=1.0),
                 reads=[t_modA], writes=[t_modA])
            for g in range(8):
                b = g % 2
                dma(S, xst[b][:], xT_v[:, :, g * 512:(g + 1) * 512], writes=[t_xst[b]], q="pool")
                for k in range(8):
                    eng = "dve" if k % 2 == 0 else "pool"
                    S.op(eng, lambda e, b=b, k=k, g=g: e.tensor_scalar(
                        out=h1T[:, k, g * 512:(g + 1) * 512], in0=xst[b][:, k, :],
                        scalar1=s1T[:, k:k + 1], scalar2=modT[:, k:k + 1], op0=ALU.mult, op1=ALU.add),
                        reads=[t_xst[b], t_modA], writes=[t_h1])
            for blk in range(4, 12):
                mod_block(blk)
            S.op("dve", lambda e: e.tensor_tensor(out=modT[:, 16:48], in0=pmod[:, 16:48], in1=bad[:, 16:48], op=ALU.add),
                 reads=[t_pmodp[1], t_bad], writes=[t_mod])
            S.op("dve", lambda e: e.tensor_scalar_add(out=A2[:], in0=modT[:, 32:40], scalar1=1.0),
                 reads=[t_mod], writes=[t_mod])
            S.op("dve", lambda e: e.tensor_tensor(out=B2[:], in0=l1T[:, 8:16], in1=A2[:], op=ALU.mult),
                 reads=[t_mod, t_l1], writes=[t_mod])
            S.op("dve", lambda e: e.tensor_tensor(out=B2[:], in0=B2[:], in1=modT[:, 24:32], op=ALU.add),
                 reads=[t_mod], writes=[t_mod])
            S.op("dve", lambda e: e.tensor_tensor(out=A2[:], in0=A2[:], in1=l1T[:, 0:8], op=ALU.mult),
                 reads=[t_mod, t_l1], writes=[t_mod])
            S.op("dve", lambda e: e.tensor_scalar_mul(out=g12[:, 0:8], in0=modT[:, 16:24], scalar1=1.0 / ALPHA),
                 reads=[t_mod], writes=[t_g12])
            S.op("dve", lambda e: e.tensor_scalar_mul(out=g12[:, 8:16], in0=modT[:, 40:48], scalar1=1.0 / ALPHA),
                 reads=[t_g12], writes=[t_g12])
            for which, dst in ((0, g1bc), (1, g2bc)):
                for k in range(8):
                    b = k % 2
                    S.op("dve", lambda e, b=b, k=k, which=which: e.tensor_scalar_mul(
                        out=dg[b][:], in0=identF[:], scalar1=g12[:, which * 8 + k:which * 8 + k + 1]),
                        reads=[t_g12, t_const], writes=[t_dg[b]])
                    S.op("pe", lambda e, b=b, k=k, which=which: e.matmul(
                        pbc[which][:, k * 128:(k + 1) * 128], lhsT=onesF[:], rhs=dg[b][:], start=True, stop=True),
                        reads=[t_dg[b], t_ones], writes=[t_pbc[which]])
                S.op("act", lambda e, which=which, dst=dst: e.copy(out=dst[:], in_=pbc[which][:]),
                     reads=[t_pbc[which]], writes=[t_const])
            S.end_phase()

        with ExitStack() as ph:
            wm = [sbuf(ph, "wm%d" % i, [128, 8, 384], BF16) for i in range(2)]
            tb = [sbuf(ph, "tb%d" % i, [128, 2, 512]) for i in range(2)]
            KT = [sbuf(ph, "KT%d" % i, [80, SEQ], BF16) for i in range(2)]
            QT = [sbuf(ph, "QT%d" % i, [80, NQ], BF16) for i in range(2)]
            VA = [sbuf(ph, "VA%d" % i, [128, 32, 128], BF16) for i in range(2)]
            PT = [sbuf(ph, "PT%d" % i, [128, 512], BF16) for i in range(6)]
            tmp1 = [sbuf(ph, "tmpa%d" % i, [128, 512]) for i in range(2)]
            tmp2 = [sbuf(ph, "tmpb%d" % i, [128, 512]) for i in range(2)]
            vb = sbuf(ph, "vb", [128, 272])
            ownhot = sbuf(ph, "ownhot", [128, 272])
            gm = sbuf(ph, "gm", [128, 272])
            sel = sbuf(ph, "sel", [128, 272])
            top8 = sbuf(ph, "top8", [128, 17, 8])
            thr = sbuf(ph, "thr", [128, 17])
            mbt = sbuf(ph, "mbt", [128, 272], BF16)
            kmf = sbuf(ph, "kmf", [64, 16])
            kmb = sbuf(ph, "kmb", [64, 16], BF16)
            dmask = sbuf(ph, "dmask", [128, 512], BF16)
            hmask = sbuf(ph, "hmask", [128, 256], BF16)
            rden = sbuf(ph, "rden", [128, 256])
            pk = [psum(ph, "pk%d" % i, [128, 512]) for i in range(2)]
            pv = psum(ph, "pv", [128, 512])
            pss = [psum(ph, "pss%d" % i, [128, 512]) for i in range(3)]
            po = [psum(ph, "po%d" % i, [128, 512]) for i in range(2)]
            t_wm = [T(), T()]
            t_tb = [T(), T()]
            t_tbs = [T(), T()]
            t_KT = [[T() for _ in range(8)] for _ in range(2)]
            t_QT = [[T() for _ in range(5)] for _ in range(2)]
            t_QA = [[T() for _ in range(5)] for _ in range(2)]
            t_VA = [[T() for _ in range(8)] for _ in range(2)]
            t_PT = [T() for _ in range(6)]
            t_t1 = [T(), T()]
            t_t2 = [T(), T()]
            t_pk = [T(), T()]
            t_pv = T()
            t_pss = [T(), T(), T()]
            t_po = [T(), T()]
            t_msk, t_gm, t_sel, t_top8, t_thr, t_mbt, t_kmf, t_kmb, t_rden, t_AT = (T() for _ in range(10))

            dma(S, vb[:], vb_d[:, :], writes=[t_msk])
            dma(S, ownhot[:], own_d[:, :], writes=[t_msk])
            dma(S, dmask[:], dmask_d[:, :], writes=[t_msk], q="pool")
            dma(S, hmask[:], hmask_d[:, :], writes=[t_msk], q="pool")
            for e2 in range(2):
                dma(S, KT[e2][64:80, :], onehot_d[:, :], writes=[t_KT[e2][0]], q="pool")
            S.op("dve", lambda e: e.memset(VA[0][:, :, 64:128], 1.0), writes=[t_VA[0][0]])
            S.op("dve", lambda e: e.memset(VA[1][:, :, 0:64], 1.0), writes=[t_VA[1][0]])
            w_moba_v = w_moba.rearrange("h (k p) n -> h p k n", p=128)
            cnt = {"tb": 0, "pss": 0, "pt": 0, "po": 0}

            def rope_side(hp, wcol, tsl, n, dst, dsl, t_dst):
                w = wm[hp % 2]
                b = cnt["tb"] % 2
                cnt["tb"] += 1
                dma(S, tb[b][:, 0, 0:n], cosM[:, tsl], writes=[t_tb[b]])
                dma(S, tb[b][:, 1, 0:n], sinM[:, tsl], writes=[t_tbs[b]])
                mm_group(S, pk[b][:, 0:n],
                         [(w[:, k, wcol:wcol + 128], h1T[:, k, tsl]) for k in range(8)],
                         reads=[t_wm[hp % 2], t_h1], writes=[t_pk[b]])
                S.op("dve", lambda e, b=b, n=n: e.tensor_tensor(out=tmp1[b][:, 0:n], in0=pk[b][:, 0:n],
                                                                in1=tb[b][:, 0, 0:n], op=ALU.mult),
                     reads=[t_pk[b], t_tb[b]], writes=[t_t1[b]])
                for (o_lo, i_lo) in ((0, 32), (32, 0), (64, 96), (96, 64)):
                    S.op("dve", lambda e, b=b, n=n, o_lo=o_lo, i_lo=i_lo: e.tensor_tensor(
                        out=tmp2[b][o_lo:o_lo + 32, 0:n], in0=pk[b][i_lo:i_lo + 32, 0:n],
                        in1=tb[b][o_lo:o_lo + 32, 1, 0:n], op=ALU.mult),
                        reads=[t_pk[b], t_tbs[b]], writes=[t_t2[b]])
                for e2 in range(2):
                    S.op("pool", lambda e, b=b, n=n, e2=e2: e.tensor_tensor(
                        out=dst[e2][0:64, dsl], in0=tmp1[b][64 * e2:64 * e2 + 64, 0:n],
                        in1=tmp2[b][64 * e2:64 * e2 + 64, 0:n], op=ALU.add),
                        reads=[t_t1[b], t_t2[b]], writes=[t_dst[e2]])

            dma(S, wm[0][:], w_moba_v[0], writes=[t_wm[0]], q="pool")
            for hp in range(4):
                w = wm[hp % 2]
                if hp < 3:
                    dma(S, wm[(hp + 1) % 2][:], w_moba_v[hp + 1], writes=[t_wm[(hp + 1) % 2]], q="pool")
                for g in range(8):
                    rope_side(hp, 0, slice(g * 512, (g + 1) * 512), 512, KT, slice(g * 512, (g + 1) * 512),
                              [t_KT[0][g], t_KT[1][g]])

                    def fnv(e, g=g, w=w):
                        ins = None
                        for j in range(4):
                            t = g * 4 + j
                            for k in range(8):
                                ins = e.matmul(pv[:, j * 128:(j + 1) * 128], lhsT=h1T[:, k, t * 128:(t + 1) * 128],
                                               rhs=w[:, k, 256:384], start=(k == 0), stop=(k == 7))
                        return ins
                    S.op("pe", fnv, reads=[t_wm[hp % 2], t_h1], writes=[t_pv])
                    pv3 = pv[:].rearrange("p (j n) -> p j n", n=128)
                    S.op("act", lambda e, g=g, pv3=pv3: e.copy(out=VA[0][:, g * 4:(g + 1) * 4, 0:64], in_=pv3[:, :, 0:64]),
                         reads=[t_pv], writes=[t_VA[0][g]])
                    S.op("act", lambda e, g=g, pv3=pv3: e.copy(out=VA[1][:, g * 4:(g + 1) * 4, 64:128], in_=pv3[:, :, 64:128]),
                         reads=[t_pv], writes=[t_VA[1][g]])
                for gi, (q0, n, l0) in enumerate(QG):
                    rope_side(hp, 128, slice(l0, l0 + n), n, QT, slice(q0, q0 + n), [t_QT[0][gi], t_QT[1][gi]])
                for e2 in range(2):
                    S.op("dve", lambda e, e2=e2: e.tensor_reduce(
                        out=kmf[:], in_=KT[e2][0:64, :].rearrange("p (j n) -> p j n", n=256), axis=AX.X, op=ALU.add),
                        reads=t_KT[e2], writes=[t_kmf])
                    S.op("act", lambda e: e.mul(out=kmb[:], in_=kmf[:], mul=1.0 / 256.0), reads=[t_kmf], writes=[t_kmb])

                    def fng(e, e2=e2):
                        ins = None
                        for qt in range(17):
                            ins = e.matmul(pk[0][:, qt * 16:(qt + 1) * 16], lhsT=QT[e2][0:64, qt * 128:(qt + 1) * 128],
                                           rhs=kmb[:], start=True, stop=True)
                        return ins
                    S.op("pe", fng, reads=t_QT[e2] + [t_kmb], writes=[t_pk[0]])
                    S.op("dve", lambda e: e.tensor_tensor(out=gm[:], in0=pk[0][:, 0:272], in1=vb[:], op=ALU.add),
                         reads=[t_pk[0], t_msk], writes=[t_gm])

                    def fnmax(e):
                        ins = None
                        for qt in range(17):
                            ins = e.max(out=top8[:, qt, :], in_=gm[:, qt * 16:(qt + 1) * 16])
                        return ins
                    S.op("dve", fnmax, reads=[t_gm], writes=[t_top8])
                    S.op("dve", lambda e: e.tensor_scalar_max(out=thr[:], in0=top8[:, :, 2], scalar1=-10000.0),
                         reads=[t_top8], writes=[t_thr])
                    S.op("dve", lambda e: e.tensor_tensor(
                        out=sel[:].rearrange("p (t j) -> p t j", j=16), in0=gm[:].rearrange("p (t j) -> p t j", j=16),
                        in1=thr[:].unsqueeze(2).to_broadcast([128, 17, 16]), op=ALU.is_ge),
                        reads=[t_gm, t_thr], writes=[t_sel])
                    S.op("dve", lambda e: e.tensor_tensor(out=sel[:], in0=sel[:], in1=ownhot[:], op=ALU.max),
                         reads=[t_sel, t_msk], writes=[t_sel])
                    S.op("dve", lambda e: e.tensor_scalar(out=mbt[:], in0=sel[:], scalar1=-1.0, scalar2=BIG,
                                                          op0=ALU.add, op1=ALU.mult),
                         reads=[t_sel], writes=[t_mbt])
                    for gi, (q0, n, l0) in enumerate(QG):
                        def fnt(e, q0=q0, n=n):
                            ins = None
                            for j in range(n // 128):
                                qt = q0 // 128 + j
                                ins = e.matmul(pk[1][0:16, j * 128:(j + 1) * 128], lhsT=mbt[:, qt * 16:(qt + 1) * 16],
                                               rhs=identB[:], start=True, stop=True)
                            return ins
                        S.op("pe", fnt, reads=[t_mbt, t_const], writes=[t_pk[1]])
                        S.op("act", lambda e, e2=e2, q0=q0, n=n: e.copy(out=QT[e2][64:80, q0:q0 + n], in_=pk[1][0:16, 0:n]),
                             reads=[t_pk[1]], writes=[t_QA[e2][gi]])
                units = []
                for e2 in range(2):
                    for qb in range(9):
                        if qb == 0:
                            q0, nq, lb, gi = 0, 128, 7, 0
                        else:
                            q0, nq, lb, gi = 128 + 256 * (qb - 1), 256, 7 + qb, 1 + (qb - 1) // 2
                        npair = lb + 1
                        pb = cnt["po"] % 2
                        cnt["po"] += 1
                        for pi in range(npair):
                            units.append((e2, qb, q0, nq, gi, npair, pb, pi))
                SKEW = 3
                bufs = {}
                for idx in range(len(units) + SKEW):
                    if idx < len(units):
                        e2, qb, q0, nq, gi, npair, pb, pi = units[idx]
                        sb_ = cnt["pss"] % 3
                        cnt["pss"] += 1
                        ptb = cnt["pt"] % 6
                        cnt["pt"] += 1
                        bufs[idx] = ptb

                        def fqk(e, e2=e2, pi=pi, q0=q0, nq=nq, sb_=sb_):
                            ins = None
                            for j in range(2):
                                kt = 2 * pi + j
                                ins = e.matmul(pss[sb_][:, j * nq:(j + 1) * nq], lhsT=KT[e2][0:80, kt * 128:(kt + 1) * 128],
                                               rhs=QT[e2][0:80, q0:q0 + nq], start=True, stop=True)
                            return ins
                        S.op("pe", fqk, reads=[t_KT[e2][pi // 2], t_KT[e2][0], t_QT[e2][gi], t_QA[e2][gi]],
                             writes=[t_pss[sb_]])
                        S.op("act", lambda e, sb_=sb_, ptb=ptb, nq=nq: e.activation(
                            out=PT[ptb][:, 0:2 * nq], in_=pss[sb_][:, 0:2 * nq], func=AF.Exp, scale=0.125),
                            reads=[t_pss[sb_]], writes=[t_PT[ptb]])
                        if pi == npair - 1:
                            msk = hmask if qb == 0 else dmask
                            S.op("pool", lambda e, ptb=ptb, nq=nq, msk=msk: e.tensor_tensor(
                                out=PT[ptb][:, 0:2 * nq], in0=PT[ptb][:, 0:2 * nq], in1=msk[:, 0:2 * nq], op=ALU.mult),
                                reads=[t_PT[ptb], t_msk], writes=[t_PT[ptb]])
                    if idx - SKEW >= 0:
                        e2, qb, q0, nq, gi, npair, pb, pi = units[idx - SKEW]
                        ptb = bufs.pop(idx - SKEW)

                        def fpv(e, e2=e2, pi=pi, nq=nq, ptb=ptb, pb=pb, npair=npair):
                            ins = None
                            for j in range(2):
                                kt = 2 * pi + j
                                ins = e.matmul(po[pb][:, 0:nq], lhsT=VA[e2][:, kt, :], rhs=PT[ptb][:, j * nq:(j + 1) * nq],
                                               start=(pi == 0 and j == 0), stop=(pi == npair - 1 and j == 1))
                            return ins
                        S.op("pe", fpv, reads=[t_VA[e2][pi // 2], t_VA[e2][0], t_PT[ptb]], writes=[t_po[pb]])
                        if pi == npair - 1:
                            nlo, dlo = (0, 64) if e2 == 0 else (64, 0)
                            S.op("dve", lambda e, pb=pb, nq=nq, nlo=nlo, dlo=dlo: e.reciprocal(
                                out=rden[nlo:nlo + 64, 0:nq], in_=po[pb][dlo:dlo + 64, 0:nq]),
                                reads=[t_po[pb]], writes=[t_rden])
                            S.op("dve", lambda e, pb=pb, nq=nq, nlo=nlo, hp=hp, q0=q0: e.tensor_tensor(
                                out=AT[nlo:nlo + 64, hp, q0:q0 + nq], in0=po[pb][nlo:nlo + 64, 0:nq],
                                in1=rden[nlo:nlo + 64, 0:nq], op=ALU.mult),
                                reads=[t_po[pb], t_rden], writes=[t_AT])
            S.end_phase()

        rts = ExitStack()
        rts.__enter__()
        RT = sbuf(rts, "RT", [128, 8, NQ], BF16)
        with ExitStack() as ph:
            wkq = sbuf(ph, "wkq", [128, 8, 256], BF16)
            wvg2 = [sbuf(ph, "wvg%d" % i, [128, 8, 512], BF16) for i in range(2)]
            tb = [sbuf(ph, "rtb%d" % i, [128, 2, 512]) for i in range(2)]
            tmp1 = [sbuf(ph, "rtmpa%d" % i, [128, 512]) for i in range(2)]
            tmp2 = [sbuf(ph, "rtmpb%d" % i, [128, 512]) for i in range(2)]
            ktg = [sbuf(ph, "ktg%d" % i, [128, 512], BF16) for i in range(2)]
            KTr = sbuf(ph, "KTr", [128, 17 * 128], BF16)
            Ktok = sbuf(ph, "Ktok", [128, 17, 128], BF16)
            Ktp = [sbuf(ph, "Ktp%d" % i, [128, 4, 128], BF16) for i in range(2)]
            Vr = sbuf(ph, "Vr", [128, 17, 256], BF16)
            Vp = [sbuf(ph, "Vp%d" % i, [128, 4, 256], BF16) for i in range(2)]
            QTr = sbuf(ph, "QTr", [128, NQ], BF16)
            SG = [sbuf(ph, "SG%d" % i, [128, 256], BF16) for i in range(4)]
            ex = [sbuf(ph, "ex%d" % i, [128, 256]) for i in range(2)]
            t_ex = [T(), T()]
            Sst = sbuf(ph, "Sst", [128, 256])
            S16 = [sbuf(ph, "S16_%d" % i, [128, 256], BF16) for i in range(2)]
            SD = [sbuf(ph, "SD%d" % i, [128, 128], BF16) for i in range(2)]
            yn = [sbuf(ph, "yn%d" % i, [128, 256], BF16) for i in range(2)]
            Rtok = [sbuf(ph, "Rtok%d" % i, [128, 256], BF16) for i in range(2)]
            st6 = [sbuf(ph, "st6_%d" % i, [128, 6]) for i in range(2)]
            mvr = [sbuf(ph, "mvr%d" % i, [128, 2]) for i in range(2)]
            rstdr = [sbuf(ph, "rstdr%d" % i, [128, 1]) for i in range(2)]
            ksp = sbuf(ph, "ksp", [128, 128])
            epsq = sbuf(ph, "epsq", [128, 4])
            tri = sbuf(ph, "tri", [128, 128])
            pk = [psum(ph, "rpk%d" % i, [128, 512]) for i in range(2)]
            pv = [psum(ph, "rpv%d" % i, [128, 512]) for i in range(2)]
            ptr = psum(ph, "ptr", [128, 1024], BF16)
            pS = psum(ph, "pS", [128, 512])
            py = [psum(ph, "py%d" % i, [128, 512]) for i in range(2)]
            t_wkq = T()
            t_wvg2 = [T(), T()]
            t_tb = [T(), T()]
            t_tbs = [T(), T()]
            t_t1 = [T(), T()]
            t_t2 = [T(), T()]
            t_ktg = [T(), T()]
            t_KTr, t_Ktok, t_Vr, t_QTr, t_S, t_cst = (T() for _ in range(6))
            t_SG = [T(), T(), T(), T()]
            t_psc = [T(), T()]
            t_stt = [T(), T()]
            t_ptr2 = [T(), T()]
            t_Ktp = [T(), T()]
            t_Vp = [T(), T()]
            t_S16 = [T(), T()]
            t_SD = [T(), T()]
            t_yn = [T(), T()]
            t_Rtok = [T(), T()]
            t_st, t_RT = T(), T()
            t_pk = [T(), T()]
            t_pv = [T(), T()]
            t_ptr, t_pS = T(), T()
            t_py = [T(), T()]
            dma(S, ksp[:], ksp_d[:, :], writes=[t_cst])
            dma(S, epsq[:], epsq_d[:, :], writes=[t_cst])
            dma(S, tri[:], tri_d[:, :], writes=[t_cst])
            w_ret_v = w_ret.rearrange("h (k p) n -> h p k n", p=128)
            cnt = {"tb": 0, "pv": 0, "py": 0, "s16": 0, "sd": 0}

            def rrope(wcol, tsl, n, dst_ap, t_dst):
                b = cnt["tb"] % 2
                cnt["tb"] += 1
                dma(S, tb[b][:, 0, 0:n], cosR[:, tsl], writes=[t_tb[b]])
                dma(S, tb[b][:, 1, 0:n], sinR[:, tsl], writes=[t_tbs[b]])
                mm_group(S, pk[b][:, 0:n],
                         [(wkq[:, k, wcol:wcol + 128], h1T[:, k, tsl]) for k in range(8)],
                         reads=[t_wkq, t_h1], writes=[t_pk[b]])
                S.op("dve", lambda e, b=b, n=n: e.tensor_tensor(out=tmp1[b][:, 0:n], in0=pk[b][:, 0:n],
                                                                in1=tb[b][:, 0, 0:n], op=ALU.mult),
                     reads=[t_pk[b], t_tb[b]], writes=[t_t1[b]])
                S.op("dve", lambda e, b=b, n=n: e.tensor_tensor(out=tmp2[b][0:64, 0:n], in0=pk[b][64:128, 0:n],
                                                                in1=tb[b][0:64, 1, 0:n], op=ALU.mult),
                     reads=[t_pk[b], t_tbs[b]], writes=[t_t2[b]])
                S.op("dve", lambda e, b=b, n=n: e.tensor_tensor(out=tmp2[b][64:128, 0:n], in0=pk[b][0:64, 0:n],
                                                                in1=tb[b][64:128, 1, 0:n], op=ALU.mult),
                     reads=[t_pk[b], t_tbs[b]], writes=[t_t2[b]])
                S.op("pool", lambda e, b=b, n=n: e.tensor_tensor(out=dst_ap, in0=tmp1[b][:, 0:n], in1=tmp2[b][:, 0:n],
                                                                 op=ALU.add),
                     reads=[t_t1[b], t_t2[b]], writes=[t_dst])

            dma(S, wkq[:], w_ret_v[0][:, :, 0:256], writes=[t_wkq], q="pool")
            dma(S, wvg2[0][:], w_ret_v[0][:, :, 256:768], writes=[t_wvg2[0]], q="pool")
            for r in range(4):
                wvg = wvg2[r % 2]
                t_wvg = t_wvg2[r % 2]
                if r < 3:
                    dma(S, wvg2[(r + 1) % 2][:], w_ret_v[r + 1][:, :, 256:768], writes=[t_wvg2[(r + 1) % 2]], q="pool")
                for g in range(8):
                    tsl = slice(g * 512, (g + 1) * 512)
                    kb = g % 2
                    if g < 4:
                        rrope(0, tsl, 512, ktg[kb][:, :], t_ktg[kb])
                    else:
                        c0 = 4 * g - 15
                        rrope(0, tsl, 512, KTr[:, c0 * 128:(c0 + 4) * 128], t_KTr)
                    if g == 3:
                        S.op("pool", lambda e, kb=kb: e.tensor_copy(out=KTr[:, 0:128], in_=ktg[kb][:, 384:512]),
                             reads=[t_ktg[kb]], writes=[t_KTr])
                    for half in range(2):
                        vb_ = cnt["pv"] % 2
                        cnt["pv"] += 1

                        def fv(e, g=g, half=half, vb_=vb_, wvg=wvg):
                            ins = None
                            for j in range(2):
                                t = g * 4 + half * 2 + j
                                for k in range(8):
                                    ins = e.matmul(pv[vb_][:, j * 256:(j + 1) * 256], lhsT=h1T[:, k, t * 128:(t + 1) * 128],
                                                   rhs=wvg[:, k, 0:256], start=(k == 0), stop=(k == 7))
                            return ins
                        S.op("pe", fv, reads=[t_wvg, t_h1], writes=[t_pv[vb_]])
                        for j in range(2):
                            n_ = g * 4 + half * 2 + j
                            col = r * 32 + n_
                            if n_ <= 14:
                                dstap, tdst = Vp[kb][:, half * 2 + j, :], t_Vp[kb]
                            else:
                                dstap, tdst = Vr[:, n_ - 15, :], t_Vr
                            S.op("act", lambda e, vb_=vb_, j=j, col=col, dstap=dstap: e.activation(
                                out=dstap, in_=pv[vb_][:, j * 256:(j + 1) * 256], func=AF.Copy, scale=ksp[:, col:col + 1]),
                                reads=[t_pv[vb_], t_cst], writes=[tdst])
                    src = ktg[kb] if g < 4 else None

                    def ftr(e, g=g, kb=kb):
                        ins = None
                        for j in range(4):
                            if g < 4:
                                in_ap = ktg[kb][:, j * 128:(j + 1) * 128]
                            else:
                                c = 4 * g - 15 + j
                                in_ap = KTr[:, c * 128:(c + 1) * 128]
                            ins = e.transpose(ptr[:, j * 128:(j + 1) * 128], in_ap, identB[:])
                        return ins
                    S.op("pe", ftr, reads=[t_ktg[kb] if g < 4 else t_KTr, t_const], writes=[t_ptr])
                    ptr3 = ptr[:, 0:512].rearrange("p (j n) -> p j n", n=128)
                    if g < 3:
                        S.op("act", lambda e, kb=kb, ptr3=ptr3, r=r: e.mul(out=Ktp[kb][:], in_=ptr3, mul=gC[r]),
                             reads=[t_ptr], writes=[t_Ktp[kb]])
                    elif g == 3:
                        S.op("act", lambda e, kb=kb, ptr3=ptr3, r=r: e.mul(out=Ktp[kb][:, 0:3, :], in_=ptr3[:, 0:3, :], mul=gC[r]),
                             reads=[t_ptr], writes=[t_Ktp[kb]])
                        S.op("act", lambda e, ptr3=ptr3, r=r: e.mul(out=Ktok[:, 0, :], in_=ptr3[:, 3, :], mul=gC[r]),
                             reads=[t_ptr], writes=[t_Ktok])
                    else:
                        c0 = 4 * g - 15
                        S.op("act", lambda e, c0=c0, ptr3=ptr3, r=r: e.mul(out=Ktok[:, c0:c0 + 4, :], in_=ptr3, mul=gC[r]),
                             reads=[t_ptr], writes=[t_Ktok])
                    if g < 4:
                        nn = 4 if g < 3 else 3

                        def fs(e, g=g, kb=kb, nn=nn):
                            ins = None
                            for j in range(nn):
                                n_ = g * 4 + j
                                ins = e.matmul(pS[:, 0:256], lhsT=Ktp[kb][:, j, :], rhs=Vp[kb][:, j, :],
                                               start=(n_ == 0), stop=(n_ == 14))
                            return ins
                        S.op("pe", fs, reads=[t_Ktp[kb], t_Vp[kb]], writes=[t_pS])
                S.op("dve", lambda e: e.tensor_copy(out=Sst[:], in_=pS[:, 0:256]), reads=[t_pS], writes=[t_S])
                sbi = cnt["s16"] % 2
                cnt["s16"] += 1
                S.op("act", lambda e, sbi=sbi: e.copy(out=S16[sbi][:], in_=pS[:, 0:256]), reads=[t_pS], writes=[t_S16[sbi]])
                for gi, (q0, n, l0) in enumerate(QG):
                    rrope(128, slice(l0, l0 + n), n, QTr[:, q0:q0 + n], t_QTr)
                if r < 3:
                    dma(S, wkq[:], w_ret_v[r + 1][:, :, 0:256], writes=[t_wkq], q="pool")
                s16_of = {}

                def stageA(qc, wvg=wvg, t_wvg=t_wvg):
                    csl = slice(qc * 128, (qc + 1) * 128)
                    l0 = 1920 + qc * 128
                    vb_ = cnt["pv"] % 2
                    cnt["pv"] += 1
                    g3 = qc % 4
                    x2 = qc % 2
                    p2 = qc % 2
                    mm_group(S, pv[vb_][:, 0:256], [(h1T[:, k, l0:l0 + 128], wvg[:, k, 256:512]) for k in range(8)],
                             reads=[t_wvg, t_h1], writes=[t_pv[vb_]])
                    S.op("act", lambda e, vb_=vb_, x2=x2: e.activation(out=ex[x2][:], in_=pv[vb_][:, 0:256], func=AF.Exp, scale=-1.0),
                         reads=[t_pv[vb_]], writes=[t_ex[x2]])
                    S.op("act", lambda e, x2=x2: e.activation(out=ex[x2][:], in_=ex[x2][:], func=AF.Ln, bias=oneA[:], scale=1.0),
                         reads=[t_ex[x2], t_const], writes=[t_ex[x2]])
                    S.op("act", lambda e, x2=x2: e.activation(out=ex[x2][:], in_=ex[x2][:], func=AF.Exp, scale=-1.0),
                         reads=[t_ex[x2]], writes=[t_ex[x2]])
                    S.op("dve", lambda e, vb_=vb_, g3=g3, x2=x2: e.tensor_tensor(out=SG[g3][:], in0=pv[vb_][:, 0:256], in1=ex[x2][:], op=ALU.mult),
                         reads=[t_pv[vb_], t_ex[x2]], writes=[t_SG[g3]])
                    S.op("pe", lambda e, csl=csl, p2=p2: e.matmul(pk[p2][:, 0:128], lhsT=KTr[:, csl], rhs=QTr[:, csl],
                                                               start=True, stop=True),
                         reads=[t_KTr, t_QTr], writes=[t_pk[p2]])
                    S.op("dve", lambda e, p2=p2: e.tensor_tensor(out=SD[p2][:], in0=pk[p2][:, 0:128], in1=tri[:], op=ALU.mult),
                         reads=[t_pk[p2], t_cst], writes=[t_SD[p2]])

                def stageB(qc, r=r):
                    csl = slice(qc * 128, (qc + 1) * 128)
                    p2 = qc % 2
                    sbi = s16_of[qc]

                    def fy(e, qc=qc, csl=csl, p2=p2, sbi=sbi):
                        e.matmul(py[p2][:, 0:256], lhsT=SD[p2][:], rhs=Vr[:, qc, :], start=True, stop=False)
                        return e.matmul(py[p2][:, 0:256], lhsT=QTr[:, csl], rhs=S16[sbi][:], start=False, stop=True)
                    S.op("pe", fy, reads=[t_SD[p2], t_Vr, t_QTr, t_S16[sbi]], writes=[t_py[p2]])
                    if qc < 16:
                        S.op("pe", lambda e, qc=qc: e.matmul(pS[:, 0:256], lhsT=Ktok[:, qc, :], rhs=Vr[:, qc, :], start=True, stop=True),
                             reads=[t_Ktok, t_Vr], writes=[t_pS])
                        S.op("dve", lambda e, r=r: e.scalar_tensor_tensor(out=Sst[:], in0=Sst[:], scalar=gC[r], in1=pS[:, 0:256],
                                                                          op0=ALU.mult, op1=ALU.add),
                             reads=[t_pS, t_S], writes=[t_S])
                        nsb = 1 - sbi
                        s16_of[qc + 1] = nsb
                        S.op("act", lambda e, nsb=nsb: e.copy(out=S16[nsb][:], in_=Sst[:]), reads=[t_S], writes=[t_S16[nsb]])

                def stageC1(qc, r=r):
                    p2 = qc % 2
                    S.op("dve", lambda e, p2=p2: e.bn_stats(out=st6[p2][:], in_=py[p2][:, 0:256]), reads=[t_py[p2]], writes=[t_stt[p2]])
                    S.op("dve", lambda e, p2=p2: e.bn_aggr(out=mvr[p2][:], in_=st6[p2][:]), reads=[t_stt[p2]], writes=[t_stt[p2]])
                    S.op("act", lambda e, r=r, p2=p2: e.activation(out=rstdr[p2][:], in_=mvr[p2][:, 1:2], func=AF.Ln,
                                                                   bias=epsq[:, r:r + 1], scale=1.0),
                         reads=[t_stt[p2], t_cst], writes=[t_stt[p2]])
                    S.op("act", lambda e, p2=p2: e.activation(out=rstdr[p2][:], in_=rstdr[p2][:], func=AF.Exp, scale=-0.5),
                         reads=[t_stt[p2]], writes=[t_stt[p2]])
                    S.op("dve", lambda e, p2=p2: e.tensor_scalar(out=yn[p2][:], in0=py[p2][:, 0:256], scalar1=mvr[p2][:, 0:1],
                                                                 scalar2=rstdr[p2][:], op0=ALU.subtract, op1=ALU.mult),
                         reads=[t_py[p2], t_stt[p2]], writes=[t_yn[p2]])

                def stageC2(qc, r=r):
                    csl = slice(qc * 128, (qc + 1) * 128)
                    p2 = qc % 2
                    g3 = qc % 4
                    S.op("pool", lambda e, p2=p2, g3=g3: e.tensor_tensor(out=Rtok[p2][:], in0=yn[p2][:], in1=SG[g3][:], op=ALU.mult),
                         reads=[t_yn[p2], t_SG[g3]], writes=[t_Rtok[p2]])
                    o0 = 512

                    def ftr2(e, p2=p2, o0=o0):
                        e.transpose(ptr[:, o0:o0 + 128], Rtok[p2][:, 0:128], identB[:])
                        return e.transpose(ptr[:, o0 + 128:o0 + 256], Rtok[p2][:, 128:256], identB[:])
                    S.op("pe", ftr2, reads=[t_Rtok[p2], t_const], writes=[t_ptr])
                    S.op("act", lambda e, r=r, csl=csl, o0=o0: e.copy(
                        out=RT[:, 2 * r:2 * r + 2, csl], in_=ptr[:, o0:o0 + 256].rearrange("p (j n) -> p j n", n=128)),
                        reads=[t_ptr], writes=[t_RT])

                s16_of[0] = sbi
                for st in range(17 + 3):
                    if st < 17:
                        stageA(st)
                    if 0 <= st - 1 < 17:
                        stageB(st - 1)
                    if 0 <= st - 2 < 17:
                        stageC1(st - 2)
                    if 0 <= st - 3 < 17:
                        stageC2(st - 3)
            S.end_phase()

        with ExitStack() as ph:
            wg = [sbuf(ph, "wg%d" % i, [128, 8, 256], BF16) for i in range(2)]
            wpa = [sbuf(ph, "wpa%d" % i, [128, 4, 128], BF16) for i in range(2)]
            wpr = [sbuf(ph, "wpr%d" % i, [128, 8, 128], BF16) for i in range(2)]
            sa = [sbuf(ph, "sa%d" % i, [128, 512]) for i in range(2)]
            sr = [sbuf(ph, "sr%d" % i, [128, 512]) for i in range(2)]
            m1 = [sbuf(ph, "m1%d" % i, [128, 512]) for i in range(2)]
            m2 = [sbuf(ph, "m2%d" % i, [128, 512]) for i in range(2)]
            mT = [sbuf(ph, "mT%d" % i, [128, 512], BF16) for i in range(2)]
            pg = [[psum(ph, "pg%d_%d" % (i, j), [128, 512]) for j in range(4)] for i in range(2)]
            t_w = [T(), T()]
            t_pg = [[T() for _ in range(4)] for _ in range(2)]
            t_sa, t_sr, t_m1, t_m2, t_mT = ([T(), T()] for _ in range(5))
            t_AT2, t_RT2 = T(), T()
            w_g_v = w_g.rearrange("(k p) n -> p k n", p=128)
            w_pa_v = w_pa.rearrange("(k p) n -> p k n", p=128)
            w_pr_v = w_pr.rearrange("(k p) n -> p k n", p=128)
            it = 0

            def load_gw(f):
                wb = f % 2
                fs_ = slice(f * 128, (f + 1) * 128)
                dma(S, wg[wb][:, :, 0:128], w_g_v[:, :, f * 128:(f + 1) * 128], writes=[t_w[wb]], q="pool")
                dma(S, wg[wb][:, :, 128:256], w_g_v[:, :, 1024 + f * 128:1024 + (f + 1) * 128], writes=[t_w[wb]], q="pool")
                dma(S, wpa[wb][:], w_pa_v[:, :, fs_], writes=[t_w[wb]], q="pool")
                dma(S, wpr[wb][:], w_pr_v[:, :, fs_], writes=[t_w[wb]], q="pool")
            load_gw(0)
            for f in range(8):
                wb = f % 2
                if f < 7:
                    load_gw(f + 1)
                for gi, (q0, n, l0) in enumerate(QG):
                    b = it % 2
                    it += 1
                    mm_group(S, pg[b][0][:, 0:n], [(wg[wb][:, k, 0:128], h1T[:, k, l0:l0 + n]) for k in range(8)],
                             reads=[t_w[wb], t_h1], writes=[t_pg[b][0]])
                    mm_group(S, pg[b][1][:, 0:n], [(wg[wb][:, k, 128:256], h1T[:, k, l0:l0 + n]) for k in range(8)],
                             reads=[t_w[wb], t_h1], writes=[t_pg[b][1]])
                    mm_group(S, pg[b][2][:, 0:n], [(wpa[wb][:, k, :], AT[:, k, q0:q0 + n]) for k in range(4)],
                             reads=[t_w[wb], t_AT2], writes=[t_pg[b][2]])
                    mm_group(S, pg[b][3][:, 0:n], [(wpr[wb][:, k, :], RT[:, k, q0:q0 + n]) for k in range(8)],
                             reads=[t_w[wb], t_RT2], writes=[t_pg[b][3]])
                    S.op("act", lambda e, b=b, n=n: e.activation(out=sa[b][:, 0:n], in_=pg[b][0][:, 0:n], func=AF.Sigmoid),
                         reads=[t_pg[b][0]], writes=[t_sa[b]])
                    S.op("act", lambda e, b=b, n=n: e.activation(out=sr[b][:, 0:n], in_=pg[b][1][:, 0:n], func=AF.Sigmoid),
                         reads=[t_pg[b][1]], writes=[t_sr[b]])
                    S.op("dve", lambda e, b=b, n=n: e.tensor_tensor(out=m1[b][:, 0:n], in0=pg[b][2][:, 0:n], in1=sa[b][:, 0:n], op=ALU.mult),
                         reads=[t_pg[b][2], t_sa[b]], writes=[t_m1[b]])
                    S.op("dve", lambda e, b=b, n=n: e.tensor_tensor(out=m2[b][:, 0:n], in0=pg[b][3][:, 0:n], in1=sr[b][:, 0:n], op=ALU.mult),
                         reads=[t_pg[b][3], t_sr[b]], writes=[t_m2[b]])
                    S.op("pool", lambda e, b=b, n=n: e.tensor_tensor(out=mT[b][:, 0:n], in0=m1[b][:, 0:n], in1=m2[b][:, 0:n], op=ALU.add),
                         reads=[t_m1[b], t_m2[b]], writes=[t_mT[b]])
                    dma(S, scrM[q0 // 128:(q0 + n) // 128, :, f, :].rearrange("t p c -> p t c"),
                        mT[b][:, 0:n].rearrange("p (t c) -> p t c", c=128), reads=[t_mT[b]])
            S.end_phase()
        rts.close()
        mix.close()

        with ExitStack() as ph:
            NBF = 3
            wo = sbuf(ph, "wo", [128, 8, D], BF16)
            lg = sbuf(ph, "lg", [128, D])
            lb_ = sbuf(ph, "lb", [128, D])
            mTt = [sbuf(ph, "mTt%d" % i, [128, 8, 128], BF16) for i in range(NBF)]
            xt = [sbuf(ph, "xt%d" % i, [128, D]) for i in range(NBF)]
            z = [sbuf(ph, "z%d" % i, [128, D]) for i in range(NBF)]
            zn = [sbuf(ph, "zn%d" % i, [128, D]) for i in range(NBF)]
            x1 = [sbuf(ph, "x1%d" % i, [128, D]) for i in range(NBF)]
            h2t = [sbuf(ph, "h2t%d" % i, [128, 8, 128], BF16) for i in range(NBF)]
            st12 = [sbuf(ph, "st12_%d" % i, [128, 2, 6]) for i in range(NBF)]
            mv = [sbuf(ph, "mv1_%d" % i, [128, 2]) for i in range(NBF)]
            rstd = [sbuf(ph, "rstd1_%d" % i, [128, 1]) for i in range(NBF)]
            py = [psum(ph, "lpy%d" % i, [128, 1024]) for i in range(2)]
            pz = [psum(ph, "lpz%d" % i, [128, 1024]) for i in range(2)]
            t_wo, t_lc = T(), T()
            t_mTt, t_xt, t_z, t_zn, t_x1, t_h2t, t_st = ([T() for _ in range(NBF)] for _ in range(7))
            t_py, t_pz = [T(), T()], [T(), T()]
            dma(S, wo[:], w_o.rearrange("(k p) n -> p k n", p=128), writes=[t_wo], q="pool")
            dma(S, lg[:], lnbc[0], writes=[t_lc])
            dma(S, lb_[:], lnbc[1], writes=[t_lc])

            def l1_A(qt):
                b = qt % NBF
                pb = qt % 2
                dma(S, mTt[b][:], scrM[qt], writes=[t_mTt[b]])
                dma(S, xt[b][:], xq[qt * 128:(qt + 1) * 128, :], writes=[t_xt[b]])
                for half in range(2):
                    mm_group(S, py[pb][:, half * 512:(half + 1) * 512],
                             [(mTt[b][:, k, :], wo[:, k, half * 512:(half + 1) * 512]) for k in range(8)],
                             reads=[t_mTt[b], t_wo], writes=[t_py[pb]])
                S.op("dve", lambda e: e.tensor_tensor(out=z[b][:], in0=py[pb][:], in1=g1bc[:], op=ALU.mult),
                     reads=[t_py[pb], t_const], writes=[t_z[b]])
                S.op("pool", lambda e: e.tensor_tensor(out=z[b][:], in0=z[b][:], in1=xt[b][:], op=ALU.add),
                     reads=[t_z[b], t_xt[b]], writes=[t_z[b]])
                for half in range(2):
                    S.op("dve", lambda e, half=half: e.bn_stats(out=st12[b][:, half, :], in_=z[b][:, half * 512:(half + 1) * 512]),
                         reads=[t_z[b]], writes=[t_st[b]])
                S.op("dve", lambda e: e.bn_aggr(out=mv[b][:], in_=st12[b][:]), reads=[t_st[b]], writes=[t_st[b]])
                S.op("act", lambda e: e.activation(out=rstd[b][:], in_=mv[b][:, 1:2], func=AF.Sqrt, bias=epsA[:], scale=1.0),
                     reads=[t_st[b], t_const], writes=[t_st[b]])
                S.op("dve", lambda e: e.reciprocal(out=rstd[b][:], in_=rstd[b][:]), reads=[t_st[b]], writes=[t_st[b]])
                S.op("dve", lambda e: e.tensor_scalar(out=zn[b][:], in0=z[b][:], scalar1=mv[b][:, 0:1], scalar2=rstd[b][:],
                                                      op0=ALU.subtract, op1=ALU.mult),
                     reads=[t_z[b], t_st[b]], writes=[t_zn[b]])
                if qt >= 1:
                    S.op("pool", lambda e: e.tensor_tensor(out=x1[b][:], in0=zn[b][:], in1=lg[:], op=ALU.mult),
                         reads=[t_zn[b], t_lc], writes=[t_x1[b]])
                    S.op("dve", lambda e: e.tensor_tensor(out=x1[b][:], in0=x1[b][:], in1=lb_[:], op=ALU.add),
                         reads=[t_x1[b], t_lc], writes=[t_x1[b]])
                    dma(S, out[(qt - 1) * 128:qt * 128, :], x1[b][:], reads=[t_x1[b]], q="pool")

            def l1_B(qt):
                b = qt % NBF
                pb = qt % 2

                def ftz(e):
                    ins = None
                    for k in range(8):
                        ins = e.transpose(pz[pb][:, k * 128:(k + 1) * 128], zn[b][:, k * 128:(k + 1) * 128], identF[:])
                    return ins
                S.op("pe", ftz, reads=[t_zn[b], t_const], writes=[t_pz[pb]])

                def fh2(e):
                    ins = None
                    for k in range(8):
                        ins = e.activation(out=h2t[b][:, k, :], in_=pz[pb][:, k * 128:(k + 1) * 128], func=AF.Identity,
                                           scale=A2[:, k:k + 1], bias=B2[:, k:k + 1])
                    return ins
                S.op("act", fh2, reads=[t_pz[pb], t_const], writes=[t_h2t[b]])
                dma(S, scrH[:, :, qt * 128:(qt + 1) * 128], h2t[b][:], reads=[t_h2t[b]], q="pool")

            for st_ in range(17 + 1):
                if st_ < 17:
                    l1_A(st_)
                if st_ - 1 >= 0:
                    l1_B(st_ - 1)
            S.end_phase()

        with ExitStack() as ph:
            wd = sbuf(ph, "wd", [128, NFC, D], BF16)
            l2g = sbuf(ph, "l2g", [128, D])
            l2b = sbuf(ph, "l2b", [128, D])
            cw = sbuf(ph, "cw", [128, NFC * 3])
            cb_ = sbuf(ph, "cb", [128, NFC])
            h2h = sbuf(ph, "h2h", [128, 8, 128], BF16)
            h2s2 = [sbuf(ph, "h2s%d" % i, [128, 8, 1024], BF16) for i in range(2)]
            t_h2s2 = [T(), T()]
            wgu = [sbuf(ph, "wgu%d" % i, [128, 8, 256], BF16) for i in range(3)]
            gs = [sbuf(ph, "gs%d" % i, [128, 1026]) for i in range(2)]
            gc = [sbuf(ph, "gc%d" % i, [128, 512]) for i in range(2)]
            ge = [sbuf(ph, "ge%d" % i, [128, 512]) for i in range(2)]
            gsave = sbuf(ph, "gsave", [128, NFC, 2])
            aT = sbuf(ph, "aT", [128, NFC, 1024], BF16)
            x1t = [sbuf(ph, "x1t%d" % i, [128, D]) for i in range(2)]
            z = [sbuf(ph, "fz%d" % i, [128, D]) for i in range(2)]
            ot = [sbuf(ph, "ot%d" % i, [128, D]) for i in range(2)]
            st12 = sbuf(ph, "fst12", [128, 2, 6])
            mv = sbuf(ph, "fmv", [128, 2])
            rstd = sbuf(ph, "frstd", [128, 1])
            pgt = [psum(ph, "fpg%d" % i, [128, 512]) for i in range(2)]
            put = [psum(ph, "fpu%d" % i, [128, 512]) for i in range(2)]
            phh = psum(ph, "fph", [128, 512])
            py2 = psum(ph, "fpy", [128, 1024])
            t_wd, t_lc, t_h2h, t_h2s, t_gsave, t_aT, t_st, t_ph, t_py2 = (T() for _ in range(9))
            t_wgu, t_gs, t_gc, t_ge, t_x1t, t_z, t_ot, t_pg, t_pu = ([T(), T(), T()] for _ in range(9))
            dma(S, l2g[:], lnbc[2], writes=[t_lc])
            dma(S, l2b[:], lnbc[3], writes=[t_lc])
            dma(S, cw[:], convw[:, :], writes=[t_lc])
            dma(S, cb_[:], convb[:, :], writes=[t_lc])
            dma(S, h2h[:], scrH[:, :, 0:128], writes=[t_h2h])
            w_gu_v = w_ffgu.rearrange("i (k p) n -> i p k n", p=128)
            it = 0

            def load_gu(u):
                dma(S, wgu[u % 3][:], w_gu_v[u % NFC], writes=[t_wgu[u % 3]], q="pool")
            load_gu(0)
            load_gu(1)
            dma(S, wd[:], w_ffd.rearrange("(i p) n -> p i n", p=128), writes=[t_wd], q="pool")
            for sg in range(2):
                dma(S, h2s2[sg][:], scrH[:, :, 128 + 1024 * sg:128 + 1024 * (sg + 1)], writes=[t_h2s2[sg]])
            for sg in range(2):
                h2s = h2s2[sg]
                t_h2s = t_h2s2[sg]
                for i in range(NFC):
                    u = sg * NFC + i
                    wb = u % 3
                    gb = i % 2
                    if u + 2 < 2 * NFC:
                        load_gu(u + 2)
                    if sg == 0:
                        mm_group(S, phh[:, 0:2], [(wgu[wb][:, k, 0:128], h2h[:, k, 126:128]) for k in range(8)],
                                 reads=[t_wgu[wb], t_h2h], writes=[t_ph])
                        S.op("act", lambda e, gb=gb: e.mul(out=gs[gb][:, 0:2], in_=phh[:, 0:2], mul=flag[:, 0:1]),
                             reads=[t_ph, t_const], writes=[t_gs[gb]])
                    else:
                        S.op("act", lambda e, gb=gb, i=i: e.copy(out=gs[gb][:, 0:2], in_=gsave[:, i, :]),
                             reads=[t_gsave], writes=[t_gs[gb]])
                    for half in range(2):
                        pb = it % 2
                        it += 1
                        tsl = slice(half * 512, (half + 1) * 512)
                        mm_group(S, pgt[pb][:], [(wgu[wb][:, k, 0:128], h2s[:, k, tsl]) for k in range(8)],
                                 reads=[t_wgu[wb], t_h2s], writes=[t_pg[pb]])
                        mm_group(S, put[pb][:], [(wgu[wb][:, k, 128:256], h2s[:, k, tsl]) for k in range(8)],
                                 reads=[t_wgu[wb], t_h2s], writes=[t_pu[pb]])
                        o0 = 2 + half * 512
                        S.op("act", lambda e, gb=gb, pb=pb, o0=o0: e.copy(out=gs[gb][:, o0:o0 + 512], in_=pgt[pb][:]),
                             reads=[t_pg[pb]], writes=[t_gs[gb]])
                        S.op("act", lambda e, pb=pb, i=i: e.activation(out=gc[pb][:], in_=pgt[pb][:], func=AF.Identity,
                                                                       scale=cw[:, 3 * i + 2:3 * i + 3], bias=cb_[:, i:i + 1]),
                             reads=[t_pg[pb], t_lc], writes=[t_gc[pb]])
                        S.op("dve", lambda e, gb=gb, pb=pb, i=i, o0=o0: e.scalar_tensor_tensor(
                            out=gc[pb][:], in0=gs[gb][:, o0 - 1:o0 + 511], scalar=cw[:, 3 * i + 1:3 * i + 2], in1=gc[pb][:],
                            op0=ALU.mult, op1=ALU.add), reads=[t_gs[gb], t_gc[pb], t_lc], writes=[t_gc[pb]])
                        S.op("dve", lambda e, gb=gb, pb=pb, i=i, o0=o0: e.scalar_tensor_tensor(
                            out=gc[pb][:], in0=gs[gb][:, o0 - 2:o0 + 510], scalar=cw[:, 3 * i:3 * i + 1], in1=gc[pb][:],
                            op0=ALU.mult, op1=ALU.add), reads=[t_gs[gb], t_gc[pb], t_lc], writes=[t_gc[pb]])
                        S.op("act", lambda e, pb=pb: e.activation(out=ge[pb][:], in_=gc[pb][:], func=AF.Gelu),
                             reads=[t_gc[pb]], writes=[t_ge[pb]])
                        S.op("dve", lambda e, pb=pb, i=i, tsl=tsl: e.tensor_tensor(out=aT[:, i, tsl], in0=put[pb][:], in1=ge[pb][:], op=ALU.mult),
                             reads=[t_pu[pb], t_ge[pb]], writes=[t_aT])
                    if sg == 0:
                        S.op("pool", lambda e, gb=gb, i=i: e.tensor_copy(out=gsave[:, i, :], in_=gs[gb][:, 1024:1026]),
                             reads=[t_gs[gb]], writes=[t_gsave])
                for t in range(8):
                    b = t % 2
                    row0 = sg * 1024 + t * 128
                    dma(S, x1t[b][:], out[row0:row0 + 128, :], writes=[t_x1t[b]])
                    for half in range(2):
                        mm_group(S, py2[:, half * 512:(half + 1) * 512],
                                 [(aT[:, i, t * 128:(t + 1) * 128], wd[:, i, half * 512:(half + 1) * 512]) for i in range(NFC)],
                                 reads=[t_aT, t_wd], writes=[t_py2])
                    S.op("dve", lambda e, b=b: e.tensor_tensor(out=z[b][:], in0=py2[:], in1=g2bc[:], op=ALU.mult),
                         reads=[t_py2, t_const], writes=[t_z[b]])
                    S.op("pool", lambda e, b=b: e.tensor_tensor(out=z[b][:], in0=z[b][:], in1=x1t[b][:], op=ALU.add),
                         reads=[t_z[b], t_x1t[b]], writes=[t_z[b]])
                    for half in range(2):
                        S.op("dve", lambda e, b=b, half=half: e.bn_stats(out=st12[:, half, :], in_=z[b][:, half * 512:(half + 1) * 512]),
                             reads=[t_z[b]], writes=[t_st])
                    S.op("dve", lambda e: e.bn_aggr(out=mv[:], in_=st12[:]), reads=[t_st], writes=[t_st])
                    S.op("act", lambda e: e.activation(out=rstd[:], in_=mv[:, 1:2], func=AF.Sqrt, bias=epsA[:], scale=1.0),
                         reads=[t_st, t_const], writes=[t_st])
                    S.op("dve", lambda e: e.reciprocal(out=rstd[:], in_=rstd[:]), reads=[t_st], writes=[t_st])
                    S.op("dve", lambda e, b=b: e.tensor_scalar(out=ot[b][:], in0=z[b][:], scalar1=mv[:, 0:1], scalar2=rstd[:],
                                                               op0=ALU.subtract, op1=ALU.mult),
                         reads=[t_z[b], t_st], writes=[t_ot[b]])
                    S.op("pool", lambda e, b=b: e.tensor_tensor(out=ot[b][:], in0=ot[b][:], in1=l2g[:], op=ALU.mult),
                         reads=[t_ot[b], t_lc], writes=[t_ot[b]])
                    S.op("pool", lambda e, b=b: e.tensor_tensor(out=ot[b][:], in0=ot[b][:], in1=l2b[:], op=ALU.add),
                         reads=[t_ot[b], t_lc], writes=[t_ot[b]])
                    dma(S, out[row0:row0 + 128, :], ot[b][:], reads=[t_ot[b], t_x1t[b]], q="pool")
            S.end_phase()
    return nc


def _const_tables(c):
    f32 = np.float32
    lt = np.arange(SEQ)
    pos = (lt if c == 1 else np.maximum(lt - HALF, 0)).astype(f32)
    inv_freq = (10000.0 ** (-np.arange(0, 64, 2, dtype=f32) / 64)).astype(f32)
    ang = pos[None, :] * inv_freq[:, None]
    cos32, sin32 = np.cos(ang).astype(f32), np.sin(ang).astype(f32)
    cos64 = np.concatenate([cos32, cos32], 0)
    sin64 = np.concatenate([-sin32, sin32], 0)
    cosM = np.concatenate([cos64, cos64], 0)
    sinM = np.concatenate([sin64, sin64], 0)
    freq = (1.0 / (10000.0 ** np.linspace(0.0, 1.0, 64, dtype=f32))).astype(f32)
    angr = pos[None, :] * freq[:, None]
    cr, sr = np.cos(angr).astype(f32), np.sin(angr).astype(f32)
    cosR = np.concatenate([cr, cr], 0)
    sinR = np.concatenate([-sr, sr], 0)
    vb = np.full((17, 16), NEGV, f32)
    own = np.zeros((17, 16), f32)
    for qt in range(17):
        lb = 7 if qt == 0 else 8 + (qt - 1) // 2
        for j in range(16):
            if j < lb and (j >= 8 or c == 1):
                vb[qt, j] = 0.0
        own[qt, lb] = 1.0
    vb = np.broadcast_to(vb.reshape(1, 272), (128, 272)).copy()
    own = np.broadcast_to(own.reshape(1, 272), (128, 272)).copy()
    p = np.arange(128, dtype=np.float64)
    ksp = np.zeros((128, 128), np.float64)
    epsq = np.zeros((128, 4), np.float64)
    for r in range(4):
        g = 1.0 - 2.0 ** (-5.0 - r)
        gc = g ** 128
        ks = (128.0 ** -0.5) * g ** (-(p + 1.0))
        for n in range(32):
            if n <= 14:
                wgt = gc ** (14 - n) * float(c)
            elif n == 15:
                wgt = float(c)
            else:
                wgt = 1.0
            ksp[:, r * 32 + n] = ks * wgt
        epsq[:, r] = LN_EPS / (g ** (p + 1.0)) ** 2
    k = np.arange(128)[:, None]
    q = np.arange(256)[None, :]
    tri256 = (k <= q).astype(f32)
    tri256b = ((k + 128) <= q).astype(f32)
    dmask = np.concatenate([tri256, tri256b], 1)
    tri = (k <= np.arange(128)[None, :]).astype(f32)
    hmask = np.concatenate([np.ones((128, 128), f32), tri], 1)
    onehot = (np.arange(SEQ)[None, :] // 256 == np.arange(16)[:, None]).astype(f32)
    return dict(cosM=cosM, sinM=sinM, cosR=cosR, sinR=sinR, vb=vb, ownhot=own, ksp=ksp.astype(f32),
                epsq=epsq.astype(f32), dmask=dmask, hmask=hmask, tri=tri, onehot=onehot,
                ident=np.eye(128, dtype=f32), flag=np.full((128, 1), float(c), f32))


_NC_CACHE = {}


def kernel(x, c, w_ada, b_ada, w_in, w_proj_moba, w_proj_ret, w_out, ln1_g, ln1_b,
           w_ff_gate, w_ff_up, ff_conv_w, ff_conv_b, w_ff_down, ln2_g, ln2_b):
    f32 = np.float32
    x = np.asarray(x, f32)
    c = np.asarray(c, f32)
    w_in0 = np.asarray(w_in, f32)[0]
    perm_m = np.concatenate([np.arange(32, 64), np.arange(0, 32)])
    perm_r = np.concatenate([np.arange(0, 128, 2), np.arange(1, 128, 2)])
    w_moba = np.empty((4, D, 384), f32)
    for hp in range(4):
        mq = w_in0[:, hp * 128:(hp + 1) * 128]
        mk = w_in0[:, 512 + hp * 128:512 + (hp + 1) * 128]
        mv = w_in0[:, 1024 + hp * 128:1024 + (hp + 1) * 128]
        w_moba[hp] = np.concatenate([mk, mq, mv], 1)
    w_ret = np.empty((4, D, 768), f32)
    for r in range(4):
        rq = w_in0[:, 1536 + r * 128:1536 + (r + 1) * 128]
        rk = w_in0[:, 2048 + r * 128:2048 + (r + 1) * 128]
        rv = w_in0[:, 2560 + r * 256:2560 + (r + 1) * 256]
        rg = w_in0[:, 3584 + r * 256:3584 + (r + 1) * 256]
        w_ret[r] = np.concatenate([rk[:, perm_r], rq[:, perm_r], rv, rg], 1)
    w_g = np.ascontiguousarray(w_in0[:, 4608:6656])
    wgate = np.asarray(w_ff_gate, f32)[0]
    wup = np.asarray(w_ff_up, f32)[0]
    w_ffgu = np.empty((NFC, D, 256), f32)
    for i in range(NFC):
        w_ffgu[i, :, 0:128] = wgate[:, i * 128:(i + 1) * 128]
        w_ffgu[i, :, 128:256] = wup[:, i * 128:(i + 1) * 128]
    convw = np.asarray(ff_conv_w, f32)[0]
    convwT = np.ascontiguousarray(convw.reshape(3, NFC, 128).transpose(2, 1, 0).reshape(128, NFC * 3))
    convbT = np.ascontiguousarray(np.asarray(ff_conv_b, f32)[0].reshape(NFC, 128).T)
    lnbc = np.stack([np.broadcast_to(np.asarray(v, f32)[0][None, :], (128, D)) for v in (ln1_g, ln1_b, ln2_g, ln2_b)]).copy()
    ln1T = np.ascontiguousarray(np.concatenate([np.asarray(ln1_g, f32)[0].reshape(8, 128).T,
                                                np.asarray(ln1_b, f32)[0].reshape(8, 128).T], 1))
    shared = dict(
        w_ada=np.ascontiguousarray(np.asarray(w_ada, f32)[0]),
        b_adaT=np.ascontiguousarray(np.asarray(b_ada, f32)[0].reshape(48, 128).T),
        w_moba=w_moba, w_ret=w_ret, w_g=w_g,
        w_pa=np.ascontiguousarray(np.asarray(w_proj_moba, f32)[0]),
        w_pr=np.ascontiguousarray(np.asarray(w_proj_ret, f32)[0]),
        w_o=np.ascontiguousarray(np.asarray(w_out, f32)[0]),
        w_ffgu=w_ffgu, w_ffd=np.ascontiguousarray(np.asarray(w_ff_down, f32)[0]),
        convw=convwT, convb=convbT, lnbc=lnbc, ln1T=ln1T,
    )
    consts = [_const_tables(0), _const_tables(1)]
    in_maps = []
    for core in range(8):
        b, h = core // 2, core % 2
        if h == 1:
            xT = np.ascontiguousarray(x[b].T)
            xq_ = np.ascontiguousarray(x[b, 1920:4096])
        else:
            xT = np.concatenate([np.zeros((D, HALF), f32), x[b, :HALF].T], 1)
            xq_ = np.concatenate([np.zeros((128, D), f32), x[b, :HALF]], 0)
        m = dict(shared)
        m.update(consts[h])
        m["xT"] = np.ascontiguousarray(xT)
        m["xq"] = np.ascontiguousarray(xq_)
        m["cT"] = np.ascontiguousarray(c[b].reshape(8, 128).T)
        in_maps.append(m)
    if "nc" not in _NC_CACHE:
        _NC_CACHE["nc"] = build_program()
    nc = _NC_CACHE["nc"]
    res = run_bass_kernel_spmd(nc, in_maps, core_ids=list(range(8)))
    outp = np.empty((NB, SEQ, D), f32)
    for core in range(8):
        b, h = core // 2, core % 2
        outp[b, h * HALF:(h + 1) * HALF] = res.results[core]["out"]
    return outp
```

```python
from contextlib import ExitStack
import math
import numpy as np
import concourse.bass as bass
import concourse.mybir as mybir
from concourse.bass_utils import run_bass_kernel_spmd

F32 = mybir.dt.float32
BF16 = mybir.dt.bfloat16
AF = mybir.ActivationFunctionType
ALU = mybir.AluOpType
AX = mybir.AxisListType

D = 1024
SEQ = 4096
NB = 4
HALF = 2048
NQ = 2176
DFF = 2816
NFC = 22
ALPHA = 2.0 ** 0.25
LN_EPS = 1e-5
EPS_A = LN_EPS / (ALPHA * ALPHA)
BIG = 240000.0
NEGV = -30000.0
QG = [(0, 128, 1920)] + [(128 + 512 * i, 512, 2048 + 512 * i) for i in range(4)]

ENGS = ("pe", "act", "dve", "pool", "dma")
SAME_ENGINE_SYNC = {"pe": False, "act": True, "dve": True, "pool": True, "dma": True}
NDMASEM = {"dma": 32, "pool": 16}


class T:
    __slots__ = ("w", "rd")

    def __init__(self):
        self.w = None
        self.rd = []


class Op:
    __slots__ = ("eng", "fn", "deps", "sig", "sigval", "ndma", "sem", "prev", "phase")


class Sched:
    def __init__(self, nc, es):
        self.nc = nc
        self.sems = {e: es.enter_context(nc.semaphore("s_" + e)) for e in ENGS if e != "dma"}
        self.dsems = {e: [es.enter_context(nc.semaphore("d%s%d" % (e, i))) for i in range(NDMASEM[e])]
                      for e in NDMASEM}
        self.cnt = {e: 0 for e in ENGS}
        self.nd = {e: 0 for e in NDMASEM}
        self.tot = {e: [0] * NDMASEM[e] for e in NDMASEM}
        self.phase = 0
        self.ops = {e: [] for e in ENGS}
        self.nops = 0

    def op(self, eng, fn, reads=(), writes=(), ndma=1, dma=None):
        if dma is None:
            dma = eng == "dma"
        o = Op()
        o.eng = eng
        o.fn = fn
        o.ndma = ndma if dma else 0
        o.sig = False
        o.sigval = None
        o.sem = None
        o.prev = 0
        o.phase = self.phase
        deps = []
        for t in reads:
            if t.w is not None:
                deps.append(t.w)
        for t in writes:
            if t.w is not None:
                deps.append(t.w)
            deps.extend(t.rd)
        seen = set()
        o.deps = []
        for d in deps:
            if d is o or id(d) in seen or d.phase != self.phase:
                continue
            seen.add(id(d))
            if d.eng == eng and not SAME_ENGINE_SYNC[eng] and not d.ndma:
                continue
            o.deps.append(d)
        for t in reads:
            t.rd.append(o)
        for t in writes:
            t.w = o
            t.rd = []
        self.ops[eng].append(o)
        self.nops += 1
        return o

    def end_phase(self):
        nc = self.nc
        ops = self.ops
        for e in ENGS:
            for o in ops[e]:
                for d in o.deps:
                    d.sig = True
        bar = {}
        for e in ENGS:
            lastc = None
            for o in ops[e]:
                if not o.ndma:
                    lastc = o
            if lastc is not None:
                lastc.sig = True
            for o in ops[e]:
                if o.ndma:
                    s = self.nd[e] % NDMASEM[e]
                    self.nd[e] += 1
                    o.sem = self.dsems[e][s]
                    o.prev = self.tot[e][s]
                    self.tot[e][s] += 16 * o.ndma
                    o.sigval = self.tot[e][s]
                    bar[id(o.sem)] = (o.sem, o.sigval)
                else:
                    o.sem = self.sems[e]
                    if o.sig:
                        self.cnt[e] += 1
                        o.sigval = self.cnt[e]
            if lastc is not None:
                bar[id(lastc.sem)] = (lastc.sem, lastc.sigval)

        def run(e, eng):
            known = {}
            for o in ops[e]:
                need = {}
                for d in o.deps:
                    k = id(d.sem)
                    if known.get(k, 0) >= d.sigval:
                        continue
                    if k not in need or need[k][1] < d.sigval:
                        need[k] = (d.sem, d.sigval)
                if o.ndma and o.prev > 0:
                    k = id(o.sem)
                    if known.get(k, 0) < o.prev and (k not in need or need[k][1] < o.prev):
                        need[k] = (o.sem, o.prev)
                for k, (s, v) in need.items():
                    eng.wait_ge(s, v)
                    known[k] = v
                if o.ndma:
                    o.fn(eng, o.sem)
                else:
                    ins = o.fn(eng)
                    if o.sig:
                        ins.then_inc(o.sem, 1)
            for k, (s, v) in bar.items():
                if known.get(k, 0) < v:
                    eng.wait_ge(s, v)

        with nc.Block() as block:
            @block.tensor
            def _(eng):
                run("pe", eng)

            @block.scalar
            def _(eng):
                run("act", eng)

            @block.vector
            def _(eng):
                run("dve", eng)

            @block.gpsimd
            def _(eng):
                run("pool", eng)

            @block.sync
            def _(eng):
                run("dma", eng)

        self.ops = {e: [] for e in ENGS}
        self.phase += 1


def mm_group(S, out, pairs, reads, writes):
    pairs = list(pairs)

    def fn(e):
        n = len(pairs)
        ins = None
        for i, (l, r) in enumerate(pairs):
            ins = e.matmul(out, lhsT=l, rhs=r, start=(i == 0), stop=(i == n - 1))
        return ins
    return S.op("pe", fn, reads, writes)


def dma(S, out, in_, reads=(), writes=(), q="dma"):
    def fn(e, s):
        e.dma_start(out=out, in_=in_).then_inc(s, 16)
    return S.op(q, fn, reads, writes, ndma=1, dma=True)


def build_program():
    nc = bass.Bass("TRN2", target_bir_lowering=False)

    def din(name, shape, dt=F32):
        return nc.dram_tensor(name, list(shape), dt, kind="ExternalInput").ap()

    xT = din("xT", [D, SEQ])
    xq = din("xq", [NQ, D])
    cT = din("cT", [128, 8])
    w_ada = din("w_ada", [D, 6 * D])
    b_adaT = din("b_adaT", [128, 48])
    w_moba = din("w_moba", [4, D, 384])
    w_ret = din("w_ret", [4, D, 768])
    w_g = din("w_g", [D, 2048])
    w_pa = din("w_pa", [512, D])
    w_pr = din("w_pr", [D, D])
    w_o = din("w_o", [D, D])
    w_ffgu = din("w_ffgu", [NFC, D, 256])
    w_ffd = din("w_ffd", [DFF, D])
    convw = din("convw", [128, NFC * 3])
    convb = din("convb", [128, NFC])
    lnbc = din("lnbc", [4, 128, D])
    ln1T = din("ln1T", [128, 16])
    cosM = din("cosM", [128, SEQ])
    sinM = din("sinM", [128, SEQ])
    cosR = din("cosR", [128, SEQ])
    sinR = din("sinR", [128, SEQ])
    vb_d = din("vb", [128, 272])
    own_d = din("ownhot", [128, 272])
    ksp_d = din("ksp", [128, 128])
    epsq_d = din("epsq", [128, 4])
    dmask_d = din("dmask", [128, 512])
    hmask_d = din("hmask", [128, 256])
    tri_d = din("tri", [128, 128])
    onehot_d = din("onehot", [16, SEQ])
    ident_d = din("ident", [128, 128])
    flag_d = din("flag", [128, 1])
    out = nc.dram_tensor("out", [HALF, D], F32, kind="ExternalOutput").ap()
    scrM = nc.dram_tensor("scrM", [17, 128, 8, 128], BF16, kind="Internal").ap()
    scrH = nc.dram_tensor("scrH", [128, 8, NQ], BF16, kind="Internal").ap()

    gam = [1.0 - 2.0 ** (-5.0 - r) for r in range(4)]
    gC = [g ** 128 for g in gam]

    top = ExitStack()
    with top:
        S = Sched(nc, top)

        def sbuf(es, name, shape, dt=F32):
            return es.enter_context(nc.sbuf_tensor("sb_" + name, list(shape), dt))

        def psum(es, name, shape, dt=F32):
            return es.enter_context(nc.psum_tensor("ps_" + name, list(shape), dt))

        identF = sbuf(top, "identF", [128, 128])
        identB = sbuf(top, "identB", [128, 128], BF16)
        modT = sbuf(top, "modT", [128, 48])
        s1T = sbuf(top, "s1T", [128, 8])
        A2 = sbuf(top, "A2", [128, 8])
        B2 = sbuf(top, "B2", [128, 8])
        g1bc = sbuf(top, "g1bc", [128, D])
        g2bc = sbuf(top, "g2bc", [128, D])
        flag = sbuf(top, "flag", [128, 1])
        epsA = sbuf(top, "epsA", [128, 1])
        oneA = sbuf(top, "oneA", [128, 1])
        t_const = T()

        mix = ExitStack()
        mix.__enter__()
        h1T = sbuf(mix, "h1T", [128, 8, SEQ], BF16)
        AT = sbuf(mix, "AT", [128, 4, NQ], BF16)
        t_h1 = T()

        with ExitStack() as ph:
            cs = sbuf(ph, "cs", [128, 8])
            sc = sbuf(ph, "sc", [128, 8])
            bad = sbuf(ph, "bad", [128, 48])
            l1T = sbuf(ph, "l1T", [128, 16])
            onesF = sbuf(ph, "onesF", [128, 128])
            wstb = [sbuf(ph, "wstb%d" % i, [128, 8, 512]) for i in range(2)]
            modrow = sbuf(ph, "modrow", [1, 6 * D])
            one1 = sbuf(ph, "one1", [1, 1])
            prow = [psum(ph, "prow%d" % i, [128, 512]) for i in range(2)]
            t_prow = [T(), T()]
            t_modrow = T()
            xst = [sbuf(ph, "xst%d" % i, [128, 8, 512]) for i in range(2)]
            dg = [sbuf(ph, "dg%d" % i, [128, 128]) for i in range(2)]
            g12 = sbuf(ph, "g12", [128, 16])
            pmod = psum(ph, "pmod", [128, 512])
            pmodA = psum(ph, "pmodA", [128, 512])
            pbc = [psum(ph, "pbc%d" % i, [128, 1024]) for i in range(2)]
            t_cs, t_sc, t_bad, t_l1, t_ones, t_pmod, t_mod = T(), T(), T(), T(), T(), T(), T()
            t_wst = [T(), T()]
            t_xst = [T(), T()]
            t_dg = [T(), T()]
            t_pbc = [T(), T()]
            t_g12 = T()

            dma(S, identF[:], ident_d[:, :], writes=[t_const])
            dma(S, flag[:], flag_d[:, :], writes=[t_const])
            dma(S, cs[:], cT[:, :], writes=[t_cs])
            dma(S, bad[:], b_adaT[:, :], writes=[t_bad])
            dma(S, l1T[:], ln1T[:, :], writes=[t_l1])
            S.op("dve", lambda e: e.tensor_copy(out=identB[:], in_=identF[:]), reads=[t_const], writes=[t_const])
            S.op("dve", lambda e: e.memset(onesF[:], 1.0), writes=[t_ones])
            S.op("dve", lambda e: e.memset(one1[:], 1.0), writes=[t_ones])
            S.op("dve", lambda e: e.memset(epsA[:], EPS_A), writes=[t_const])
            S.op("dve", lambda e: e.memset(oneA[:], 1.0), writes=[t_const])
            S.op("act", lambda e: e.activation(out=sc[:], in_=cs[:], func=AF.Silu), reads=[t_cs], writes=[t_sc])
            w_ada_v = w_ada.rearrange("(k p) n -> p k n", p=128)
            xT_v = xT.rearrange("(k p) t -> p k t", p=128)

            def mod_block(blk):
                b = blk % 2
                dma(S, wstb[b][:], w_ada_v[:, :, blk * 512:(blk + 1) * 512], writes=[t_wst[b]])
                mm_group(S, prow[b][0:1, :], [(sc[:, k:k + 1], wstb[b][:, k, :]) for k in range(8)],
                         reads=[t_wst[b], t_sc], writes=[t_prow[b]])
                S.op("act", lambda e, b=b, blk=blk: e.copy(out=modrow[0:1, blk * 512:(blk + 1) * 512], in_=prow[b][0:1, :]),
                     reads=[t_prow[b]], writes=[t_modrow])

                def fn(e, blk=blk):
                    ins = None
                    pm = pmodA if blk < 4 else pmod
                    for j in range(4):
                        col = blk * 4 + j
                        ins = e.matmul(pm[:, col:col + 1], lhsT=modrow[0:1, col * 128:(col + 1) * 128],
                                       rhs=one1[0:1, 0:1], start=True, stop=True)
                    return ins
                S.op("pe", fn, reads=[t_modrow, t_ones], writes=[t_pmodp[0 if blk < 4 else 1]])
            t_pmodp = [T(), T(), T()]
            t_modA = T()
            for blk in range(4):
                mod_block(blk)
            S.op("dve", lambda e: e.tensor_tensor(out=modT[:, 0:16], in0=pmodA[:, 0:16], in1=bad[:, 0:16], op=ALU.add),
                 reads=[t_pmodp[0], t_bad], writes=[t_modA])
            S.op("dve", lambda e: e.tensor_scalar_add(out=s1T[:], in0=modT[:, 8:16], scalar1=1.0),
                 reads=[t_modA], writes=[t_modA])
            for g in range(8):
                b = g % 2
                dma(S, xst[b][:], xT_v[:, :, g * 512:(g + 1) * 512], writes=[t_xst[b]], q="pool")
                for k in range(8):
                    eng = "dve" if k % 2 == 0 else "pool"
                    S.op(eng, lambda e, b=b, k=k, g=g: e.tensor_scalar(
                        out=h1T[:, k, g * 512:(g + 1) * 512], in0=xst[b][:, k, :],
                        scalar1=s1T[:, k:k + 1], scalar2=modT[:, k:k + 1], op0=ALU.mult, op1=ALU.add),
                        reads=[t_xst[b], t_modA], writes=[t_h1])
            for blk in range(4, 12):
                mod_block(blk)
            S.op("dve", lambda e: e.tensor_tensor(out=modT[:, 16:48], in0=pmod[:, 16:48], in1=bad[:, 16:48], op=ALU.add),
                 reads=[t_pmodp[1], t_bad], writes=[t_mod])
            S.op("dve", lambda e: e.tensor_scalar_add(out=A2[:], in0=modT[:, 32:40], scalar1=1.0),
                 reads=[t_mod], writes=[t_mod])
            S.op("dve", lambda e: e.tensor_tensor(out=B2[:], in0=l1T[:, 8:16], in1=A2[:], op=ALU.mult),
                 reads=[t_mod, t_l1], writes=[t_mod])
            S.op("dve", lambda e: e.tensor_tensor(out=B2[:], in0=B2[:], in1=modT[:, 24:32], op=ALU.add),
                 reads=[t_mod], writes=[t_mod])
            S.op("dve", lambda e: e.tensor_tensor(out=A2[:], in0=A2[:], in1=l1T[:, 0:8], op=ALU.mult),
                 reads=[t_mod, t_l1], writes=[t_mod])
            S.op("dve", lambda e: e.tensor_scalar_mul(out=g12[:, 0:8], in0=modT[:, 16:24], scalar1=1.0 / ALPHA),
                 reads=[t_mod], writes=[t_g12])
            S.op("dve", lambda e: e.tensor_scalar_mul(out=g12[:, 8:16], in0=modT[:, 40:48], scalar1=1.0 / ALPHA),
                 reads=[t_g12], writes=[t_g12])
            for which, dst in ((0, g1bc), (1, g2bc)):
                for k in range(8):
                    b = k % 2
                    S.op("dve", lambda e, b=b, k=k, which=which: e.tensor_scalar_mul(
                        out=dg[b][:], in0=identF[:], scalar1=g12[:, which * 8 + k:which * 8 + k + 1]),
                        reads=[t_g12, t_const], writes=[t_dg[b]])
                    S.op("pe", lambda e, b=b, k=k, which=which: e.matmul(
                        pbc[which][:, k * 128:(k + 1) * 128], lhsT=onesF[:], rhs=dg[b][:], start=True, stop=True),
                        reads=[t_dg[b], t_ones], writes=[t_pbc[which]])
                S.op("act", lambda e, which=which, dst=dst: e.copy(out=dst[:], in_=pbc[which][:]),
                     reads=[t_pbc[which]], writes=[t_const])
            S.end_phase()

        with ExitStack() as ph:
            wm = [sbuf(ph, "wm%d" % i, [128, 8, 384], BF16) for i in range(2)]
            tb = [sbuf(ph, "tb%d" % i, [128, 2, 512]) for i in range(2)]
            KT = [sbuf(ph, "KT%d" % i, [80, SEQ], BF16) for i in range(2)]
            QT = [sbuf(ph, "QT%d" % i, [80, NQ], BF16) for i in range(2)]
            VA = [sbuf(ph, "VA%d" % i, [128, 32, 128], BF16) for i in range(2)]
            PT = [sbuf(ph, "PT%d" % i, [128, 512], BF16) for i in range(8)]
            tmp1 = [sbuf(ph, "tmpa%d" % i, [128, 512]) for i in range(2)]
            tmp2 = [sbuf(ph, "tmpb%d" % i, [128, 512]) for i in range(2)]
            vb = sbuf(ph, "vb", [128, 272])
            ownhot = sbuf(ph, "ownhot", [128, 272])
            gm = sbuf(ph, "gm", [128, 272])
            sel = sbuf(ph, "sel", [128, 272])
            top8 = sbuf(ph, "top8", [128, 17, 8])
            thr = sbuf(ph, "thr", [128, 17])
            mbt = sbuf(ph, "mbt", [128, 272], BF16)
            kmf = sbuf(ph, "kmf", [64, 16])
            kmb = sbuf(ph, "kmb", [64, 16], BF16)
            dmask = sbuf(ph, "dmask", [128, 512], BF16)
            hmask = sbuf(ph, "hmask", [128, 256], BF16)
            rden = sbuf(ph, "rden", [128, 256])
            pk = [psum(ph, "pk%d" % i, [128, 512]) for i in range(2)]
            pv = psum(ph, "pv", [128, 512])
            pss = [psum(ph, "pss%d" % i, [128, 512]) for i in range(3)]
            po = [psum(ph, "po%d" % i, [128, 512]) for i in range(2)]
            t_wm = [T(), T()]
            t_tb = [T(), T()]
            t_tbs = [T(), T()]
            t_KT = [[T() for _ in range(8)] for _ in range(2)]
            t_QT = [[T() for _ in range(5)] for _ in range(2)]
            t_QA = [[T() for _ in range(5)] for _ in range(2)]
            t_VA = [[T() for _ in range(8)] for _ in range(2)]
            t_PT = [T() for _ in range(8)]
            t_t1 = [T(), T()]
            t_t2 = [T(), T()]
            t_pk = [T(), T()]
            t_pv = T()
            t_pss = [T(), T(), T()]
            t_po = [T(), T()]
            t_msk, t_gm, t_sel, t_top8, t_thr, t_mbt, t_kmf, t_kmb, t_rden, t_AT = (T() for _ in range(10))

            dma(S, vb[:], vb_d[:, :], writes=[t_msk])
            dma(S, ownhot[:], own_d[:, :], writes=[t_msk])
            dma(S, dmask[:], dmask_d[:, :], writes=[t_msk], q="pool")
            dma(S, hmask[:], hmask_d[:, :], writes=[t_msk], q="pool")
            for e2 in range(2):
                dma(S, KT[e2][64:80, :], onehot_d[:, :], writes=[t_KT[e2][0]], q="pool")
            S.op("dve", lambda e: e.memset(VA[0][:, :, 64:128], 1.0), writes=[t_VA[0][0]])
            S.op("dve", lambda e: e.memset(VA[1][:, :, 0:64], 1.0), writes=[t_VA[1][0]])
            w_moba_v = w_moba.rearrange("h (k p) n -> h p k n", p=128)
            cnt = {"tb": 0, "pss": 0, "pt": 0, "po": 0}

            def rope_side(hp, wcol, tsl, n, dst, dsl, t_dst):
                w = wm[hp % 2]
                b = cnt["tb"] % 2
                cnt["tb"] += 1
                dma(S, tb[b][:, 0, 0:n], cosM[:, tsl], writes=[t_tb[b]])
                dma(S, tb[b][:, 1, 0:n], sinM[:, tsl], writes=[t_tbs[b]])
                mm_group(S, pk[b][:, 0:n],
                         [(w[:, k, wcol:wcol + 128], h1T[:, k, tsl]) for k in range(8)],
                         reads=[t_wm[hp % 2], t_h1], writes=[t_pk[b]])
                S.op("dve", lambda e, b=b, n=n: e.tensor_tensor(out=tmp1[b][:, 0:n], in0=pk[b][:, 0:n],
                                                                in1=tb[b][:, 0, 0:n], op=ALU.mult),
                     reads=[t_pk[b], t_tb[b]], writes=[t_t1[b]])
                for (o_lo, i_lo) in ((0, 32), (32, 0), (64, 96), (96, 64)):
                    S.op("dve", lambda e, b=b, n=n, o_lo=o_lo, i_lo=i_lo: e.tensor_tensor(
                        out=tmp2[b][o_lo:o_lo + 32, 0:n], in0=pk[b][i_lo:i_lo + 32, 0:n],
                        in1=tb[b][o_lo:o_lo + 32, 1, 0:n], op=ALU.mult),
                        reads=[t_pk[b], t_tbs[b]], writes=[t_t2[b]])
                for e2 in range(2):
                    S.op("pool", lambda e, b=b, n=n, e2=e2: e.tensor_tensor(
                        out=dst[e2][0:64, dsl], in0=tmp1[b][64 * e2:64 * e2 + 64, 0:n],
                        in1=tmp2[b][64 * e2:64 * e2 + 64, 0:n], op=ALU.add),
                        reads=[t_t1[b], t_t2[b]], writes=[t_dst[e2]])

            dma(S, wm[0][:], w_moba_v[0], writes=[t_wm[0]], q="pool")
            for hp in range(4):
                w = wm[hp % 2]
                if hp < 3:
                    dma(S, wm[(hp + 1) % 2][:], w_moba_v[hp + 1], writes=[t_wm[(hp + 1) % 2]], q="pool")
                for g in range(8):
                    rope_side(hp, 0, slice(g * 512, (g + 1) * 512), 512, KT, slice(g * 512, (g + 1) * 512),
                              [t_KT[0][g], t_KT[1][g]])

                    def fnv(e, g=g, w=w):
                        ins = None
                        for j in range(4):
                            t = g * 4 + j
                            for k in range(8):
                                ins = e.matmul(pv[:, j * 128:(j + 1) * 128], lhsT=h1T[:, k, t * 128:(t + 1) * 128],
                                               rhs=w[:, k, 256:384], start=(k == 0), stop=(k == 7))
                        return ins
                    S.op("pe", fnv, reads=[t_wm[hp % 2], t_h1], writes=[t_pv])
                    pv3 = pv[:].rearrange("p (j n) -> p j n", n=128)
                    S.op("act", lambda e, g=g, pv3=pv3: e.copy(out=VA[0][:, g * 4:(g + 1) * 4, 0:64], in_=pv3[:, :, 0:64]),
                         reads=[t_pv], writes=[t_VA[0][g]])
                    S.op("act", lambda e, g=g, pv3=pv3: e.copy(out=VA[1][:, g * 4:(g + 1) * 4, 64:128], in_=pv3[:, :, 64:128]),
                         reads=[t_pv], writes=[t_VA[1][g]])
                for gi, (q0, n, l0) in enumerate(QG):
                    rope_side(hp, 128, slice(l0, l0 + n), n, QT, slice(q0, q0 + n), [t_QT[0][gi], t_QT[1][gi]])
                for e2 in range(2):
                    S.op("dve", lambda e, e2=e2: e.tensor_reduce(
                        out=kmf[:], in_=KT[e2][0:64, :].rearrange("p (j n) -> p j n", n=256), axis=AX.X, op=ALU.add),
                        reads=t_KT[e2], writes=[t_kmf])
                    S.op("act", lambda e: e.mul(out=kmb[:], in_=kmf[:], mul=1.0 / 256.0), reads=[t_kmf], writes=[t_kmb])

                    def fng(e, e2=e2):
                        ins = None
                        for qt in range(17):
                            ins = e.matmul(pk[0][:, qt * 16:(qt + 1) * 16], lhsT=QT[e2][0:64, qt * 128:(qt + 1) * 128],
                                           rhs=kmb[:], start=True, stop=True)
                        return ins
                    S.op("pe", fng, reads=t_QT[e2] + [t_kmb], writes=[t_pk[0]])
                    S.op("dve", lambda e: e.tensor_tensor(out=gm[:], in0=pk[0][:, 0:272], in1=vb[:], op=ALU.add),
                         reads=[t_pk[0], t_msk], writes=[t_gm])

                    def fnmax(e):
                        ins = None
                        for qt in range(17):
                            ins = e.max(out=top8[:, qt, :], in_=gm[:, qt * 16:(qt + 1) * 16])
                        return ins
                    S.op("dve", fnmax, reads=[t_gm], writes=[t_top8])
                    S.op("dve", lambda e: e.tensor_scalar_max(out=thr[:], in0=top8[:, :, 2], scalar1=-10000.0),
                         reads=[t_top8], writes=[t_thr])
                    S.op("dve", lambda e: e.tensor_tensor(
                        out=sel[:].rearrange("p (t j) -> p t j", j=16), in0=gm[:].rearrange("p (t j) -> p t j", j=16),
                        in1=thr[:].unsqueeze(2).to_broadcast([128, 17, 16]), op=ALU.is_ge),
                        reads=[t_gm, t_thr], writes=[t_sel])
                    S.op("dve", lambda e: e.tensor_tensor(out=sel[:], in0=sel[:], in1=ownhot[:], op=ALU.max),
                         reads=[t_sel, t_msk], writes=[t_sel])
                    S.op("dve", lambda e: e.tensor_scalar(out=mbt[:], in0=sel[:], scalar1=-1.0, scalar2=BIG,
                                                          op0=ALU.add, op1=ALU.mult),
                         reads=[t_sel], writes=[t_mbt])
                    for gi, (q0, n, l0) in enumerate(QG):
                        def fnt(e, q0=q0, n=n):
                            ins = None
                            for j in range(n // 128):
                                qt = q0 // 128 + j
                                ins = e.matmul(pk[1][0:16, j * 128:(j + 1) * 128], lhsT=mbt[:, qt * 16:(qt + 1) * 16],
                                               rhs=identB[:], start=True, stop=True)
                            return ins
                        S.op("pe", fnt, reads=[t_mbt, t_const], writes=[t_pk[1]])
                        S.op("act", lambda e, e2=e2, q0=q0, n=n: e.copy(out=QT[e2][64:80, q0:q0 + n], in_=pk[1][0:16, 0:n]),
                             reads=[t_pk[1]], writes=[t_QA[e2][gi]])
                units = []
                for e2 in range(2):
                    for qb in range(9):
                        if qb == 0:
                            q0, nq, lb, gi = 0, 128, 7, 0
                        else:
                            q0, nq, lb, gi = 128 + 256 * (qb - 1), 256, 7 + qb, 1 + (qb - 1) // 2
                        npair = lb + 1
                        pb = cnt["po"] % 2
                        cnt["po"] += 1
                        for pi in range(npair):
                            units.append((e2, qb, q0, nq, gi, npair, pb, pi))
                SKEW = 4
                bufs = {}
                for idx in range(len(units) + SKEW):
                    if idx < len(units):
                        e2, qb, q0, nq, gi, npair, pb, pi = units[idx]
                        sb_ = cnt["pss"] % 3
                        cnt["pss"] += 1
                        ptb = cnt["pt"] % 8
                        cnt["pt"] += 1
                        bufs[idx] = ptb

                        def fqk(e, e2=e2, pi=pi, q0=q0, nq=nq, sb_=sb_):
                            ins = None
                            for j in range(2):
                                kt = 2 * pi + j
                                ins = e.matmul(pss[sb_][:, j * nq:(j + 1) * nq], lhsT=KT[e2][0:80, kt * 128:(kt + 1) * 128],
                                               rhs=QT[e2][0:80, q0:q0 + nq], start=True, stop=True)
                            return ins
                        S.op("pe", fqk, reads=[t_KT[e2][pi // 2], t_KT[e2][0], t_QT[e2][gi], t_QA[e2][gi]],
                             writes=[t_pss[sb_]])
                        S.op("act", lambda e, sb_=sb_, ptb=ptb, nq=nq: e.activation(
                            out=PT[ptb][:, 0:2 * nq], in_=pss[sb_][:, 0:2 * nq], func=AF.Exp, scale=0.125),
                            reads=[t_pss[sb_]], writes=[t_PT[ptb]])
                        if pi == npair - 1:
                            msk = hmask if qb == 0 else dmask
                            S.op("pool", lambda e, ptb=ptb, nq=nq, msk=msk: e.tensor_tensor(
                                out=PT[ptb][:, 0:2 * nq], in0=PT[ptb][:, 0:2 * nq], in1=msk[:, 0:2 * nq], op=ALU.mult),
                                reads=[t_PT[ptb], t_msk], writes=[t_PT[ptb]])
                    if idx - SKEW >= 0:
                        e2, qb, q0, nq, gi, npair, pb, pi = units[idx - SKEW]
                        ptb = bufs.pop(idx - SKEW)

                        def fpv(e, e2=e2, pi=pi, nq=nq, ptb=ptb, pb=pb, npair=npair):
                            ins = None
                            for j in range(2):
                                kt = 2 * pi + j
                                ins = e.matmul(po[pb][:, 0:nq], lhsT=VA[e2][:, kt, :], rhs=PT[ptb][:, j * nq:(j + 1) * nq],
                                               start=(pi == 0 and j == 0), stop=(pi == npair - 1 and j == 1))
                            return ins
                        S.op("pe", fpv, reads=[t_VA[e2][pi // 2], t_VA[e2][0], t_PT[ptb]], writes=[t_po[pb]])
                        if pi == npair - 1:
                            nlo, dlo = (0, 64) if e2 == 0 else (64, 0)
                            S.op("dve", lambda e, pb=pb, nq=nq, nlo=nlo, dlo=dlo: e.reciprocal(
                                out=rden[nlo:nlo + 64, 0:nq], in_=po[pb][dlo:dlo + 64, 0:nq]),
                                reads=[t_po[pb]], writes=[t_rden])
                            S.op("dve", lambda e, pb=pb, nq=nq, nlo=nlo, hp=hp, q0=q0: e.tensor_tensor(
                                out=AT[nlo:nlo + 64, hp, q0:q0 + nq], in0=po[pb][nlo:nlo + 64, 0:nq],
                                in1=rden[nlo:nlo + 64, 0:nq], op=ALU.mult),
                                reads=[t_po[pb], t_rden], writes=[t_AT])
            S.end_phase()

        rts = ExitStack()
        rts.__enter__()
        RT = sbuf(rts, "RT", [128, 8, NQ], BF16)
        with ExitStack() as ph:
            wkq = sbuf(ph, "wkq", [128, 8, 256], BF16)
            wvg2 = [sbuf(ph, "wvg%d" % i, [128, 8, 512], BF16) for i in range(2)]
            tb = [sbuf(ph, "rtb%d" % i, [128, 2, 512]) for i in range(2)]
            tmp1 = [sbuf(ph, "rtmpa%d" % i, [128, 512]) for i in range(2)]
            tmp2 = [sbuf(ph, "rtmpb%d" % i, [128, 512]) for i in range(2)]
            ktg = [sbuf(ph, "ktg%d" % i, [128, 512], BF16) for i in range(2)]
            KTr = sbuf(ph, "KTr", [128, 17 * 128], BF16)
            Ktok = sbuf(ph, "Ktok", [128, 17, 128], BF16)
            Ktp = [sbuf(ph, "Ktp%d" % i, [128, 4, 128], BF16) for i in range(2)]
            Vr = sbuf(ph, "Vr", [128, 17, 256], BF16)
            Vp = [sbuf(ph, "Vp%d" % i, [128, 4, 256], BF16) for i in range(2)]
            QTr = sbuf(ph, "QTr", [128, NQ], BF16)
            SG = [sbuf(ph, "SG%d" % i, [128, 256], BF16) for i in range(4)]
            ex = [sbuf(ph, "ex%d" % i, [128, 256]) for i in range(2)]
            t_ex = [T(), T()]
            Sst = sbuf(ph, "Sst", [128, 256])
            S16 = [sbuf(ph, "S16_%d" % i, [128, 256], BF16) for i in range(2)]
            SD = [sbuf(ph, "SD%d" % i, [128, 128], BF16) for i in range(2)]
            yn = [sbuf(ph, "yn%d" % i, [128, 256], BF16) for i in range(2)]
            Rtok = [sbuf(ph, "Rtok%d" % i, [128, 256], BF16) for i in range(2)]
            st6 = [sbuf(ph, "st6_%d" % i, [128, 6]) for i in range(2)]
            mvr = [sbuf(ph, "mvr%d" % i, [128, 2]) for i in range(2)]
            rstdr = [sbuf(ph, "rstdr%d" % i, [128, 1]) for i in range(2)]
            ksp = sbuf(ph, "ksp", [128, 128])
            epsq = sbuf(ph, "epsq", [128, 4])
            tri = sbuf(ph, "tri", [128, 128])
            pk = [psum(ph, "rpk%d" % i, [128, 512]) for i in range(2)]
            pv = [psum(ph, "rpv%d" % i, [128, 512]) for i in range(2)]
            ptr = psum(ph, "ptr", [128, 1024], BF16)
            pS = psum(ph, "pS", [128, 512])
            py = [psum(ph, "py%d" % i, [128, 512]) for i in range(2)]
            t_wkq = T()
            t_wvg2 = [T(), T()]
            t_tb = [T(), T()]
            t_tbs = [T(), T()]
            t_t1 = [T(), T()]
            t_t2 = [T(), T()]
            t_ktg = [T(), T()]
            t_KTr, t_Ktok, t_Vr, t_QTr, t_S, t_cst = (T() for _ in range(6))
            t_SG = [T(), T(), T(), T()]
            t_psc = [T(), T()]
            t_stt = [T(), T()]
            t_ptr2 = [T(), T()]
            t_Ktp = [T(), T()]
            t_Vp = [T(), T()]
            t_S16 = [T(), T()]
            t_SD = [T(), T()]
            t_yn = [T(), T()]
            t_Rtok = [T(), T()]
            t_st, t_RT = T(), T()
            t_pk = [T(), T()]
            t_pv = [T(), T()]
            t_ptr, t_pS = T(), T()
            t_py = [T(), T()]
            dma(S, ksp[:], ksp_d[:, :], writes=[t_cst])
            dma(S, epsq[:], epsq_d[:, :], writes=[t_cst])
            dma(S, tri[:], tri_d[:, :], writes=[t_cst])
            w_ret_v = w_ret.rearrange("h (k p) n -> h p k n", p=128)
            cnt = {"tb": 0, "pv": 0, "py": 0, "s16": 0, "sd": 0}

            def rrope(wcol, tsl, n, dst_ap, t_dst):
                b = cnt["tb"] % 2
                cnt["tb"] += 1
                dma(S, tb[b][:, 0, 0:n], cosR[:, tsl], writes=[t_tb[b]])
                dma(S, tb[b][:, 1, 0:n], sinR[:, tsl], writes=[t_tbs[b]])
                mm_group(S, pk[b][:, 0:n],
                         [(wkq[:, k, wcol:wcol + 128], h1T[:, k, tsl]) for k in range(8)],
                         reads=[t_wkq, t_h1], writes=[t_pk[b]])
                S.op("dve", lambda e, b=b, n=n: e.tensor_tensor(out=tmp1[b][:, 0:n], in0=pk[b][:, 0:n],
                                                                in1=tb[b][:, 0, 0:n], op=ALU.mult),
                     reads=[t_pk[b], t_tb[b]], writes=[t_t1[b]])
                S.op("dve", lambda e, b=b, n=n: e.tensor_tensor(out=tmp2[b][0:64, 0:n], in0=pk[b][64:128, 0:n],
                                                                in1=tb[b][0:64, 1, 0:n], op=ALU.mult),
                     reads=[t_pk[b], t_tbs[b]], writes=[t_t2[b]])
                S.op("dve", lambda e, b=b, n=n: e.tensor_tensor(out=tmp2[b][64:128, 0:n], in0=pk[b][0:64, 0:n],
                                                                in1=tb[b][64:128, 1, 0:n], op=ALU.mult),
                     reads=[t_pk[b], t_tbs[b]], writes=[t_t2[b]])
                S.op("pool", lambda e, b=b, n=n: e.tensor_tensor(out=dst_ap, in0=tmp1[b][:, 0:n], in1=tmp2[b][:, 0:n],
                                                                 op=ALU.add),
                     reads=[t_t1[b], t_t2[b]], writes=[t_dst])

            dma(S, wkq[:], w_ret_v[0][:, :, 0:256], writes=[t_wkq], q="pool")
            dma(S, wvg2[0][:], w_ret_v[0][:, :, 256:768], writes=[t_wvg2[0]], q="pool")
            for r in range(4):
                wvg = wvg2[r % 2]
                t_wvg = t_wvg2[r % 2]
                if r < 3:
                    dma(S, wvg2[(r + 1) % 2][:], w_ret_v[r + 1][:, :, 256:768], writes=[t_wvg2[(r + 1) % 2]], q="pool")
                for g in range(8):
                    tsl = slice(g * 512, (g + 1) * 512)
                    kb = g % 2
                    if g < 4:
                        rrope(0, tsl, 512, ktg[kb][:, :], t_ktg[kb])
                    else:
                        c0 = 4 * g - 15
                        rrope(0, tsl, 512, KTr[:, c0 * 128:(c0 + 4) * 128], t_KTr)
                    if g == 3:
                        S.op("pool", lambda e, kb=kb: e.tensor_copy(out=KTr[:, 0:128], in_=ktg[kb][:, 384:512]),
                             reads=[t_ktg[kb]], writes=[t_KTr])
                    for half in range(2):
                        vb_ = cnt["pv"] % 2
                        cnt["pv"] += 1

                        def fv(e, g=g, half=half, vb_=vb_, wvg=wvg):
                            ins = None
                            for j in range(2):
                                t = g * 4 + half * 2 + j
                                for k in range(8):
                                    ins = e.matmul(pv[vb_][:, j * 256:(j + 1) * 256], lhsT=h1T[:, k, t * 128:(t + 1) * 128],
                                                   rhs=wvg[:, k, 0:256], start=(k == 0), stop=(k == 7))
                            return ins
                        S.op("pe", fv, reads=[t_wvg, t_h1], writes=[t_pv[vb_]])
                        for j in range(2):
                            n_ = g * 4 + half * 2 + j
                            col = r * 32 + n_
                            if n_ <= 14:
                                dstap, tdst = Vp[kb][:, half * 2 + j, :], t_Vp[kb]
                            else:
                                dstap, tdst = Vr[:, n_ - 15, :], t_Vr
                            S.op("act", lambda e, vb_=vb_, j=j, col=col, dstap=dstap: e.activation(
                                out=dstap, in_=pv[vb_][:, j * 256:(j + 1) * 256], func=AF.Copy, scale=ksp[:, col:col + 1]),
                                reads=[t_pv[vb_], t_cst], writes=[tdst])
                    src = ktg[kb] if g < 4 else None

                    def ftr(e, g=g, kb=kb):
                        ins = None
                        for j in range(4):
                            if g < 4:
                                in_ap = ktg[kb][:, j * 128:(j + 1) * 128]
                            else:
                                c = 4 * g - 15 + j
                                in_ap = KTr[:, c * 128:(c + 1) * 128]
                            ins = e.transpose(ptr[:, j * 128:(j + 1) * 128], in_ap, identB[:])
                        return ins
                    S.op("pe", ftr, reads=[t_ktg[kb] if g < 4 else t_KTr, t_const], writes=[t_ptr])
                    ptr3 = ptr[:, 0:512].rearrange("p (j n) -> p j n", n=128)
                    if g < 3:
                        S.op("act", lambda e, kb=kb, ptr3=ptr3, r=r: e.mul(out=Ktp[kb][:], in_=ptr3, mul=gC[r]),
                             reads=[t_ptr], writes=[t_Ktp[kb]])
                    elif g == 3:
                        S.op("act", lambda e, kb=kb, ptr3=ptr3, r=r: e.mul(out=Ktp[kb][:, 0:3, :], in_=ptr3[:, 0:3, :], mul=gC[r]),
                             reads=[t_ptr], writes=[t_Ktp[kb]])
                        S.op("act", lambda e, ptr3=ptr3, r=r: e.mul(out=Ktok[:, 0, :], in_=ptr3[:, 3, :], mul=gC[r]),
                             reads=[t_ptr], writes=[t_Ktok])
                    else:
                        c0 = 4 * g - 15
                        S.op("act", lambda e, c0=c0, ptr3=ptr3, r=r: e.mul(out=Ktok[:, c0:c0 + 4, :], in_=ptr3, mul=gC[r]),
                             reads=[t_ptr], writes=[t_Ktok])
                    if g < 4:
                        nn = 4 if g < 3 else 3

                        def fs(e, g=g, kb=kb, nn=nn):
                            ins = None
                            for j in range(nn):
                                n_ = g * 4 + j
                                ins = e.matmul(pS[:, 0:256], lhsT=Ktp[kb][:, j, :], rhs=Vp[kb][:, j, :],
                                               start=(n_ == 0), stop=(n_ == 14))
                            return ins
                        S.op("pe", fs, reads=[t_Ktp[kb], t_Vp[kb]], writes=[t_pS])
                S.op("dve", lambda e: e.tensor_copy(out=Sst[:], in_=pS[:, 0:256]), reads=[t_pS], writes=[t_S])
                sbi = cnt["s16"] % 2
                cnt["s16"] += 1
                S.op("act", lambda e, sbi=sbi: e.copy(out=S16[sbi][:], in_=pS[:, 0:256]), reads=[t_pS], writes=[t_S16[sbi]])
                for gi, (q0, n, l0) in enumerate(QG):
                    rrope(128, slice(l0, l0 + n), n, QTr[:, q0:q0 + n], t_QTr)
                if r < 3:
                    dma(S, wkq[:], w_ret_v[r + 1][:, :, 0:256], writes=[t_wkq], q="pool")
                s16_of = {}

                def stageA(qc, wvg=wvg, t_wvg=t_wvg):
                    csl = slice(qc * 128, (qc + 1) * 128)
                    l0 = 1920 + qc * 128
                    vb_ = cnt["pv"] % 2
                    cnt["pv"] += 1
                    g3 = qc % 4
                    x2 = qc % 2
                    p2 = qc % 2
                    mm_group(S, pv[vb_][:, 0:256], [(h1T[:, k, l0:l0 + 128], wvg[:, k, 256:512]) for k in range(8)],
                             reads=[t_wvg, t_h1], writes=[t_pv[vb_]])
                    S.op("act", lambda e, vb_=vb_, x2=x2: e.activation(out=ex[x2][:], in_=pv[vb_][:, 0:256], func=AF.Exp, scale=-1.0),
                         reads=[t_pv[vb_]], writes=[t_ex[x2]])
                    S.op("act", lambda e, x2=x2: e.activation(out=ex[x2][:], in_=ex[x2][:], func=AF.Ln, bias=oneA[:], scale=1.0),
                         reads=[t_ex[x2], t_const], writes=[t_ex[x2]])
                    S.op("act", lambda e, x2=x2: e.activation(out=ex[x2][:], in_=ex[x2][:], func=AF.Exp, scale=-1.0),
                         reads=[t_ex[x2]], writes=[t_ex[x2]])
                    S.op("dve", lambda e, vb_=vb_, g3=g3, x2=x2: e.tensor_tensor(out=SG[g3][:], in0=pv[vb_][:, 0:256], in1=ex[x2][:], op=ALU.mult),
                         reads=[t_pv[vb_], t_ex[x2]], writes=[t_SG[g3]])
                    S.op("pe", lambda e, csl=csl, p2=p2: e.matmul(pk[p2][:, 0:128], lhsT=KTr[:, csl], rhs=QTr[:, csl],
                                                               start=True, stop=True),
                         reads=[t_KTr, t_QTr], writes=[t_pk[p2]])
                    S.op("dve", lambda e, p2=p2: e.tensor_tensor(out=SD[p2][:], in0=pk[p2][:, 0:128], in1=tri[:], op=ALU.mult),
                         reads=[t_pk[p2], t_cst], writes=[t_SD[p2]])

                def stageB(qc, r=r):
                    csl = slice(qc * 128, (qc + 1) * 128)
                    p2 = qc % 2
                    sbi = s16_of[qc]

                    def fy(e, qc=qc, csl=csl, p2=p2, sbi=sbi):
                        e.matmul(py[p2][:, 0:256], lhsT=SD[p2][:], rhs=Vr[:, qc, :], start=True, stop=False)
                        return e.matmul(py[p2][:, 0:256], lhsT=QTr[:, csl], rhs=S16[sbi][:], start=False, stop=True)
                    S.op("pe", fy, reads=[t_SD[p2], t_Vr, t_QTr, t_S16[sbi]], writes=[t_py[p2]])
                    if qc < 16:
                        S.op("pe", lambda e, qc=qc: e.matmul(pS[:, 0:256], lhsT=Ktok[:, qc, :], rhs=Vr[:, qc, :], start=True, stop=True),
                             reads=[t_Ktok, t_Vr], writes=[t_pS])
                        S.op("dve", lambda e, r=r: e.scalar_tensor_tensor(out=Sst[:], in0=Sst[:], scalar=gC[r], in1=pS[:, 0:256],
                                                                          op0=ALU.mult, op1=ALU.add),
                             reads=[t_pS, t_S], writes=[t_S])
                        nsb = 1 - sbi
                        s16_of[qc + 1] = nsb
                        S.op("act", lambda e, nsb=nsb: e.copy(out=S16[nsb][:], in_=Sst[:]), reads=[t_S], writes=[t_S16[nsb]])

                def stageC1(qc, r=r):
                    p2 = qc % 2
                    S.op("dve", lambda e, p2=p2: e.bn_stats(out=st6[p2][:], in_=py[p2][:, 0:256]), reads=[t_py[p2]], writes=[t_stt[p2]])
                    S.op("dve", lambda e, p2=p2: e.bn_aggr(out=mvr[p2][:], in_=st6[p2][:]), reads=[t_stt[p2]], writes=[t_stt[p2]])
                    S.op("act", lambda e, r=r, p2=p2: e.activation(out=rstdr[p2][:], in_=mvr[p2][:, 1:2], func=AF.Ln,
                                                                   bias=epsq[:, r:r + 1], scale=1.0),
                         reads=[t_stt[p2], t_cst], writes=[t_stt[p2]])
                    S.op("act", lambda e, p2=p2: e.activation(out=rstdr[p2][:], in_=rstdr[p2][:], func=AF.Exp, scale=-0.5),
                         reads=[t_stt[p2]], writes=[t_stt[p2]])
                    S.op("dve", lambda e, p2=p2: e.tensor_scalar(out=yn[p2][:], in0=py[p2][:, 0:256], scalar1=mvr[p2][:, 0:1],
                                                                 scalar2=rstdr[p2][:], op0=ALU.subtract, op1=ALU.mult),
                         reads=[t_py[p2], t_stt[p2]], writes=[t_yn[p2]])

                def stageC2(qc, r=r):
                    csl = slice(qc * 128, (qc + 1) * 128)
                    p2 = qc % 2
                    g3 = qc % 4
                    S.op("pool", lambda e, p2=p2, g3=g3: e.tensor_tensor(out=Rtok[p2][:], in0=yn[p2][:], in1=SG[g3][:], op=ALU.mult),
                         reads=[t_yn[p2], t_SG[g3]], writes=[t_Rtok[p2]])
                    o0 = 512

                    def ftr2(e, p2=p2, o0=o0):
                        e.transpose(ptr[:, o0:o0 + 128], Rtok[p2][:, 0:128], identB[:])
                        return e.transpose(ptr[:, o0 + 128:o0 + 256], Rtok[p2][:, 128:256], identB[:])
                    S.op("pe", ftr2, reads=[t_Rtok[p2], t_const], writes=[t_ptr])
                    S.op("act", lambda e, r=r, csl=csl, o0=o0: e.copy(
                        out=RT[:, 2 * r:2 * r + 2, csl], in_=ptr[:, o0:o0 + 256].rearrange("p (j n) -> p j n", n=128)),
                        reads=[t_ptr], writes=[t_RT])

                s16_of[0] = sbi
                for st in range(17 + 3):
                    if st < 17:
                        stageA(st)
                    if 0 <= st - 1 < 17:
                        stageB(st - 1)
                    if 0 <= st - 2 < 17:
                        stageC1(st - 2)
                    if 0 <= st - 3 < 17:
                        stageC2(st - 3)
            S.end_phase()

        with ExitStack() as ph:
            wg = [sbuf(ph, "wg%d" % i, [128, 8, 256], BF16) for i in range(2)]
            wpa = [sbuf(ph, "wpa%d" % i, [128, 4, 128], BF16) for i in range(2)]
            wpr = [sbuf(ph, "wpr%d" % i, [128, 8, 128], BF16) for i in range(2)]
            sa = [sbuf(ph, "sa%d" % i, [128, 512]) for i in range(2)]
            sr = [sbuf(ph, "sr%d" % i, [128, 512]) for i in range(2)]
            m1 = [sbuf(ph, "m1%d" % i, [128, 512]) for i in range(2)]
            m2 = [sbuf(ph, "m2%d" % i, [128, 512]) for i in range(2)]
            mT = [sbuf(ph, "mT%d" % i, [128, 512], BF16) for i in range(2)]
            pg = [[psum(ph, "pg%d_%d" % (i, j), [128, 512]) for j in range(4)] for i in range(2)]
            t_w = [T(), T()]
            t_pg = [[T() for _ in range(4)] for _ in range(2)]
            t_sa, t_sr, t_m1, t_m2, t_mT = ([T(), T()] for _ in range(5))
            t_AT2, t_RT2 = T(), T()
            w_g_v = w_g.rearrange("(k p) n -> p k n", p=128)
            w_pa_v = w_pa.rearrange("(k p) n -> p k n", p=128)
            w_pr_v = w_pr.rearrange("(k p) n -> p k n", p=128)
            it = 0

            def load_gw(f):
                wb = f % 2
                fs_ = slice(f * 128, (f + 1) * 128)
                dma(S, wg[wb][:, :, 0:128], w_g_v[:, :, f * 128:(f + 1) * 128], writes=[t_w[wb]], q="pool")
                dma(S, wg[wb][:, :, 128:256], w_g_v[:, :, 1024 + f * 128:1024 + (f + 1) * 128], writes=[t_w[wb]], q="pool")
                dma(S, wpa[wb][:], w_pa_v[:, :, fs_], writes=[t_w[wb]], q="pool")
                dma(S, wpr[wb][:], w_pr_v[:, :, fs_], writes=[t_w[wb]], q="pool")
            load_gw(0)
            for f in range(8):
                wb = f % 2
                if f < 7:
                    load_gw(f + 1)
                for gi, (q0, n, l0) in enumerate(QG):
                    b = it % 2
                    it += 1
                    mm_group(S, pg[b][0][:, 0:n], [(wg[wb][:, k, 0:128], h1T[:, k, l0:l0 + n]) for k in range(8)],
                             reads=[t_w[wb], t_h1], writes=[t_pg[b][0]])
                    mm_group(S, pg[b][1][:, 0:n], [(wg[wb][:, k, 128:256], h1T[:, k, l0:l0 + n]) for k in range(8)],
                             reads=[t_w[wb], t_h1], writes=[t_pg[b][1]])
                    mm_group(S, pg[b][2][:, 0:n], [(wpa[wb][:, k, :], AT[:, k, q0:q0 + n]) for k in range(4)],
                             reads=[t_w[wb], t_AT2], writes=[t_pg[b][2]])
                    mm_group(S, pg[b][3][:, 0:n], [(wpr[wb][:, k, :], RT[:, k, q0:q0 + n]) for k in range(8)],
                             reads=[t_w[wb], t_RT2], writes=[t_pg[b][3]])
                    S.op("act", lambda e, b=b, n=n: e.activation(out=sa[b][:, 0:n], in_=pg[b][0][:, 0:n], func=AF.Sigmoid),
                         reads=[t_pg[b][0]], writes=[t_sa[b]])
                    S.op("act", lambda e, b=b, n=n: e.activation(out=sr[b][:, 0:n], in_=pg[b][1][:, 0:n], func=AF.Sigmoid),
                         reads=[t_pg[b][1]], writes=[t_sr[b]])
                    S.op("dve", lambda e, b=b, n=n: e.tensor_tensor(out=m1[b][:, 0:n], in0=pg[b][2][:, 0:n], in1=sa[b][:, 0:n], op=ALU.mult),
                         reads=[t_pg[b][2], t_sa[b]], writes=[t_m1[b]])
                    S.op("dve", lambda e, b=b, n=n: e.tensor_tensor(out=m2[b][:, 0:n], in0=pg[b][3][:, 0:n], in1=sr[b][:, 0:n], op=ALU.mult),
                         reads=[t_pg[b][3], t_sr[b]], writes=[t_m2[b]])
                    S.op("pool", lambda e, b=b, n=n: e.tensor_tensor(out=mT[b][:, 0:n], in0=m1[b][:, 0:n], in1=m2[b][:, 0:n], op=ALU.add),
                         reads=[t_m1[b], t_m2[b]], writes=[t_mT[b]])
                    dma(S, scrM[q0 // 128:(q0 + n) // 128, :, f, :].rearrange("t p c -> p t c"),
                        mT[b][:, 0:n].rearrange("p (t c) -> p t c", c=128), reads=[t_mT[b]])
            S.end_phase()
        rts.close()
        mix.close()

        with ExitStack() as ph:
            NBF = 3
            wo = sbuf(ph, "wo", [128, 8, D], BF16)
            lg = sbuf(ph, "lg", [128, D])
            lb_ = sbuf(ph, "lb", [128, D])
            mTt = [sbuf(ph, "mTt%d" % i, [128, 8, 128], BF16) for i in range(NBF)]
            xt = [sbuf(ph, "xt%d" % i, [128, D]) for i in range(NBF)]
            z = [sbuf(ph, "z%d" % i, [128, D]) for i in range(NBF)]
            zn = [sbuf(ph, "zn%d" % i, [128, D]) for i in range(NBF)]
            x1 = [sbuf(ph, "x1%d" % i, [128, D]) for i in range(NBF)]
            h2t = [sbuf(ph, "h2t%d" % i, [128, 8, 128], BF16) for i in range(NBF)]
            st12 = [sbuf(ph, "st12_%d" % i, [128, 2, 6]) for i in range(NBF)]
            mv = [sbuf(ph, "mv1_%d" % i, [128, 2]) for i in range(NBF)]
            rstd = [sbuf(ph, "rstd1_%d" % i, [128, 1]) for i in range(NBF)]
            py = [psum(ph, "lpy%d" % i, [128, 1024]) for i in range(2)]
            pz = [psum(ph, "lpz%d" % i, [128, 1024]) for i in range(2)]
            t_wo, t_lc = T(), T()
            t_mTt, t_xt, t_z, t_zn, t_x1, t_h2t, t_st = ([T() for _ in range(NBF)] for _ in range(7))
            t_py, t_pz = [T(), T()], [T(), T()]
            dma(S, wo[:], w_o.rearrange("(k p) n -> p k n", p=128), writes=[t_wo], q="pool")
            dma(S, lg[:], lnbc[0], writes=[t_lc])
            dma(S, lb_[:], lnbc[1], writes=[t_lc])

            def l1_A(qt):
                b = qt % NBF
                pb = qt % 2
                dma(S, mTt[b][:], scrM[qt], writes=[t_mTt[b]])
                dma(S, xt[b][:], xq[qt * 128:(qt + 1) * 128, :], writes=[t_xt[b]])
                for half in range(2):
                    mm_group(S, py[pb][:, half * 512:(half + 1) * 512],
                             [(mTt[b][:, k, :], wo[:, k, half * 512:(half + 1) * 512]) for k in range(8)],
                             reads=[t_mTt[b], t_wo], writes=[t_py[pb]])
                S.op("dve", lambda e: e.tensor_tensor(out=z[b][:], in0=py[pb][:], in1=g1bc[:], op=ALU.mult),
                     reads=[t_py[pb], t_const], writes=[t_z[b]])
                S.op("pool", lambda e: e.tensor_tensor(out=z[b][:], in0=z[b][:], in1=xt[b][:], op=ALU.add),
                     reads=[t_z[b], t_xt[b]], writes=[t_z[b]])
                for half in range(2):
                    S.op("dve", lambda e, half=half: e.bn_stats(out=st12[b][:, half, :], in_=z[b][:, half * 512:(half + 1) * 512]),
                         reads=[t_z[b]], writes=[t_st[b]])
                S.op("dve", lambda e: e.bn_aggr(out=mv[b][:], in_=st12[b][:]), reads=[t_st[b]], writes=[t_st[b]])
                S.op("act", lambda e: e.activation(out=rstd[b][:], in_=mv[b][:, 1:2], func=AF.Sqrt, bias=epsA[:], scale=1.0),
                     reads=[t_st[b], t_const], writes=[t_st[b]])
                S.op("dve", lambda e: e.reciprocal(out=rstd[b][:], in_=rstd[b][:]), reads=[t_st[b]], writes=[t_st[b]])
                S.op("dve", lambda e: e.tensor_scalar(out=zn[b][:], in0=z[b][:], scalar1=mv[b][:, 0:1], scalar2=rstd[b][:],
                                                      op0=ALU.subtract, op1=ALU.mult),
                     reads=[t_z[b], t_st[b]], writes=[t_zn[b]])
                if qt >= 1:
                    S.op("pool", lambda e: e.tensor_tensor(out=x1[b][:], in0=zn[b][:], in1=lg[:], op=ALU.mult),
                         reads=[t_zn[b], t_lc], writes=[t_x1[b]])
                    S.op("dve", lambda e: e.tensor_tensor(out=x1[b][:], in0=x1[b][:], in1=lb_[:], op=ALU.add),
                         reads=[t_x1[b], t_lc], writes=[t_x1[b]])
                    dma(S, out[(qt - 1) * 128:qt * 128, :], x1[b][:], reads=[t_x1[b]], q="pool")

            def l1_B(qt):
                b = qt % NBF
                pb = qt % 2

                def ftz(e):
                    ins = None
                    for k in range(8):
                        ins = e.transpose(pz[pb][:, k * 128:(k + 1) * 128], zn[b][:, k * 128:(k + 1) * 128], identF[:])
                    return ins
                S.op("pe", ftz, reads=[t_zn[b], t_const], writes=[t_pz[pb]])

                def fh2(e):
                    ins = None
                    for k in range(8):
                        ins = e.activation(out=h2t[b][:, k, :], in_=pz[pb][:, k * 128:(k + 1) * 128], func=AF.Identity,
                                           scale=A2[:, k:k + 1], bias=B2[:, k:k + 1])
                    return ins
                S.op("act", fh2, reads=[t_pz[pb], t_const], writes=[t_h2t[b]])
                dma(S, scrH[:, :, qt * 128:(qt + 1) * 128], h2t[b][:], reads=[t_h2t[b]], q="pool")

            for st_ in range(17 + 1):
                if st_ < 17:
                    l1_A(st_)
                if st_ - 1 >= 0:
                    l1_B(st_ - 1)
            S.end_phase()

        with ExitStack() as ph:
            wd = sbuf(ph, "wd", [128, NFC, D], BF16)
            l2g = sbuf(ph, "l2g", [128, D])
            l2b = sbuf(ph, "l2b", [128, D])
            cw = sbuf(ph, "cw", [128, NFC * 3])
            cb_ = sbuf(ph, "cb", [128, NFC])
            h2h = sbuf(ph, "h2h", [128, 8, 128], BF16)
            h2s2 = [sbuf(ph, "h2s%d" % i, [128, 8, 1024], BF16) for i in range(2)]
            t_h2s2 = [T(), T()]
            wgu = [sbuf(ph, "wgu%d" % i, [128, 8, 256], BF16) for i in range(3)]
            gs = [sbuf(ph, "gs%d" % i, [128, 1026]) for i in range(2)]
            gc = [sbuf(ph, "gc%d" % i, [128, 512]) for i in range(2)]
            ge = [sbuf(ph, "ge%d" % i, [128, 512]) for i in range(2)]
            gsave = sbuf(ph, "gsave", [128, NFC, 2])
            aT = sbuf(ph, "aT", [128, NFC, 1024], BF16)
            x1t = [sbuf(ph, "x1t%d" % i, [128, D]) for i in range(2)]
            z = [sbuf(ph, "fz%d" % i, [128, D]) for i in range(2)]
            ot = [sbuf(ph, "ot%d" % i, [128, D]) for i in range(2)]
            st12 = sbuf(ph, "fst12", [128, 2, 6])
            mv = sbuf(ph, "fmv", [128, 2])
            rstd = sbuf(ph, "frstd", [128, 1])
            pgt = [psum(ph, "fpg%d" % i, [128, 512]) for i in range(2)]
            put = [psum(ph, "fpu%d" % i, [128, 512]) for i in range(2)]
            phh = psum(ph, "fph", [128, 512])
            py2 = psum(ph, "fpy", [128, 1024])
            t_wd, t_lc, t_h2h, t_h2s, t_gsave, t_aT, t_st, t_ph, t_py2 = (T() for _ in range(9))
            t_wgu, t_gs, t_gc, t_ge, t_x1t, t_z, t_ot, t_pg, t_pu = ([T(), T(), T()] for _ in range(9))
            dma(S, l2g[:], lnbc[2], writes=[t_lc])
            dma(S, l2b[:], lnbc[3], writes=[t_lc])
            dma(S, cw[:], convw[:, :], writes=[t_lc])
            dma(S, cb_[:], convb[:, :], writes=[t_lc])
            dma(S, h2h[:], scrH[:, :, 0:128], writes=[t_h2h])
            w_gu_v = w_ffgu.rearrange("i (k p) n -> i p k n", p=128)
            it = 0

            def load_gu(u):
                dma(S, wgu[u % 3][:], w_gu_v[u % NFC], writes=[t_wgu[u % 3]], q="pool")
            load_gu(0)
            load_gu(1)
            dma(S, wd[:], w_ffd.rearrange("(i p) n -> p i n", p=128), writes=[t_wd], q="pool")
            for sg in range(2):
                dma(S, h2s2[sg][:], scrH[:, :, 128 + 1024 * sg:128 + 1024 * (sg + 1)], writes=[t_h2s2[sg]])
            for sg in range(2):
                h2s = h2s2[sg]
                t_h2s = t_h2s2[sg]
                for i in range(NFC):
                    u = sg * NFC + i
                    wb = u % 3
                    gb = i % 2
                    if u + 2 < 2 * NFC:
                        load_gu(u + 2)
                    if sg == 0:
                        mm_group(S, phh[:, 0:2], [(wgu[wb][:, k, 0:128], h2h[:, k, 126:128]) for k in range(8)],
                                 reads=[t_wgu[wb], t_h2h], writes=[t_ph])
                        S.op("act", lambda e, gb=gb: e.mul(out=gs[gb][:, 0:2], in_=phh[:, 0:2], mul=flag[:, 0:1]),
                             reads=[t_ph, t_const], writes=[t_gs[gb]])
                    else:
                        S.op("act", lambda e, gb=gb, i=i: e.copy(out=gs[gb][:, 0:2], in_=gsave[:, i, :]),
                             reads=[t_gsave], writes=[t_gs[gb]])
                    for half in range(2):
                        pb = it % 2
                        it += 1
                        tsl = slice(half * 512, (half + 1) * 512)
                        mm_group(S, pgt[pb][:], [(wgu[wb][:, k, 0:128], h2s[:, k, tsl]) for k in range(8)],
                                 reads=[t_wgu[wb], t_h2s], writes=[t_pg[pb]])
                        mm_group(S, put[pb][:], [(wgu[wb][:, k, 128:256], h2s[:, k, tsl]) for k in range(8)],
                                 reads=[t_wgu[wb], t_h2s], writes=[t_pu[pb]])
                        o0 = 2 + half * 512
                        S.op("act", lambda e, gb=gb, pb=pb, o0=o0: e.copy(out=gs[gb][:, o0:o0 + 512], in_=pgt[pb][:]),
                             reads=[t_pg[pb]], writes=[t_gs[gb]])
                        S.op("act", lambda e, pb=pb, i=i: e.activation(out=gc[pb][:], in_=pgt[pb][:], func=AF.Identity,
                                                                       scale=cw[:, 3 * i + 2:3 * i + 3], bias=cb_[:, i:i + 1]),
                             reads=[t_pg[pb], t_lc], writes=[t_gc[pb]])
                        S.op("dve", lambda e, gb=gb, pb=pb, i=i, o0=o0: e.scalar_tensor_tensor(
                            out=gc[pb][:], in0=gs[gb][:, o0 - 1:o0 + 511], scalar=cw[:, 3 * i + 1:3 * i + 2], in1=gc[pb][:],
                            op0=ALU.mult, op1=ALU.add), reads=[t_gs[gb], t_gc[pb], t_lc], writes=[t_gc[pb]])
                        S.op("dve", lambda e, gb=gb, pb=pb, i=i, o0=o0: e.scalar_tensor_tensor(
                            out=gc[pb][:], in0=gs[gb][:, o0 - 2:o0 + 510], scalar=cw[:, 3 * i:3 * i + 1], in1=gc[pb][:],
                            op0=ALU.mult, op1=ALU.add), reads=[t_gs[gb], t_gc[pb], t_lc], writes=[t_gc[pb]])
                        S.op("act", lambda e, pb=pb: e.activation(out=ge[pb][:], in_=gc[pb][:], func=AF.Gelu),
                             reads=[t_gc[pb]], writes=[t_ge[pb]])
                        S.op("dve", lambda e, pb=pb, i=i, tsl=tsl: e.tensor_tensor(out=aT[:, i, tsl], in0=put[pb][:], in1=ge[pb][:], op=ALU.mult),
                             reads=[t_pu[pb], t_ge[pb]], writes=[t_aT])
                    if sg == 0:
                        S.op("pool", lambda e, gb=gb, i=i: e.tensor_copy(out=gsave[:, i, :], in_=gs[gb][:, 1024:1026]),
                             reads=[t_gs[gb]], writes=[t_gsave])
                for t in range(8):
                    b = t % 2
                    row0 = sg * 1024 + t * 128
                    dma(S, x1t[b][:], out[row0:row0 + 128, :], writes=[t_x1t[b]])
                    for half in range(2):
                        mm_group(S, py2[:, half * 512:(half + 1) * 512],
                                 [(aT[:, i, t * 128:(t + 1) * 128], wd[:, i, half * 512:(half + 1) * 512]) for i in range(NFC)],
                                 reads=[t_aT, t_wd], writes=[t_py2])
                    S.op("dve", lambda e, b=b: e.tensor_tensor(out=z[b][:], in0=py2[:], in1=g2bc[:], op=ALU.mult),
                         reads=[t_py2, t_const], writes=[t_z[b]])
                    S.op("pool", lambda e, b=b: e.tensor_tensor(out=z[b][:], in0=z[b][:], in1=x1t[b][:], op=ALU.add),
                         reads=[t_z[b], t_x1t[b]], writes=[t_z[b]])
                    for half in range(2):
                        S.op("dve", lambda e, b=b, half=half: e.bn_stats(out=st12[:, half, :], in_=z[b][:, half * 512:(half + 1) * 512]),
                             reads=[t_z[b]], writes=[t_st])
                    S.op("dve", lambda e: e.bn_aggr(out=mv[:], in_=st12[:]), reads=[t_st], writes=[t_st])
                    S.op("act", lambda e: e.activation(out=rstd[:], in_=mv[:, 1:2], func=AF.Sqrt, bias=epsA[:], scale=1.0),
                         reads=[t_st, t_const], writes=[t_st])
                    S.op("dve", lambda e: e.reciprocal(out=rstd[:], in_=rstd[:]), reads=[t_st], writes=[t_st])
                    S.op("dve", lambda e, b=b: e.tensor_scalar(out=ot[b][:], in0=z[b][:], scalar1=mv[:, 0:1], scalar2=rstd[:],
                                                               op0=ALU.subtract, op1=ALU.mult),
                         reads=[t_z[b], t_st], writes=[t_ot[b]])
                    S.op("pool", lambda e, b=b: e.tensor_tensor(out=ot[b][:], in0=ot[b][:], in1=l2g[:], op=ALU.mult),
                         reads=[t_ot[b], t_lc], writes=[t_ot[b]])
                    S.op("pool", lambda e, b=b: e.tensor_tensor(out=ot[b][:], in0=ot[b][:], in1=l2b[:], op=ALU.add),
                         reads=[t_ot[b], t_lc], writes=[t_ot[b]])
                    dma(S, out[row0:row0 + 128, :], ot[b][:], reads=[t_ot[b], t_x1t[b]], q="pool")
            S.end_phase()
    return nc


def _const_tables(c):
    f32 = np.float32
    lt = np.arange(SEQ)
    pos = (lt if c == 1 else np.maximum(lt - HALF, 0)).astype(f32)
    inv_freq = (10000.0 ** (-np.arange(0, 64, 2, dtype=f32) / 64)).astype(f32)
    ang = pos[None, :] * inv_freq[:, None]
    cos32, sin32 = np.cos(ang).astype(f32), np.sin(ang).astype(f32)
    cos64 = np.concatenate([cos32, cos32], 0)
    sin64 = np.concatenate([-sin32, sin32], 0)
    cosM = np.concatenate([cos64, cos64], 0)
    sinM = np.concatenate([sin64, sin64], 0)
    freq = (1.0 / (10000.0 ** np.linspace(0.0, 1.0, 64, dtype=f32))).astype(f32)
    angr = pos[None, :] * freq[:, None]
    cr, sr = np.cos(angr).astype(f32), np.sin(angr).astype(f32)
    cosR = np.concatenate([cr, cr], 0)
    sinR = np.concatenate([-sr, sr], 0)
    vb = np.full((17, 16), NEGV, f32)
    own = np.zeros((17, 16), f32)
    for qt in range(17):
        lb = 7 if qt == 0 else 8 + (qt - 1) // 2
        for j in range(16):
            if j < lb and (j >= 8 or c == 1):
                vb[qt, j] = 0.0
        own[qt, lb] = 1.0
    vb = np.broadcast_to(vb.reshape(1, 272), (128, 272)).copy()
    own = np.broadcast_to(own.reshape(1, 272), (128, 272)).copy()
    p = np.arange(128, dtype=np.float64)
    ksp = np.zeros((128, 128), np.float64)
    epsq = np.zeros((128, 4), np.float64)
    for r in range(4):
        g = 1.0 - 2.0 ** (-5.0 - r)
        gc = g ** 128
        ks = (128.0 ** -0.5) * g ** (-(p + 1.0))
        for n in range(32):
            if n <= 14:
                wgt = gc ** (14 - n) * float(c)
            elif n == 15:
                wgt = float(c)
            else:
                wgt = 1.0
            ksp[:, r * 32 + n] = ks * wgt
        epsq[:, r] = LN_EPS / (g ** (p + 1.0)) ** 2
    k = np.arange(128)[:, None]
    q = np.arange(256)[None, :]
    tri256 = (k <= q).astype(f32)
    tri256b = ((k + 128) <= q).astype(f32)
    dmask = np.concatenate([tri256, tri256b], 1)
    tri = (k <= np.arange(128)[None, :]).astype(f32)
    hmask = np.concatenate([np.ones((128, 128), f32), tri], 1)
    onehot = (np.arange(SEQ)[None, :] // 256 == np.arange(16)[:, None]).astype(f32)
    return dict(cosM=cosM, sinM=sinM, cosR=cosR, sinR=sinR, vb=vb, ownhot=own, ksp=ksp.astype(f32),
                epsq=epsq.astype(f32), dmask=dmask, hmask=hmask, tri=tri, onehot=onehot,
                ident=np.eye(128, dtype=f32), flag=np.full((128, 1), float(c), f32))


_NC_CACHE = {}


def kernel(x, c, w_ada, b_ada, w_in, w_proj_moba, w_proj_ret, w_out, ln1_g, ln1_b,
           w_ff_gate, w_ff_up, ff_conv_w, ff_conv_b, w_ff_down, ln2_g, ln2_b):
    f32 = np.float32
    x = np.asarray(x, f32)
    c = np.asarray(c, f32)
    w_in0 = np.asarray(w_in, f32)[0]
    perm_m = np.concatenate([np.arange(32, 64), np.arange(0, 32)])
    perm_r = np.concatenate([np.arange(0, 128, 2), np.arange(1, 128, 2)])
    w_moba = np.empty((4, D, 384), f32)
    for hp in range(4):
        mq = w_in0[:, hp * 128:(hp + 1) * 128]
        mk = w_in0[:, 512 + hp * 128:512 + (hp + 1) * 128]
        mv = w_in0[:, 1024 + hp * 128:1024 + (hp + 1) * 128]
        w_moba[hp] = np.concatenate([mk, mq, mv], 1)
    w_ret = np.empty((4, D, 768), f32)
    for r in range(4):
        rq = w_in0[:, 1536 + r * 128:1536 + (r + 1) * 128]
        rk = w_in0[:, 2048 + r * 128:2048 + (r + 1) * 128]
        rv = w_in0[:, 2560 + r * 256:2560 + (r + 1) * 256]
        rg = w_in0[:, 3584 + r * 256:3584 + (r + 1) * 256]
        w_ret[r] = np.concatenate([rk[:, perm_r], rq[:, perm_r], rv, rg], 1)
    w_g = np.ascontiguousarray(w_in0[:, 4608:6656])
    wgate = np.asarray(w_ff_gate, f32)[0]
    wup = np.asarray(w_ff_up, f32)[0]
    w_ffgu = np.empty((NFC, D, 256), f32)
    for i in range(NFC):
        w_ffgu[i, :, 0:128] = wgate[:, i * 128:(i + 1) * 128]
        w_ffgu[i, :, 128:256] = wup[:, i * 128:(i + 1) * 128]
    convw = np.asarray(ff_conv_w, f32)[0]
    convwT = np.ascontiguousarray(convw.reshape(3, NFC, 128).transpose(2, 1, 0).reshape(128, NFC * 3))
    convbT = np.ascontiguousarray(np.asarray(ff_conv_b, f32)[0].reshape(NFC, 128).T)
    lnbc = np.stack([np.broadcast_to(np.asarray(v, f32)[0][None, :], (128, D)) for v in (ln1_g, ln1_b, ln2_g, ln2_b)]).copy()
    ln1T = np.ascontiguousarray(np.concatenate([np.asarray(ln1_g, f32)[0].reshape(8, 128).T,
                                                np.asarray(ln1_b, f32)[0].reshape(8, 128).T], 1))
    shared = dict(
        w_ada=np.ascontiguousarray(np.asarray(w_ada, f32)[0]),
        b_adaT=np.ascontiguousarray(np.asarray(b_ada, f32)[0].reshape(48, 128).T),
        w_moba=w_moba, w_ret=w_ret, w_g=w_g,
        w_pa=np.ascontiguousarray(np.asarray(w_proj_moba, f32)[0]),
        w_pr=np.ascontiguousarray(np.asarray(w_proj_ret, f32)[0]),
        w_o=np.ascontiguousarray(np.asarray(w_out, f32)[0]),
        w_ffgu=w_ffgu, w_ffd=np.ascontiguousarray(np.asarray(w_ff_down, f32)[0]),
        convw=convwT, convb=convbT, lnbc=lnbc, ln1T=ln1T,
    )
    consts = [_const_tables(0), _const_tables(1)]
    in_maps = []
    for core in range(8):
        b, h = core // 2, core % 2
        if h == 1:
            xT = np.ascontiguousarray(x[b].T)
            xq_ = np.ascontiguousarray(x[b, 1920:4096])
        else:
            xT = np.concatenate([np.zeros((D, HALF), f32), x[b, :HALF].T], 1)
            xq_ = np.concatenate([np.zeros((128, D), f32), x[b, :HALF]], 0)
        m = dict(shared)
        m.update(consts[h])
        m["xT"] = np.ascontiguousarray(xT)
        m["xq"] = np.ascontiguousarray(xq_)
        m["cT"] = np.ascontiguousarray(c[b].reshape(8, 128).T)
        in_maps.append(m)
    if "nc" not in _NC_CACHE:
        _NC_CACHE["nc"] = build_program()
    nc = _NC_CACHE["nc"]
    res = run_bass_kernel_spmd(nc, in_maps, core_ids=list(range(8)))
    outp = np.empty((NB, SEQ, D), f32)
    for core in range(8):
        b, h = core // 2, core % 2
        outp[b, h * HALF:(h + 1) * HALF] = res.results[core]["out"]
    return outp
```
